# Optimizing a Trainium2 kernel written in Bass

```python
import math
import jax, jax.numpy as jnp
from jax import lax
import numpy as np

D_MODEL = 1024
BATCH = 16
SEQ = 2048
DEPTH = 2

N_HEADS = 16
HEAD_DIM = D_MODEL // N_HEADS
N_KV_HEADS = 4
GROUP = N_HEADS // N_KV_HEADS
ATTN_WIDTH = N_HEADS * HEAD_DIM
KV_WIDTH = N_KV_HEADS * HEAD_DIM
IDX_HEADS = 8
IDX_DIM = 64
DSA_TOPK_MAX = 256
MOBA_BLOCK = 256
MOBA_TOPK = 3
REL_BUCKETS = 32
REL_MAX_DIST = 128
EPS = 1e-6
N_A_LAYERS = max(1, DEPTH // 2)
N_B_LAYERS = DEPTH - N_A_LAYERS
QBLK_A = 64
QBLK_B = 16
A_PARTS = [ATTN_WIDTH, KV_WIDTH, KV_WIDTH, ATTN_WIDTH, IDX_HEADS * IDX_DIM, IDX_HEADS, IDX_DIM]
A_COLS = sum(A_PARTS)
A_SPLITS = list(np.cumsum(A_PARTS)[:-1])

kernel_name = "yoco_dsa_moba_hybrid"


def rmsnorm(x, g):
    xf = x.astype(jnp.float32)
    y = xf * lax.rsqrt(jnp.mean(xf * xf, axis=-1, keepdims=True) + EPS)
    return (y * g.astype(jnp.float32)).astype(x.dtype)


def rel_bucket(dist):
    n = jnp.maximum(dist, 0)
    max_exact = REL_BUCKETS // 2
    nf = jnp.maximum(n, 1).astype(jnp.float32)
    large = max_exact + (jnp.log(nf / max_exact) / math.log(REL_MAX_DIST / max_exact)
                         * (REL_BUCKETS - max_exact)).astype(jnp.int32)
    large = jnp.minimum(large, REL_BUCKETS - 1)
    return jnp.where(n < max_exact, n, large)


def dsa_layer(x, norm_g, w_in, qn_g, kn_g, w_out, rel_bias):
    B, T, _ = x.shape
    h = rmsnorm(x, norm_g)
    q, k, v, gate, iq, iw, ik = jnp.split(h @ w_in, A_SPLITS, axis=-1)
    q = rmsnorm(q.reshape(B, T, N_KV_HEADS, GROUP, HEAD_DIM), qn_g)
    k = rmsnorm(k.reshape(B, T, N_KV_HEADS, HEAD_DIM), kn_g)
    v = v.reshape(B, T, N_KV_HEADS, HEAD_DIM)
    iq = iq.reshape(B, T, IDX_HEADS, IDX_DIM)
    iw = iw * IDX_HEADS ** -0.5
    topk = min(DSA_TOPK_MAX, T // 4)
    nblk = T // QBLK_A
    spos = jnp.arange(T)
    bidx = jnp.arange(B)[:, None, None]

    def block(i):
        t0 = i * QBLK_A
        qb = lax.dynamic_slice_in_dim(q, t0, QBLK_A, axis=1)
        iqb = lax.dynamic_slice_in_dim(iq, t0, QBLK_A, axis=1)
        iwb = lax.dynamic_slice_in_dim(iw, t0, QBLK_A, axis=1)
        tpos = t0 + jnp.arange(QBLK_A)
        causal = spos[None, :] <= tpos[:, None]
        isc = jnp.einsum('bqhd,bsd->bqhs', iqb, ik) * IDX_DIM ** -0.5
        iscore = jnp.einsum('bqh,bqhs->bqs', iwb, jax.nn.relu(isc)).astype(jnp.float32)
        iscore = jnp.where(causal[None], iscore, -jnp.inf)
        _, sel = lax.top_k(iscore, topk)
        valid = sel <= tpos[None, :, None]
        ks = k[bidx, sel]
        vs = v[bidx, sel]
        logits = jnp.einsum('bqgjd,bqkgd->bqgjk', qb, ks).astype(jnp.float32) * HEAD_DIM ** -0.5
        bias = rel_bias[rel_bucket(tpos[None, :, None] - sel)]
        bias = bias.reshape(B, QBLK_A, topk, N_KV_HEADS, GROUP).transpose(0, 1, 3, 4, 2)
        logits = jnp.where(valid[:, :, None, None, :], logits + bias.astype(jnp.float32), -jnp.inf)
        p = jax.nn.softmax(logits, axis=-1).astype(vs.dtype)
        o = jnp.einsum('bqgjk,bqkgd->bqgjd', p, vs)
        return o.reshape(B, QBLK_A, ATTN_WIDTH)

    o = lax.map(block, jnp.arange(nblk))
    o = o.transpose(1, 0, 2, 3).reshape(B, T, ATTN_WIDTH)
    return x + (o * jax.nn.silu(gate)) @ w_out


def shared_kv(x, norm_g, w_kv, kn_g):
    B, T, _ = x.shape
    h = rmsnorm(x, norm_g)
    k, v = jnp.split(h @ w_kv, 2, axis=-1)
    k = rmsnorm(k.reshape(B, T, N_KV_HEADS, HEAD_DIM), kn_g)
    v = v.reshape(B, T, N_KV_HEADS, HEAD_DIM)
    nb = -(-T // MOBA_BLOCK)
    pad = nb * MOBA_BLOCK - T
    kp = jnp.pad(k, ((0, 0), (0, pad), (0, 0), (0, 0)))
    vp = jnp.pad(v, ((0, 0), (0, pad), (0, 0), (0, 0)))
    kb = kp.reshape(B, nb, MOBA_BLOCK, N_KV_HEADS, HEAD_DIM).transpose(0, 3, 1, 2, 4)
    vb = vp.reshape(B, nb, MOBA_BLOCK, N_KV_HEADS, HEAD_DIM).transpose(0, 3, 1, 2, 4)
    counts = jnp.minimum(T - jnp.arange(nb) * MOBA_BLOCK, MOBA_BLOCK).astype(kb.dtype)
    kmean = kb.sum(axis=3) / counts[None, None, :, None]
    return kb, vb, kmean


def moba_layer(x, norm_g, w_in, qn_g, w_out, rel_bias, kb, vb, kmean):
    B, T, _ = x.shape
    h = rmsnorm(x, norm_g)
    q, gate = jnp.split(h @ w_in, 2, axis=-1)
    q = rmsnorm(q.reshape(B, T, N_KV_HEADS, GROUP, HEAD_DIM), qn_g)
    nb = kb.shape[2]
    nsel = min(MOBA_TOPK, nb - 1)
    bias_g = rel_bias.reshape(REL_BUCKETS, N_KV_HEADS, GROUP).transpose(1, 0, 2)
    r = jnp.arange(MOBA_BLOCK)
    bidx = jnp.arange(B)[:, None, None, None]
    gidx = jnp.arange(N_KV_HEADS)[None, None, :, None]
    nblk = T // QBLK_B

    def block(i):
        t0 = i * QBLK_B
        own = t0 // MOBA_BLOCK
        qb = lax.dynamic_slice_in_dim(q, t0, QBLK_B, axis=1)
        tpos = t0 + jnp.arange(QBLK_B)
        own_blk = jnp.full((B, QBLK_B, N_KV_HEADS, 1), own, dtype=jnp.int32)
        own_ok = jnp.ones((B, QBLK_B, N_KV_HEADS, 1), dtype=bool)
        if nsel > 0:
            gs = jnp.einsum('bqgjd,bgnd->bqgn', qb, kmean).astype(jnp.float32)
            gs = jnp.where(jnp.arange(nb) < own, gs, -jnp.inf)
            gv, sel = lax.top_k(gs, nsel)
            blocks = jnp.concatenate([sel.astype(jnp.int32), own_blk], axis=-1)
            ok = jnp.concatenate([jnp.isfinite(gv), own_ok], axis=-1)
        else:
            blocks, ok = own_blk, own_ok
        S = blocks.shape[-1]
        ks = kb[bidx, gidx, blocks]
        vs = vb[bidx, gidx, blocks]
        kpos = blocks[..., None] * MOBA_BLOCK + r
        mask = ok[..., None] & (kpos <= tpos[None, :, None, None, None])
        logits = jnp.einsum('bqgjd,bqgsrd->bqgjsr', qb, ks).astype(jnp.float32) * HEAD_DIM ** -0.5
        bucket = rel_bucket(tpos[None, :, None, None, None] - kpos)
        bias = bias_g[gidx[..., None], bucket]
        bias = jnp.moveaxis(bias, -1, 3).astype(jnp.float32)
        logits = jnp.where(mask[:, :, :, None], logits + bias, -jnp.inf)
        logits = logits.reshape(B, QBLK_B, N_KV_HEADS, GROUP, S * MOBA_BLOCK)
        p = jax.nn.softmax(logits, axis=-1).astype(vs.dtype)
        p = p.reshape(B, QBLK_B, N_KV_HEADS, GROUP, S, MOBA_BLOCK)
        o = jnp.einsum('bqgjsr,bqgsrd->bqgjd', p, vs)
        return o.reshape(B, QBLK_B, ATTN_WIDTH)

    o = lax.map(block, jnp.arange(nblk))
    o = o.transpose(1, 0, 2, 3).reshape(B, T, ATTN_WIDTH)
    return x + (o * jax.nn.silu(gate)) @ w_out


def setup_inputs(seed: int = 0) -> dict:
    key = jax.random.key(seed)
    ks = jax.random.split(key, 16)
    f32 = jnp.float32

    def nrm(k, shape, scale):
        return jax.random.normal(k, shape, f32) * scale

    def gain(k, shape):
        return 1.0 + 0.02 * jax.random.normal(k, shape, f32)

    return {
        "x": jax.random.normal(ks[0], (BATCH, SEQ, D_MODEL), f32),
        "norm_a_g": gain(ks[1], (N_A_LAYERS, D_MODEL)),
        "w_in_a": nrm(ks[2], (N_A_LAYERS, D_MODEL, A_COLS), D_MODEL ** -0.5),
        "qn_a_g": gain(ks[3], (N_A_LAYERS, HEAD_DIM)),
        "kn_a_g": gain(ks[4], (N_A_LAYERS, HEAD_DIM)),
        "w_out_a": nrm(ks[5], (N_A_LAYERS, ATTN_WIDTH, D_MODEL), ATTN_WIDTH ** -0.5),
        "rel_bias": nrm(ks[6], (REL_BUCKETS, N_HEADS), 0.5),
        "norm_kv_g": gain(ks[7], (D_MODEL,)),
        "w_kv": nrm(ks[8], (D_MODEL, 2 * KV_WIDTH), D_MODEL ** -0.5),
        "kn_b_g": gain(ks[9], (HEAD_DIM,)),
        "norm_b_g": gain(ks[10], (N_B_LAYERS, D_MODEL)),
        "w_in_b": nrm(ks[11], (N_B_LAYERS, D_MODEL, 2 * ATTN_WIDTH), D_MODEL ** -0.5),
        "qn_b_g": gain(ks[12], (N_B_LAYERS, HEAD_DIM)),
        "w_out_b": nrm(ks[13], (N_B_LAYERS, ATTN_WIDTH, D_MODEL), ATTN_WIDTH ** -0.5),
    }


def reference(x, norm_a_g, w_in_a, qn_a_g, kn_a_g, w_out_a, rel_bias, norm_kv_g, w_kv,
              kn_b_g, norm_b_g, w_in_b, qn_b_g, w_out_b):
    h = x
    kb = vb = kmean = None
    for layer in range(DEPTH):
        if layer < N_A_LAYERS:
            h = dsa_layer(h, norm_a_g[layer], w_in_a[layer], qn_a_g[layer], kn_a_g[layer],
                          w_out_a[layer], rel_bias)
            if layer == N_A_LAYERS - 1:
                kb, vb, kmean = shared_kv(h, norm_kv_g, w_kv, kn_b_g)
        else:
            j = layer - N_A_LAYERS
            h = moba_layer(h, norm_b_g[j], w_in_b[j], qn_b_g[j], w_out_b[j], rel_bias,
                           kb, vb, kmean)
    return h
```

```python
import numpy as np
import concourse.bass as bass
import concourse.mybir as mybir
from concourse.bass_utils import run_bass_kernel_spmd
from contextlib import ExitStack

F32 = mybir.dt.float32
BF16 = mybir.dt.bfloat16
AF = mybir.ActivationFunctionType
ALU = mybir.AluOpType
AX = mybir.AxisListType

T = 2048
D = 1024
NT = 16
NSEQ = 2
NCORES = 8
H = 16
G = 4
DH = 64
EPS = 1e-6
BIG = 30000.0
A_COLS = 3144
NBIS = 16
TOPK = 256

ENGS = ("pe", "act", "dve", "pool", "sp")


class Buf:
    __slots__ = ("name", "last_w", "readers")

    def __init__(self, name):
        self.name = name
        self.last_w = None
        self.readers = []


class Op:
    __slots__ = ("eng", "idx", "fn", "deps", "inc", "incval", "stream", "is_dma", "waits")

    def __init__(self, eng, idx, fn, is_dma=False, stream=None):
        self.eng = eng
        self.idx = idx
        self.fn = fn
        self.deps = []
        self.inc = False
        self.incval = 0
        self.stream = stream
        self.is_dma = is_dma
        self.waits = []


class Sched:
    def __init__(self, nc, es):
        self.nc = nc
        self.es = es
        self.ops = {e: [] for e in ENGS}
        self.streams = {}
        self.sems = {}
        self.bufs = {}

    def _B(self, x):
        if isinstance(x, Buf):
            return x
        b = self.bufs.get(x)
        if b is None:
            b = self.bufs[x] = Buf(x)
        return b

    def _record(self, op, reads, writes):
        deps = []
        for r in reads:
            r = self._B(r)
            if r.last_w is not None:
                deps.append(r.last_w)
            r.readers.append(op)
        for w in writes:
            w = self._B(w)
            if w.last_w is not None:
                deps.append(w.last_w)
            last = {}
            for x in w.readers:
                if x is op:
                    continue
                key = ("dma", x.stream, x.idx) if x.is_dma else x.eng
                if key not in last or x.idx > last[key].idx:
                    last[key] = x
            deps.extend(last.values())
            w.last_w = op
            w.readers = []
        op.deps = deps

    def op(self, eng, fn, reads=(), writes=()):
        o = Op(eng, len(self.ops[eng]), fn)
        self.ops[eng].append(o)
        self._record(o, reads, writes)
        return o

    def dma(self, eng, stream, fn, reads=(), writes=()):
        o = Op(eng, len(self.ops[eng]), fn, is_dma=True, stream=stream)
        self.ops[eng].append(o)
        self.streams.setdefault(stream, []).append(o)
        self._record(o, reads, writes)
        return o

    def finalize(self, final_wait_eng="sp"):
        fin = Op(final_wait_eng, len(self.ops[final_wait_eng]), None)
        fin.deps = [lst[-1] for lst in self.streams.values()]
        self.ops[final_wait_eng].append(fin)
        for name, lst in self.streams.items():
            for i, o in enumerate(lst):
                o.incval = 16 * (i + 1)
        sel = {}
        for e in ENGS:
            waited = {}
            for o in self.ops[e]:
                need = {}
                for d in o.deps:
                    if d.is_dma:
                        key = "dma:" + d.stream
                        pos = d.incval
                    else:
                        if d.eng == o.eng and d.eng == "pe":
                            continue
                        key = d.eng
                        pos = d.idx
                    if key not in need or pos > need[key][0]:
                        need[key] = (pos, d)
                lst = []
                for k, (pos, d) in need.items():
                    if pos > waited.get(k, -1):
                        waited[k] = pos
                        lst.append((k, d))
                        if not d.is_dma:
                            d.inc = True
                sel[id(o)] = lst
        for e in ENGS:
            c = 0
            for o in self.ops[e]:
                if o.is_dma:
                    continue
                if o.inc:
                    c += 1
                    o.incval = c
        for e in ENGS:
            for o in self.ops[e]:
                o.waits = [(k, d.incval) for (k, d) in sel[id(o)]]

    def emit(self):
        nc = self.nc
        keys = list(ENGS) + ["dma:" + s for s in self.streams]
        for k in keys:
            self.sems[k] = self.es.enter_context(nc.semaphore("s_" + k.replace(":", "_")))
        sems = self.sems
        ops = self.ops

        def run(engname, eng):
            for o in ops[engname]:
                if o.fn is None:
                    for (k, v) in o.waits:
                        eng.wait_ge(sems[k], v)
                    continue
                for (k, v) in o.waits[1:]:
                    eng.wait_ge(sems[k], v)
                ins = o.fn(eng)
                if o.waits:
                    ins._wait_ge(sems[o.waits[0][0]], o.waits[0][1])
                if o.is_dma:
                    ins.then_inc(sems["dma:" + o.stream], 16)
                elif o.inc:
                    ins.then_inc(sems[engname], 1)

        with nc.Block() as block:
            @block.tensor
            def _(e):
                run("pe", e)

            @block.scalar
            def _(e):
                run("act", e)

            @block.vector
            def _(e):
                run("dve", e)

            @block.gpsimd
            def _(e):
                run("pool", e)

            @block.sync
            def _(e):
                run("sp", e)


def _rel_bucket_np(dist):
    n = np.maximum(dist, 0)
    max_exact = 16
    nf = np.maximum(n, 1).astype(np.float32)
    large = max_exact + (np.log(nf / np.float32(max_exact)) / np.float32(np.log(128 / max_exact))
                         * np.float32(32 - max_exact)).astype(np.int32)
    large = np.minimum(large, 31)
    return np.where(n < max_exact, n, large)


def _near_bias_layout(rel_bias):
    s = np.arange(128)[:, None]
    t = np.arange(128)[None, :]
    out = np.empty((2, 128, H, 128), np.float32)
    for d in range(2):
        bk = _rel_bucket_np(128 * d + t - s)
        out[d] = np.transpose(rel_bias[bk], (0, 2, 1))
    return np.ascontiguousarray(out)


def build_program(mode="AB", nseq=NSEQ, ntiles=NT, dbg_tile=None, pipeline=True):
    nc = bass.Bass("TRN2", target_bir_lowering=False)
    doA = "A" in mode
    doB = "B" in mode

    def din(name, shape):
        return nc.dram_tensor(name, shape, F32, kind="ExternalInput").ap()

    if doA:
        x_d = din("x", [nseq, T, D])
        w_in_a = din("w_in_a", [D, A_COLS])
        w_out_a = din("w_out_a", [D, D])
        norm_a_g = din("norm_a_g", [D])
        qn_a_g = din("qn_a_g", [DH])
        kn_a_g = din("kn_a_g", [DH])
    if doB:
        w_kv = din("w_kv", [D, 512])
        w_in_b = din("w_in_b", [D, 2048])
        w_out_b = din("w_out_b", [D, D])
        norm_kv_g = din("norm_kv_g", [D])
        norm_b_g = din("norm_b_g", [D])
        kn_b_g = din("kn_b_g", [DH])
        qn_b_g = din("qn_b_g", [DH])
        out_d = nc.dram_tensor("out", [nseq, T, D], F32, kind="ExternalOutput").ap()
    rel_bias = din("rel_bias", [32, H])
    nbias_d = din("nbias", [2, 128, H * 128])
    if mode == "A":
        h1_d = nc.dram_tensor("h1", [nseq, T, D], F32, kind="ExternalOutput").ap()
    elif mode == "B":
        h1_d = din("h1", [nseq, T, D])
    else:
        h1_d = nc.dram_tensor("h1", [nseq, T, D], F32, kind="Internal").ap()

    es = ExitStack()
    S = Sched(nc, es)

    def sb(name, shape, dt):
        return es.enter_context(nc.sbuf_tensor(name, shape, dt))

    def ps(name, shape, dt):
        return es.enter_context(nc.psum_tensor(name, shape, dt))

    W = 73

    ident = sb("ident", [128, 128], BF16)
    NB = sb("NB", [128, 2, H, 128], BF16)
    g1 = sb("g1", [128, D], F32)
    gk = sb("gk", [128, DH], F32)
    gtmp = sb("gtmp", [128, DH], F32)
    cb = sb("cb", [128, H], F32)
    epsb = sb("epsb", [128, 2], F32)
    negc = sb("negc", [128, 128], F32)
    halfpow = sb("halfpow", [128, NBIS], F32)
    vc = sb("vc", [128, NT, G, 65], BF16)
    kTc = sb("kTc", [W, G, T], BF16)
    ikT = sb("ikT", [64, T], BF16)
    wbig = sb("wbig", [128, 8, A_COLS], BF16)
    wo = sb("wo", [128, 8, D], BF16)
    xt = [sb("xt%d" % k, [128, D], F32) for k in range(2)]
    junk_a = sb("junk_a", [128, D], BF16)
    sqb = sb("sqb", [128, D], F32)
    hn0 = sb("hn0", [128, D], BF16)
    hT0 = sb("hT0", [128, 8, 128], BF16)
    q_aug = [sb("q_aug%d" % k, [128, H, W], BF16) for k in range(2)]
    k_aug = sb("k_aug", [128, G, W], BF16)
    ktmp = sb("ktmp", [128, G * DH], F32)
    qT = [sb("qT%d" % k, [W, H, 128], BF16) for k in range(2)]
    thb = sb("thb", [128, D], F32)
    sg = [sb("sg%d" % k, [128, D], BF16) for k in range(2)]
    sgp = sb("sgp", [128, D], BF16)
    sgT = sb("sgT", [128, 8, 128], BF16)
    st = sb("st", [128, 96], F32)
    ik_tok = sb("ik_tok", [128, 64], BF16)
    iq_tok = sb("iq_tok", [128, 512], BF16)
    iqT = sb("iqT", [64, 8, 128], BF16)
    wst = sb("wst", [128, 16], F32)
    dsg = sb("dsg", [128, 8, 128], BF16)
    NRB = 4
    Rb = [sb("Rb%d" % k, [128, 512], BF16) for k in range(NRB)]
    isc = sb("isc", [128, T], F32)
    bis = sb("bis", [128, 8 + 2 * NBIS], F32)
    g2 = isc[:, 0:1024]
    hn1 = isc[:, 1024:1536].bitcast(BF16)
    hT1 = isc[:, 1536:2048].bitcast(BF16)
    HN = [(hn0[:, :], "hn0"), (hn1, "isc")]
    HT = [(hT0[:, :, :].rearrange("p a b -> p (a b)"), "hT0"), (hT1, "isc")]
    mask_tok = sb("mask_tok", [128, T], BF16)
    maskT = [sb("maskT%d" % k, [128, NT, 128], BF16) for k in range(2)]
    NPT = 5
    pT = [sb("pT%d" % k, [128, 512], BF16) for k in range(NPT)]
    drow = [sb("drow%d" % k, [1, 512], BF16) for k in range(2)]
    ones1 = sb("ones1", [1, 8], BF16)
    rdt = sb("rdt", [128, H], F32)
    ogT = sb("ogT", [128, 8, 128], BF16)
    if doB:
        kmT = sb("kmT", [64, G, 8], BF16)
        kmTf = sb("kmTf", [64, G, 8], F32)
        qsum = sb("qsum", [128, G, DH], BF16)
        qsumf = sb("qsumf", [128, G, DH], F32)
        qsT = sb("qsT", [64, G, 128], BF16)
        gsb = sb("gsb", [128, G, 8], F32)
        m8 = sb("m8", [128, G, 8], F32)
        sel = sb("sel", [128, G, 8], F32)

    RBK = ps("PS_R", [128, 1536], F32)
    PO = ps("PS_O", [128, 2048], F32)
    PT = ps("PS_T", [128, 1024], BF16)
    PTF = PT[:, :].bitcast(F32)
    NRK = 3
    R = [RBK[:, k * 512:(k + 1) * 512] for k in range(NRK)]
    RN = ["R%d" % k for k in range(NRK)]
    OB = ["O0", "O1", "O2", "O3"]
    bank_ctr = [0]
    held = set()

    def bank(hold=False):
        while True:
            k = bank_ctr[0] % NRK
            bank_ctr[0] += 1
            if k not in held:
                break
        if hold:
            held.add(k)
        return k

    S.op("pool", lambda e: e.memset(isc[:, 0:128], 0.0), writes=["isc"])
    S.op("pool", lambda e: e.affine_select(out=isc[:, 0:128], in_=isc[:, 0:128], pattern=[[-1, 128]],
                                           compare_op=ALU.not_equal, fill=1.0, base=0, channel_multiplier=1),
         reads=["isc"], writes=["isc"])
    S.op("dve", lambda e: e.tensor_copy(out=ident[:], in_=isc[:, 0:128]), reads=["isc"], writes=["ident"])
    S.op("pool", lambda e: e.memset(junk_a[:, :], 0.0))
    S.op("pool", lambda e: e.memset(ones1[:, :], 1.0), writes=["ones1"])
    S.op("pool", lambda e: e.memset(epsb[:, 0:1], float(D * EPS)), writes=["epsb"])
    S.op("pool", lambda e: e.memset(epsb[:, 1:2], float(DH * EPS)), reads=["epsb"], writes=["epsb"])
    S.op("pool", lambda e: e.memset(negc[:], 0.0), writes=["negc"])
    S.op("pool", lambda e: e.affine_select(out=negc[:], in_=negc[:], pattern=[[-1, 128]],
                                           compare_op=ALU.is_ge, fill=-BIG, base=0, channel_multiplier=1),
         reads=["negc"], writes=["negc"])
    for k in range(NBIS):
        S.op("pool", (lambda k: lambda e: e.memset(halfpow[:, k:k + 1], 2.0 ** -(k + 1)))(k), writes=["halfpow"])
    S.op("pool", lambda e: e.memset(vc[:, :, :, 64:65], 1.0), writes=["vc_init"])
    S.op("pool", lambda e: e.memset(k_aug[:, :, 64:W], 0.0), writes=["k_aug"])
    S.op("pool", lambda e: e.memset(k_aug[:, :, 64:65], 1.0), reads=["k_aug"], writes=["k_aug"])
    S.dma("sp", "cb", lambda e: e.dma_start(out=cb[:], in_=rel_bias[31, :].partition_broadcast(128)), writes=["cb"])
    for k in range(2):
        S.op("pool", (lambda k: lambda e: e.memset(q_aug[k][:, :, 64:W], 0.0))(k), writes=["q_aug%d" % k])
        S.op("dve", (lambda k: lambda e: e.tensor_copy(out=q_aug[k][:, :, 64:65], in_=cb[:, :].unsqueeze(2)))(k),
             reads=["cb", "q_aug%d" % k], writes=["q_aug%d" % k])
    for d in range(2):
        S.dma("sp", "isc", (lambda d: lambda e: e.dma_start(out=isc[:, :], in_=nbias_d[d]))(d), writes=["isc"])
        S.op("dve", lambda e: e.tensor_tensor(out=isc[:, :].rearrange("p (h t) -> p h t", h=H),
                                              in0=isc[:, :].rearrange("p (h t) -> p h t", h=H),
                                              in1=cb[:, :].unsqueeze(2).to_broadcast([128, H, 128]),
                                              op=ALU.subtract), reads=["isc", "cb"], writes=["isc"])
        if d == 0:
            S.op("pool", lambda e: e.affine_select(out=isc[:, :].rearrange("p (h t) -> p h t", h=H),
                                                   in_=isc[:, :].rearrange("p (h t) -> p h t", h=H),
                                                   pattern=[[0, H], [1, 128]], compare_op=ALU.is_ge, fill=-BIG,
                                                   base=0, channel_multiplier=-1), reads=["isc"], writes=["isc"])
        S.op("act", (lambda d: lambda e: e.copy(out=NB[:, d, :, :], in_=isc[:, :].rearrange("p (h t) -> p h t", h=H)))(d),
             reads=["isc"], writes=["NB"])

    def load_gain(dst, src_ap, scale, name):
        S.dma("sp", name, lambda e: e.dma_start(out=dst[:], in_=src_ap.partition_broadcast(128)), writes=[name])
        S.op("dve", lambda e: e.tensor_scalar(out=dst[:], in0=dst[:], scalar1=float(scale), scalar2=None, op0=ALU.mult),
             reads=[name], writes=[name])

    def load_gk(kn_ap, qn_ap):
        S.dma("sp", "gk", lambda e: e.dma_start(out=gk[:], in_=kn_ap.partition_broadcast(128)), writes=["gk"])
        S.dma("sp", "gtmp", lambda e: e.dma_start(out=gtmp[:], in_=qn_ap.partition_broadcast(128)), writes=["gtmp"])
        S.op("dve", lambda e: e.scalar_tensor_tensor(out=gk[:], in0=gk[:], scalar=8.0, in1=gtmp[:],
                                                     op0=ALU.mult, op1=ALU.mult), reads=["gk", "gtmp"], writes=["gk"])

    def load_w(dst, src, ncols, col0, bufname):
        srcv = src.rearrange("(kc p) n -> p kc n", p=128)
        for kc in range(8):
            S.dma("pool", bufname + str(kc),
                  (lambda kc: lambda e: e.dma_start(out=dst[:, kc, col0:col0 + ncols], in_=srcv[:, kc, :]))(kc),
                  writes=[bufname + str(kc)])

    pt_ctr = [0]
    rb_ctr = [0]
    dbg_n = [0]

    def fence_dve(bufs):
        S.op("dve", lambda e: e.tensor_copy(out=junk_a[:, 0:512], in_=junk_a[:, 512:1024]), reads=bufs, writes=bufs)

    def dbg_dump(name, ap, shape, dt, bufs):
        dn = nc.dram_tensor("dbg_" + name, list(shape), dt, kind="ExternalOutput").ap()
        dbg_n[0] += 1
        S.dma("sp", "dbg%d" % dbg_n[0], lambda e: e.dma_start(out=dn, in_=ap), reads=bufs)

    def rsqrt_act(out_ap, in_ap, c, rbuf, wbuf, toff):
        n = in_ap.shape[1]
        tmp = st[:, toff:toff + n]
        tn = "st_tmp%d" % toff
        S.op("act", lambda e: e.activation(out=tmp, in_=in_ap, func=AF.Ln, bias=epsb[:, {1024: 0, 64: 1}[int(round(c / EPS))]:{1024: 1, 64: 2}[int(round(c / EPS))]]),
             reads=[rbuf, "epsb"], writes=[tn])
        S.op("act", lambda e: e.activation(out=out_ap, in_=tmp, func=AF.Exp, scale=-0.5), reads=[tn], writes=[wbuf])

    def rms_and_transpose(xs, gains, nh):
        S.op("act", lambda e: e.activation(out=junk_a[:], in_=xt[xs][:], func=AF.Square, accum_out=st[:, 0:1]),
             reads=["xt%d" % xs], writes=["st_ss"])
        rsqrt_act(st[:, 1:2], st[:, 0:1], float(D * EPS), "st_ss", "st_r", 2)
        for k in range(nh):
            hn_ap, hn_nm = HN[k]
            ht_ap, ht_nm = HT[k]
            g_ap, g_nm = gains[k]
            S.op("dve", (lambda hn_ap, g_ap: lambda e: e.scalar_tensor_tensor(out=hn_ap, in0=xt[xs][:], scalar=st[:, 1:2],
                                                                             in1=g_ap, op0=ALU.mult, op1=ALU.mult))(hn_ap, g_ap),
                 reads=["xt%d" % xs, "st_r", g_nm], writes=[hn_nm])
            for kc in range(8):
                S.op("pe", (lambda hn_ap, kc: lambda e: e.transpose(out=PT[:, kc * 128:(kc + 1) * 128],
                                                                   in_=hn_ap[:, kc * 128:(kc + 1) * 128], identity=ident[:]))(hn_ap, kc),
                     reads=[hn_nm, "ident"], writes=["T"])
            S.op("act", (lambda ht_ap: lambda e: e.copy(out=ht_ap, in_=PT[:, :]))(ht_ap), reads=["T"], writes=[ht_nm])

    def proj(b, hk, col0, ncols):
        ht_ap, ht_nm = HT[hk]
        for kc in range(8):
            S.op("pe", (lambda kc: lambda e: e.matmul(R[b][:, 0:ncols], lhsT=ht_ap[:, kc * 128:(kc + 1) * 128], rhs=wbig[:, kc, col0:col0 + ncols],
                                                      start=(kc == 0), stop=(kc == 7)))(kc),
                 reads=[ht_nm, "wbig%d" % kc], writes=[RN[b]])

    def q_group(qs, grp, hk, col0):
        b = bank()
        proj(b, hk, col0, 512)
        so = 16 + 8 * grp
        S.op("act", lambda e: e.activation(out=sqb[:, grp * 512:(grp + 1) * 512], in_=R[b], func=AF.Square),
             reads=[RN[b]], writes=["sqb%d" % grp])
        S.op("dve", lambda e: e.tensor_reduce(out=st[:, so:so + 8], in_=sqb[:, grp * 512:(grp + 1) * 512].rearrange("p (h d) -> p h d", h=8),
                                              axis=AX.X, op=ALU.add), reads=["sqb%d" % grp], writes=["st_q%d" % grp])
        rsqrt_act(st[:, so:so + 8], st[:, so:so + 8], float(DH * EPS), "st_q%d" % grp, "st_q%d" % grp, 40 + 8 * grp)
        S.op("dve", lambda e: e.tensor_tensor(out=q_aug[qs][:, 8 * grp:8 * grp + 8, 0:DH], in0=R[b].rearrange("p (h d) -> p h d", h=8),
                                              in1=st[:, so:so + 8].unsqueeze(2).to_broadcast([128, 8, DH]), op=ALU.mult),
             reads=[RN[b], "st_q%d" % grp, "q_aug%d" % qs], writes=["q_aug%d" % qs])

    def kv_group(i, hk, col0):
        b = bank()
        proj(b, hk, col0, 512)
        kv_ap = R[b]
        S.op("act", lambda e: e.activation(out=junk_a[:, 0:256], in_=kv_ap[:, 0:256], func=AF.Square), reads=[RN[b]], writes=["sqk"])
        S.op("dve", lambda e: e.tensor_reduce(out=st[:, 32:36], in_=junk_a[:, 0:256].rearrange("p (g d) -> p g d", g=G),
                                              axis=AX.X, op=ALU.add), reads=["sqk"], writes=["st_k"])
        rsqrt_act(st[:, 32:36], st[:, 32:36], float(DH * EPS), "st_k", "st_k", 56)
        S.op("dve", lambda e: e.tensor_tensor(out=ktmp[:, :].rearrange("p (g d) -> p g d", g=G),
                                              in0=kv_ap[:, 0:256].rearrange("p (g d) -> p g d", g=G),
                                              in1=st[:, 32:36].unsqueeze(2).to_broadcast([128, G, DH]), op=ALU.mult),
             reads=[RN[b], "st_k"], writes=["ktmp"])
        S.op("pool", lambda e: e.tensor_tensor(out=k_aug[:, :, 0:DH], in0=ktmp[:, :].rearrange("p (g d) -> p g d", g=G),
                                               in1=gk[:, :].unsqueeze(1).to_broadcast([128, G, DH]), op=ALU.mult),
             reads=["ktmp", "gk", "k_aug"], writes=["k_aug"])
        S.op("act", lambda e: e.copy(out=vc[:, i, :, 0:DH], in_=kv_ap[:, 256:512].rearrange("p (g d) -> p g d", g=G)),
             reads=[RN[b], "vc_init"], writes=["vc_%d" % i])

    def k_transpose(i, Wk):
        for g in range(G):
            S.op("pe", (lambda g: lambda e: e.transpose(out=PT[0:Wk, g * 128:(g + 1) * 128], in_=k_aug[:, g, 0:Wk],
                                                        identity=ident[:]))(g), reads=["k_aug", "ident"], writes=["T"])
        S.op("act", lambda e: e.copy(out=kTc[0:Wk, :, i * 128:(i + 1) * 128],
                                     in_=PT[0:Wk, 0:512].rearrange("p (g t) -> p g t", g=G)), reads=["T"], writes=["kTc_%d" % i])

    def gate_group(gs_, grp, hk, col0):
        b = bank()
        proj(b, hk, col0, 512)
        tb = thb[:, grp * 512:(grp + 1) * 512]
        tn = "thb%d" % grp
        S.op("act", lambda e: e.activation(out=tb, in_=R[b], func=AF.Exp, scale=-1.0), reads=[RN[b]], writes=[tn])
        S.op("act", lambda e: e.activation(out=tb, in_=tb, func=AF.Ln, bias=1.0), reads=[tn], writes=[tn])
        S.op("act", lambda e: e.activation(out=tb, in_=tb, func=AF.Exp, scale=-1.0), reads=[tn], writes=[tn])
        S.op("dve", lambda e: e.tensor_tensor(out=sg[gs_][:, grp * 512:(grp + 1) * 512], in0=R[b], in1=tb, op=ALU.mult),
             reads=[tn, RN[b], "sg%d" % gs_], writes=["sg%d" % gs_])

    def q_transpose(qs, Wq):
        for rnd in range(2):
            b = bank()
            pv = R[b].bitcast(BF16)
            for hh in range(8):
                h = rnd * 8 + hh
                S.op("pe", (lambda h, hh, pv: lambda e: e.transpose(out=pv[0:Wq, hh * 128:(hh + 1) * 128], in_=q_aug[qs][:, h, 0:Wq],
                                                                    identity=ident[:]))(h, hh, pv), reads=["q_aug%d" % qs, "ident"], writes=[RN[b]])
            S.op("act", (lambda rnd, pv: lambda e: e.copy(out=qT[qs][0:Wq, rnd * 8:(rnd + 1) * 8, :].rearrange("p a b -> p (a b)"), in_=pv[0:Wq, :]))(rnd, pv),
                 reads=[RN[b], "qT%d" % qs], writes=["qT%d" % qs])

    def attention(i, qs, Wq, ms):
        steps = [(j, g) for j in range(i + 1) for g in range(G)]
        info = {}

        def qk(n):
            j, g = steps[n]
            near = (i - j) <= 1
            slot = pt_ctr[0] % NPT
            pt_ctr[0] += 1
            b = bank(hold=True)
            info[n] = (b, slot)
            S.op("pe", (lambda g, j, b: lambda e: e.matmul(
                R[b], lhsT=kTc[0:Wq, g, j * 128:(j + 1) * 128],
                rhs=qT[qs][0:Wq, 4 * g:4 * g + 4, :].rearrange("p a b -> p (a b)"),
                start=True, stop=(not near)))(g, j, b),
                reads=["kTc_%d" % j, "qT%d" % qs], writes=[RN[b]])
            if near:
                S.op("pe", (lambda g, j, b: lambda e: e.matmul(
                    R[b], lhsT=ident[:], rhs=NB[:, i - j, 4 * g:4 * g + 4, :].rearrange("p a b -> p (a b)"),
                    start=False, stop=True))(g, j, b), reads=["ident", "NB"], writes=[RN[b]])

        qk(0)
        for n in range(len(steps)):
            j, g = steps[n]
            b, slot = info.pop(n)
            S.op("act", (lambda b, slot: lambda e: e.activation(out=pT[slot][:], in_=R[b], func=AF.Exp))(b, slot),
                 reads=[RN[b]], writes=["pT%d" % slot])
            held.discard(b)
            if ms is not None:
                S.op("dve", (lambda slot, j: lambda e: e.tensor_tensor(
                    out=pT[slot][:, :].rearrange("p (h t) -> p h t", h=4),
                    in0=pT[slot][:, :].rearrange("p (h t) -> p h t", h=4),
                    in1=maskT[ms][:, j, :].unsqueeze(1).to_broadcast([128, 4, 128]), op=ALU.mult))(slot, j),
                    reads=["pT%d" % slot, "maskT%d" % ms], writes=["pT%d" % slot])
            if n + 1 < len(steps):
                qk(n + 1)
            S.op("pe", (lambda g, slot, j: lambda e: e.matmul(
                PO[0:65, g * 512:(g + 1) * 512], lhsT=vc[:, j, g, :], rhs=pT[slot][:, :],
                start=(j == 0), stop=(j == i)))(g, slot, j),
                reads=["pT%d" % slot, "vc_%d" % j], writes=[OB[g]])
            yield

    def finish_tile(sq, i, xs, gs_, dst_d, out_stream):
        bd = bank()
        for g in range(G):
            rs = g % 2
            S.op("act", (lambda g, rs: lambda e: e.copy(out=drow[rs][0:1, :], in_=PO[64:65, g * 512:(g + 1) * 512]))(g, rs),
                 reads=[OB[g]], writes=["drow%d" % rs])
            for jh in range(4):
                h = 4 * g + jh
                S.op("pe", (lambda rs, jh, h, bd: lambda e: e.matmul(R[bd][:, h:h + 1], lhsT=drow[rs][0:1, jh * 128:(jh + 1) * 128],
                                                                     rhs=ones1[0:1, 0:1], start=True, stop=True))(rs, jh, h, bd),
                     reads=["drow%d" % rs, "ones1"], writes=[RN[bd]])
        S.op("dve", lambda e: e.reciprocal(out=rdt[:, :], in_=R[bd][:, 0:H]), reads=[RN[bd]], writes=["rdt"])
        S.op("dve", lambda e: e.tensor_tensor(out=sgp[:, :].rearrange("p (h d) -> p h d", h=H),
                                              in0=sg[gs_][:, :].rearrange("p (h d) -> p h d", h=H),
                                              in1=rdt[:, :].unsqueeze(2).to_broadcast([128, H, DH]), op=ALU.mult),
             reads=["sg%d" % gs_, "rdt"], writes=["sgp"])
        yield
        bt = bank()
        ptv = R[bt].bitcast(BF16)
        for kc in range(8):
            S.op("pe", (lambda kc: lambda e: e.transpose(out=ptv[:, kc * 128:(kc + 1) * 128], in_=sgp[:, kc * 128:(kc + 1) * 128],
                                                         identity=ident[:]))(kc), reads=["sgp", "ident"], writes=[RN[bt]])
        S.op("act", lambda e: e.copy(out=sgT[:, :, :].rearrange("p a b -> p (a b)"), in_=ptv[:, :]), reads=[RN[bt]], writes=["sgT"])
        yield
        for g in range(G):
            for par in range(2):
                S.op("dve", (lambda g, par: lambda e: e.tensor_tensor(
                    out=ogT[par * 64:(par + 1) * 64, 2 * g:2 * g + 2, :],
                    in0=PO[0:64, g * 512:(g + 1) * 512].rearrange("p (a b t) -> p a b t", a=2, b=2)[:, :, par, :],
                    in1=sgT[par * 64:(par + 1) * 64, 2 * g:2 * g + 2, :], op=ALU.mult))(g, par),
                    reads=[OB[g], "sgT", "ogT"], writes=["ogT"])
            yield
        for nb in range(2):
            b = bank()
            for kc in range(8):
                S.op("pe", (lambda nb, kc, b: lambda e: e.matmul(R[b], lhsT=ogT[:, kc, :], rhs=wo[:, kc, nb * 512:(nb + 1) * 512],
                                                                 start=(kc == 0), stop=(kc == 7)))(nb, kc, b),
                     reads=["ogT", "wo%d" % kc], writes=[RN[b]])
            S.op("dve", (lambda nb, b: lambda e: e.tensor_tensor(out=xt[xs][:, nb * 512:(nb + 1) * 512], in0=xt[xs][:, nb * 512:(nb + 1) * 512],
                                                                 in1=R[b], op=ALU.add))(nb, b),
                 reads=["xt%d" % xs, RN[b]], writes=["xt%d" % xs])
            yield
        S.dma("sp", out_stream + str(xs), lambda e: e.dma_start(out=dst_d[sq, i * 128:(i + 1) * 128, :], in_=xt[xs][:]),
              reads=["xt%d" % xs], writes=[out_stream + "_dram%d" % xs])

    def stage1_A(sq, i, tc):
        xs = tc % 2
        qs = tc % 2
        use_sel = i >= 2
        S.dma("sp", "xt%d" % xs, lambda e: e.dma_start(out=xt[xs][:], in_=x_d[sq, i * 128:(i + 1) * 128, :]), writes=["xt%d" % xs])
        rms_and_transpose(xs, [(g1[:], "g1")], 1)
        yield
        kv_group(i, 0, 1024)
        yield
        b = bank()
        proj(b, 0, 3072, 72)
        S.op("act", lambda e: e.copy(out=ik_tok[:], in_=R[b][:, 8:72]), reads=[RN[b]], writes=["ik_tok"])
        if use_sel:
            S.op("act", lambda e: e.activation(out=wst[:, 0:8], in_=R[b][:, 0:8], func=AF.Abs), reads=[RN[b]], writes=["wabs"])
            S.op("act", lambda e: e.activation(out=wst[:, 8:16], in_=R[b][:, 0:8], func=AF.Sign), reads=[RN[b]], writes=["wsg"])
            for h in range(8):
                S.op("dve", (lambda h: lambda e: e.tensor_scalar(out=dsg[:, h, :], in0=ident[:], scalar1=wst[:, 8 + h:9 + h],
                                                                 scalar2=None, op0=ALU.mult))(h),
                     reads=["ident", "wsg", "dsg"], writes=["dsg"])
            fence_dve(["dsg"])
            b2 = bank()
            proj(b2, 0, 2560, 512)
            S.op("dve", lambda e: e.tensor_tensor(out=iq_tok[:, :].rearrange("p (h d) -> p h d", h=8),
                                                  in0=R[b2].rearrange("p (h d) -> p h d", h=8),
                                                  in1=wst[:, 0:8].unsqueeze(2).to_broadcast([128, 8, 64]), op=ALU.mult),
                 reads=[RN[b2], "wabs"], writes=["iq_tok"])
        yield
        k_transpose(i, 65)
        S.op("pe", lambda e: e.transpose(out=PT[0:64, 512:640], in_=ik_tok[:, :], identity=ident[:]),
             reads=["ik_tok", "ident"], writes=["T"])
        S.op("act", lambda e: e.copy(out=ikT[:, i * 128:(i + 1) * 128], in_=PT[0:64, 512:640]), reads=["T"], writes=["ikT_%d" % i])
        yield
        ms = None
        if use_sel:
            ms = tc % 2
            n = 128 * (i + 1)
            for h in range(8):
                S.op("pe", (lambda h: lambda e: e.transpose(out=PT[0:64, h * 128:(h + 1) * 128], in_=iq_tok[:, h * 64:(h + 1) * 64],
                                                            identity=ident[:]))(h), reads=["iq_tok", "ident"], writes=["T"])
            S.op("act", lambda e: e.copy(out=iqT[:, :, :].rearrange("p a b -> p (a b)"), in_=PT[0:64, :]), reads=["T"], writes=["iqT"])
            yield
            nchunk = (n + 511) // 512
            for c in range(nchunk):
                ncol = min(512, n - 512 * c)
                ikbufs = ["ikT_%d" % jj for jj in range(4 * c, min(4 * c + 4, i + 1))]
                for h in range(8):
                    by = bank()
                    rs = rb_ctr[0] % NRB
                    rb_ctr[0] += 1
                    S.op("pe", (lambda h, by, c, ncol: lambda e: e.matmul(
                        R[by][:, 0:ncol], lhsT=iqT[:, h, :], rhs=ikT[:, c * 512:c * 512 + ncol],
                        start=True, stop=True))(h, by, c, ncol), reads=["iqT"] + ikbufs, writes=[RN[by]])
                    if h % 2 == 0:
                        S.op("act", (lambda by, rs, ncol: lambda e: e.activation(out=Rb[rs][:, 0:ncol], in_=R[by][:, 0:ncol], func=AF.Relu))(by, rs, ncol),
                             reads=[RN[by]], writes=["Rb%d" % rs])
                    else:
                        S.op("dve", (lambda by, rs, ncol: lambda e: e.tensor_scalar(out=Rb[rs][:, 0:ncol], in0=R[by][:, 0:ncol],
                                                                                    scalar1=0.0, scalar2=None, op0=ALU.max))(by, rs, ncol),
                             reads=[RN[by]], writes=["Rb%d" % rs])
                    S.op("pe", (lambda h, rs, ncol: lambda e: e.matmul(
                        PTF[:, 0:ncol], lhsT=dsg[:, h, :], rhs=Rb[rs][:, 0:ncol],
                        start=(h == 0), stop=(h == 7)))(h, rs, ncol), reads=["dsg", "Rb%d" % rs], writes=["T"])
                    if h % 2 == 1:
                        yield
                S.op("act", (lambda c, ncol: lambda e: e.copy(out=isc[:, c * 512:c * 512 + ncol], in_=PTF[:, 0:ncol]))(c, ncol),
                     reads=["T", "isc"], writes=["isc"])
        pending = [lambda: q_group(qs, 0, 0, 0), lambda: q_group(qs, 1, 0, 512),
                   lambda: gate_group(qs, 0, 0, 1536), lambda: gate_group(qs, 1, 0, 2048),
                   lambda: q_transpose(qs, 65)]
        if not use_sel:
            for f in pending:
                f()
                yield
        if use_sel:
            S.op("dve", lambda e: e.tensor_tensor(out=isc[:, i * 128:(i + 1) * 128], in0=isc[:, i * 128:(i + 1) * 128],
                                                  in1=negc[:], op=ALU.add), reads=["isc", "negc"], writes=["isc"])
            HI, LO, W0, MID, CNT, TMP, THR = 0, 1, 2, 3, 4, 5, 6
            HB = 8
            S.op("dve", lambda e: e.tensor_reduce(out=bis[:, HI:HI + 1], in_=isc[:, 0:n], axis=AX.X, op=ALU.max),
                 reads=["isc"], writes=["b_hi"])
            S.op("dve", lambda e: e.tensor_reduce(out=bis[:, LO:LO + 1], in_=isc[:, 0:128 * i], axis=AX.X, op=ALU.min),
                 reads=["isc"], writes=["b_lo"])
            S.op("dve", lambda e: e.tensor_tensor(out=bis[:, W0:W0 + 1], in0=bis[:, HI:HI + 1], in1=bis[:, LO:LO + 1], op=ALU.subtract),
                 reads=["b_hi", "b_lo"], writes=["b_w0"])
            S.op("dve", lambda e: e.tensor_scalar(out=bis[:, HB:HB + NBIS], in0=halfpow[:], scalar1=bis[:, W0:W0 + 1], scalar2=None,
                                                  op0=ALU.mult), reads=["halfpow", "b_w0"], writes=["b_h"])
            S.op("dve", lambda e: e.tensor_scalar(out=bis[:, HB + NBIS:HB + 2 * NBIS], in0=bis[:, HB:HB + NBIS], scalar1=2.0, scalar2=None,
                                                  op0=ALU.mult), reads=["b_h"], writes=["b_h2"])
            S.op("dve", lambda e: e.tensor_tensor(out=bis[:, MID:MID + 1], in0=bis[:, LO:LO + 1], in1=bis[:, HB:HB + 1], op=ALU.add),
                 reads=["b_lo", "b_h"], writes=["b_mid"])
            yield
            for k in range(NBIS):
                S.op("dve", lambda e: e.tensor_scalar(out=mask_tok[:, 0:n], in0=isc[:, 0:n], scalar1=bis[:, MID:MID + 1], scalar2=0.0,
                                                      op0=ALU.is_ge, op1=ALU.add, accum_out=bis[:, CNT:CNT + 1]),
                     reads=["isc", "b_mid"], writes=["b_cnt", "mask_tok"])
                if k < NBIS - 1:
                    S.op("dve", (lambda k: lambda e: e.scalar_tensor_tensor(
                        out=bis[:, TMP:TMP + 1], in0=bis[:, CNT:CNT + 1], scalar=TOPK - 0.5,
                        in1=bis[:, HB + NBIS + k + 1:HB + NBIS + k + 2], op0=ALU.is_ge, op1=ALU.mult))(k),
                        reads=["b_cnt", "b_h2"], writes=["b_tmp"])
                    S.op("dve", (lambda k: lambda e: e.scalar_tensor_tensor(
                        out=bis[:, MID:MID + 1], in0=bis[:, TMP:TMP + 1], scalar=bis[:, HB + k + 1:HB + k + 2],
                        in1=bis[:, MID:MID + 1], op0=ALU.subtract, op1=ALU.add))(k),
                        reads=["b_tmp", "b_h", "b_mid"], writes=["b_mid"])
                else:
                    S.op("dve", (lambda k: lambda e: e.scalar_tensor_tensor(
                        out=bis[:, TMP:TMP + 1], in0=bis[:, CNT:CNT + 1], scalar=TOPK - 0.5,
                        in1=bis[:, HB + k:HB + k + 1], op0=ALU.is_ge, op1=ALU.mult))(k),
                        reads=["b_cnt", "b_h"], writes=["b_tmp"])
                    S.op("dve", (lambda k: lambda e: e.scalar_tensor_tensor(
                        out=bis[:, THR:THR + 1], in0=bis[:, TMP:TMP + 1], scalar=bis[:, HB + k:HB + k + 1],
                        in1=bis[:, MID:MID + 1], op0=ALU.subtract, op1=ALU.add))(k),
                        reads=["b_tmp", "b_h", "b_mid"], writes=["b_thr"])
                if k % 3 == 1 and pending:
                    pending.pop(0)()
                yield
            while pending:
                pending.pop(0)()
                yield
            S.op("dve", lambda e: e.tensor_scalar(out=mask_tok[:, 0:n], in0=isc[:, 0:n], scalar1=bis[:, THR:THR + 1], scalar2=None,
                                                  op0=ALU.is_ge), reads=["isc", "b_thr"], writes=["mask_tok"])
            for j0 in range(0, i + 1, 8):
                j1 = min(i + 1, j0 + 8)
                for j in range(j0, j1):
                    S.op("pe", (lambda j, j0: lambda e: e.transpose(out=PT[:, (j - j0) * 128:(j - j0 + 1) * 128],
                                                                    in_=mask_tok[:, j * 128:(j + 1) * 128], identity=ident[:]))(j, j0),
                         reads=["mask_tok", "ident"], writes=["T"])
                S.op("act", (lambda j0, j1: lambda e: e.copy(out=maskT[ms][:, j0:j1, :].rearrange("p a b -> p (a b)"),
                                                             in_=PT[:, 0:(j1 - j0) * 128]))(j0, j1),
                     reads=["T", "maskT%d" % ms], writes=["maskT%d" % ms])
                yield

    def stage2_A(sq, i, tc):
        xs = tc % 2
        qs = tc % 2
        ms = (tc % 2) if i >= 2 else None
        yield from attention(i, qs, 65, ms)
        yield from finish_tile(sq, i, xs, qs, h1_d, "h1o")

    def stage1_B(sq, i, tc):
        xs = tc % 2
        qs = tc % 2
        own = i // 2
        S.dma("sp", "xt%d" % xs, lambda e: e.dma_start(out=xt[xs][:], in_=h1_d[sq, i * 128:(i + 1) * 128, :]),
              reads=["h1o_dram0", "h1o_dram1"], writes=["xt%d" % xs])
        rms_and_transpose(xs, [(g1[:], "g1"), (g2, "isc")], 2)
        yield
        S.op("pool", lambda e: e.memset(k_aug[:, :, 65:W], 0.0), reads=["k_aug"], writes=["k_aug"])
        S.op("pool", lambda e: e.memset(k_aug[:, :, 65 + own:66 + own], 1.0), reads=["k_aug"], writes=["k_aug"])
        kv_group(i, 0, 0)
        yield
        k_transpose(i, W)
        if i % 2 == 1:
            S.op("dve", lambda e: e.tensor_reduce(out=kmTf[:, :, own], in_=kTc[0:64, :, own * 256:(own + 1) * 256], axis=AX.X, op=ALU.add),
                 reads=["kTc_%d" % (i - 1), "kTc_%d" % i, "kmTf"], writes=["kmTf"])
            S.op("dve", lambda e: e.tensor_copy(out=kmT[:], in_=kmTf[:]), reads=["kmTf"], writes=["kmT"])
        yield
        q_group(qs, 0, 1, 512)
        yield
        q_group(qs, 1, 1, 1024)
        yield
        if own >= 4:
            S.op("dve", lambda e: e.tensor_reduce(out=qsumf[:, :, :], in_=q_aug[qs][:, :, 0:DH].rearrange("p (g j) d -> p g d j", g=G),
                                                  axis=AX.X, op=ALU.add), reads=["q_aug%d" % qs], writes=["qsumf"])
            S.op("dve", lambda e: e.tensor_copy(out=qsum[:], in_=qsumf[:]), reads=["qsumf"], writes=["qsum"])
            for g in range(G):
                S.op("pe", (lambda g: lambda e: e.transpose(out=PT[0:64, g * 128:(g + 1) * 128], in_=qsum[:, g, :], identity=ident[:]))(g),
                     reads=["qsum", "ident"], writes=["T"])
            S.op("act", lambda e: e.copy(out=qsT[:, :, :].rearrange("p a b -> p (a b)"), in_=PT[0:64, 0:512]), reads=["T"], writes=["qsT"])
            b = bank()
            for g in range(G):
                S.op("pe", (lambda g: lambda e: e.matmul(R[b][:, g * 8:(g + 1) * 8], lhsT=qsT[:, g, :], rhs=kmT[:, g, :],
                                                         start=True, stop=True))(g), reads=["qsT", "kmT"], writes=[RN[b]])
            S.op("dve", lambda e: e.tensor_copy(out=gsb[:, :, :].rearrange("p g n -> p (g n)"), in_=R[b][:, 0:32]), reads=[RN[b]], writes=["gsb"])
            S.op("dve", lambda e: e.memset(gsb[:, :, own:8], -1e30), reads=["gsb"], writes=["gsb"])
            for g in range(G):
                S.op("dve", (lambda g: lambda e: e.max(out=m8[:, g, :], in_=gsb[:, g, :]))(g), reads=["gsb", "m8"], writes=["m8"])
            for g in range(G):
                S.op("dve", (lambda g: lambda e: e.tensor_scalar(out=sel[:, g, :], in0=gsb[:, g, :], scalar1=m8[:, g, 2:3], scalar2=BIG,
                                                                 op0=ALU.is_ge, op1=ALU.mult))(g), reads=["gsb", "m8", "sel"], writes=["sel"])
            for g in range(G):
                S.op("dve", (lambda g: lambda e: e.tensor_scalar(out=q_aug[qs][:, 4 * g:4 * g + 4, 65:W],
                                                                 in0=sel[:, g, :].unsqueeze(1).to_broadcast([128, 4, 8]),
                                                                 scalar1=-BIG, scalar2=None, op0=ALU.add))(g),
                     reads=["sel", "q_aug%d" % qs], writes=["q_aug%d" % qs])
            S.op("dve", lambda e: e.memset(q_aug[qs][:, :, 65 + own:66 + own], 0.0), reads=["q_aug%d" % qs], writes=["q_aug%d" % qs])
            fence_dve(["q_aug%d" % qs])
            if dbg_tile is not None and i == dbg_tile and sq == 0:
                dbg_dump("gsb", gsb[:], [128, G, 8], F32, ["gsb"])
                dbg_dump("qaug", q_aug[qs][:], [128, H, W], BF16, ["q_aug%d" % qs])
        else:
            S.op("dve", lambda e: e.memset(q_aug[qs][:, :, 65:W], 0.0), reads=["q_aug%d" % qs], writes=["q_aug%d" % qs])
            fence_dve(["q_aug%d" % qs])
        yield
        q_transpose(qs, W)
        yield
        gate_group(qs, 0, 1, 1536)
        yield
        gate_group(qs, 1, 1, 2048)
        yield

    def stage2_B(sq, i, tc):
        xs = tc % 2
        qs = tc % 2
        yield from attention(i, qs, W, None)
        yield from finish_tile(sq, i, xs, qs, out_d, "outo")

    class _Null:
        pass

    def count_steps(genfn, args):
        saved = (bank_ctr[0], pt_ctr[0], rb_ctr[0], set(held), dbg_n[0])
        real_op, real_dma = S.op, S.dma
        S.op = lambda *a, **k: None
        S.dma = lambda *a, **k: None
        n = 0
        for _ in genfn(*args):
            n += 1
        S.op, S.dma = real_op, real_dma
        bank_ctr[0], pt_ctr[0], rb_ctr[0] = saved[0], saved[1], saved[2]
        held.clear()
        held.update(saved[3])
        dbg_n[0] = saved[4]
        return n + 1

    def drain(gen):
        for _ in gen:
            pass

    def run_phase(tiles, s1, s2):
        prev = None
        for t in list(tiles) + [None]:
            if not pipeline:
                if t is not None:
                    drain(s1(*t))
                    drain(s2(*t))
                continue
            if t is not None and prev is not None:
                n1 = count_steps(s1, t)
                n2 = count_steps(s2, prev)
                ga, gb = s1(*t), s2(*prev)
                d1 = d2 = 0
                a_alive = b_alive = True
                while a_alive or b_alive:
                    if b_alive and (not a_alive or d2 * n1 <= d1 * n2):
                        try:
                            next(gb)
                            d2 += 1
                        except StopIteration:
                            b_alive = False
                    else:
                        try:
                            next(ga)
                            d1 += 1
                        except StopIteration:
                            a_alive = False
            elif t is not None:
                drain(s1(*t))
            elif prev is not None:
                drain(s2(*prev))
            prev = t

    tile_ctr = [0]

    def tiles_of_phase():
        out = []
        for sq in range(nseq):
            for i in range(ntiles):
                out.append((sq, i, tile_ctr[0]))
                tile_ctr[0] += 1
        return out

    if doA:
        load_gain(g1, norm_a_g, 32.0, "g1")
        load_gk(kn_a_g, qn_a_g)
        load_w(wbig, w_in_a, A_COLS, 0, "wbig")
        load_w(wo, w_out_a, D, 0, "wo")
        run_phase(tiles_of_phase(), stage1_A, stage2_A)

    if doB:
        load_gain(g1, norm_kv_g, 32.0, "g1")
        S.dma("sp", "g2", lambda e: e.dma_start(out=g2, in_=norm_b_g.partition_broadcast(128)), writes=["isc"])
        S.op("dve", lambda e: e.tensor_scalar(out=g2, in0=g2, scalar1=32.0, scalar2=None, op0=ALU.mult), reads=["isc"], writes=["isc"])
        load_gk(kn_b_g, qn_b_g)
        load_w(wbig, w_kv, 512, 0, "wbig")
        load_w(wbig, w_in_b, 2048, 512, "wbig")
        load_w(wo, w_out_b, D, 0, "wo")
        S.op("pool", lambda e: e.memset(kmT[:], 0.0), writes=["kmT"])
        S.op("pool", lambda e: e.memset(kmTf[:], 0.0), writes=["kmTf"])
        run_phase(tiles_of_phase(), stage1_B, stage2_B)

    S.finalize()
    S.emit()
    es.close()
    return nc


_PROG_CACHE = {}


def _get_prog(mode):
    if mode not in _PROG_CACHE:
        _PROG_CACHE[mode] = build_program(mode)
    return _PROG_CACHE[mode]


FUSED = True


def kernel(x, norm_a_g, w_in_a, qn_a_g, kn_a_g, w_out_a, rel_bias, norm_kv_g, w_kv,
           kn_b_g, norm_b_g, w_in_b, qn_b_g, w_out_b):
    f = lambda a: np.ascontiguousarray(np.asarray(a, dtype=np.float32))
    x = f(x)
    rel_bias = f(rel_bias)
    nbias = _near_bias_layout(rel_bias).reshape(2, 128, H * 128)
    a_in = {"w_in_a": f(w_in_a)[0], "w_out_a": f(w_out_a)[0], "norm_a_g": f(norm_a_g)[0],
            "qn_a_g": f(qn_a_g)[0], "kn_a_g": f(kn_a_g)[0]}
    b_in = {"w_kv": f(w_kv), "w_in_b": f(w_in_b)[0], "w_out_b": f(w_out_b)[0], "norm_kv_g": f(norm_kv_g),
            "norm_b_g": f(norm_b_g)[0], "kn_b_g": f(kn_b_g), "qn_b_g": f(qn_b_g)[0]}
    common = {"rel_bias": rel_bias, "nbias": nbias}
    xs = [np.ascontiguousarray(x[NSEQ * c:NSEQ * (c + 1)]) for c in range(NCORES)]
    cores = list(range(NCORES))
    if FUSED:
        nc = _get_prog("AB")
        in_maps = [dict(x=xs[c], **a_in, **b_in, **common) for c in cores]
        res = run_bass_kernel_spmd(nc, in_maps, core_ids=cores)
        outs = [r["out"] for r in res.results]
    else:
        ncA = _get_prog("A")
        resA = run_bass_kernel_spmd(ncA, [dict(x=xs[c], **a_in, **common) for c in cores], core_ids=cores)
        h1 = [r["h1"] for r in resA.results]
        ncB = _get_prog("B")
        resB = run_bass_kernel_spmd(ncB, [dict(h1=h1[c], **b_in, **common) for c in cores], core_ids=cores)
        outs = [r["out"] for r in resB.results]
    return np.concatenate(outs, axis=0).astype(np.float32)
```

```python
import numpy as np
import concourse.bass as bass
import concourse.mybir as mybir
from concourse.bass_utils import run_bass_kernel_spmd
from contextlib import ExitStack

F32 = mybir.dt.float32
BF16 = mybir.dt.bfloat16
AF = mybir.ActivationFunctionType
ALU = mybir.AluOpType
AX = mybir.AxisListType

T = 2048
D = 1024
NT = 16
NSEQ = 2
NCORES = 8
H = 16
G = 4
DH = 64
EPS = 1e-6
BIG = 30000.0
A_COLS = 3144
NBIS = 16
TOPK = 256

ENGS = ("pe", "act", "dve", "pool", "sp")


class Buf:
    __slots__ = ("name", "last_w", "readers")

    def __init__(self, name):
        self.name = name
        self.last_w = None
        self.readers = []


class Op:
    __slots__ = ("eng", "idx", "fn", "deps", "inc", "incval", "stream", "is_dma", "waits")

    def __init__(self, eng, idx, fn, is_dma=False, stream=None):
        self.eng = eng
        self.idx = idx
        self.fn = fn
        self.deps = []
        self.inc = False
        self.incval = 0
        self.stream = stream
        self.is_dma = is_dma
        self.waits = []


class Sched:
    def __init__(self, nc, es):
        self.nc = nc
        self.es = es
        self.ops = {e: [] for e in ENGS}
        self.streams = {}
        self.sems = {}
        self.bufs = {}

    def _B(self, x):
        if isinstance(x, Buf):
            return x
        b = self.bufs.get(x)
        if b is None:
            b = self.bufs[x] = Buf(x)
        return b

    def _record(self, op, reads, writes):
        deps = []
        for r in reads:
            r = self._B(r)
            if r.last_w is not None:
                deps.append(r.last_w)
            r.readers.append(op)
        for w in writes:
            w = self._B(w)
            if w.last_w is not None:
                deps.append(w.last_w)
            last = {}
            for x in w.readers:
                if x is op:
                    continue
                key = ("dma", x.stream, x.idx) if x.is_dma else x.eng
                if key not in last or x.idx > last[key].idx:
                    last[key] = x
            deps.extend(last.values())
            w.last_w = op
            w.readers = []
        op.deps = deps

    def op(self, eng, fn, reads=(), writes=()):
        o = Op(eng, len(self.ops[eng]), fn)
        self.ops[eng].append(o)
        self._record(o, reads, writes)
        return o

    def dma(self, eng, stream, fn, reads=(), writes=()):
        o = Op(eng, len(self.ops[eng]), fn, is_dma=True, stream=stream)
        self.ops[eng].append(o)
        self.streams.setdefault(stream, []).append(o)
        self._record(o, reads, writes)
        return o

    def finalize(self, final_wait_eng="sp"):
        fin = Op(final_wait_eng, len(self.ops[final_wait_eng]), None)
        fin.deps = [lst[-1] for lst in self.streams.values()]
        self.ops[final_wait_eng].append(fin)
        for name, lst in self.streams.items():
            for i, o in enumerate(lst):
                o.incval = 16 * (i + 1)
        sel = {}
        for e in ENGS:
            waited = {}
            for o in self.ops[e]:
                need = {}
                for d in o.deps:
                    if d.is_dma:
                        key = "dma:" + d.stream
                        pos = d.incval
                    else:
                        if d.eng == o.eng and d.eng == "pe":
                            continue
                        key = d.eng
                        pos = d.idx
                    if key not in need or pos > need[key][0]:
                        need[key] = (pos, d)
                lst = []
                for k, (pos, d) in need.items():
                    if pos > waited.get(k, -1):
                        waited[k] = pos
                        lst.append((k, d))
                        if not d.is_dma:
                            d.inc = True
                sel[id(o)] = lst
        for e in ENGS:
            c = 0
            for o in self.ops[e]:
                if o.is_dma:
                    continue
                if o.inc:
                    c += 1
                    o.incval = c
        for e in ENGS:
            for o in self.ops[e]:
                o.waits = [(k, d.incval) for (k, d) in sel[id(o)]]

    def emit(self):
        nc = self.nc
        keys = list(ENGS) + ["dma:" + s for s in self.streams]
        for k in keys:
            self.sems[k] = self.es.enter_context(nc.semaphore("s_" + k.replace(":", "_")))
        sems = self.sems
        ops = self.ops

        def run(engname, eng):
            for o in ops[engname]:
                if o.fn is None:
                    for (k, v) in o.waits:
                        eng.wait_ge(sems[k], v)
                    continue
                for (k, v) in o.waits[1:]:
                    eng.wait_ge(sems[k], v)
                ins = o.fn(eng)
                if o.waits:
                    ins._wait_ge(sems[o.waits[0][0]], o.waits[0][1])
                if o.is_dma:
                    ins.then_inc(sems["dma:" + o.stream], 16)
                elif o.inc:
                    ins.then_inc(sems[engname], 1)

        with nc.Block() as block:
            @block.tensor
            def _(e):
                run("pe", e)

            @block.scalar
            def _(e):
                run("act", e)

            @block.vector
            def _(e):
                run("dve", e)

            @block.gpsimd
            def _(e):
                run("pool", e)

            @block.sync
            def _(e):
                run("sp", e)


def _rel_bucket_np(dist):
    n = np.maximum(dist, 0)
    max_exact = 16
    nf = np.maximum(n, 1).astype(np.float32)
    large = max_exact + (np.log(nf / np.float32(max_exact)) / np.float32(np.log(128 / max_exact))
                         * np.float32(32 - max_exact)).astype(np.int32)
    large = np.minimum(large, 31)
    return np.where(n < max_exact, n, large)


def _near_bias_layout(rel_bias):
    s = np.arange(128)[:, None]
    t = np.arange(128)[None, :]
    out = np.empty((2, 128, H, 128), np.float32)
    for d in range(2):
        bk = _rel_bucket_np(128 * d + t - s)
        out[d] = np.transpose(rel_bias[bk], (0, 2, 1))
    return np.ascontiguousarray(out)


def build_program(mode="AB", nseq=NSEQ, ntiles=NT, dbg_tile=None, pipeline=True):
    nc = bass.Bass("TRN2", target_bir_lowering=False)
    doA = "A" in mode
    doB = "B" in mode

    def din(name, shape):
        return nc.dram_tensor(name, shape, F32, kind="ExternalInput").ap()

    if doA:
        x_d = din("x", [nseq, T, D])
        w_in_a = din("w_in_a", [D, A_COLS])
        w_out_a = din("w_out_a", [D, D])
        norm_a_g = din("norm_a_g", [D])
        qn_a_g = din("qn_a_g", [DH])
        kn_a_g = din("kn_a_g", [DH])
    if doB:
        w_kv = din("w_kv", [D, 512])
        w_in_b = din("w_in_b", [D, 2048])
        w_out_b = din("w_out_b", [D, D])
        norm_kv_g = din("norm_kv_g", [D])
        norm_b_g = din("norm_b_g", [D])
        kn_b_g = din("kn_b_g", [DH])
        qn_b_g = din("qn_b_g", [DH])
        out_d = nc.dram_tensor("out", [nseq, T, D], F32, kind="ExternalOutput").ap()
    rel_bias = din("rel_bias", [32, H])
    nbias_d = din("nbias", [2, 128, H * 128])
    if mode == "A":
        h1_d = nc.dram_tensor("h1", [nseq, T, D], F32, kind="ExternalOutput").ap()
    elif mode == "B":
        h1_d = din("h1", [nseq, T, D])
    else:
        h1_d = nc.dram_tensor("h1", [nseq, T, D], F32, kind="Internal").ap()

    es = ExitStack()
    S = Sched(nc, es)

    def sb(name, shape, dt):
        return es.enter_context(nc.sbuf_tensor(name, shape, dt))

    def ps(name, shape, dt):
        return es.enter_context(nc.psum_tensor(name, shape, dt))

    W = 73

    ident = sb("ident", [128, 128], BF16)
    NB = sb("NB", [128, 2, H, 128], BF16)
    g1 = sb("g1", [128, D], F32)
    gk = sb("gk", [128, DH], F32)
    gtmp = sb("gtmp", [128, DH], F32)
    cb = sb("cb", [128, H], F32)
    epsb = sb("epsb", [128, 2], F32)
    negc = sb("negc", [128, 128], F32)
    halfpow = sb("halfpow", [128, NBIS], F32)
    vc = sb("vc", [128, NT, G, 65], BF16)
    kTc = sb("kTc", [W, G, T], BF16)
    ikT = sb("ikT", [64, T], BF16)
    wbig = sb("wbig", [128, 8, A_COLS], BF16)
    wo = sb("wo", [128, 8, D], BF16)
    xt = [sb("xt%d" % k, [128, D], F32) for k in range(2)]
    junk_a = sb("junk_a", [128, D], BF16)
    sqb = sb("sqb", [128, D], F32)
    hn0 = sb("hn0", [128, D], BF16)
    hT0 = sb("hT0", [128, 8, 128], BF16)
    q_aug = [sb("q_aug%d" % k, [128, H, W], BF16) for k in range(2)]
    k_aug = sb("k_aug", [128, G, W], BF16)
    ktmp = sb("ktmp", [128, G * DH], F32)
    qT = [sb("qT%d" % k, [W, H, 128], BF16) for k in range(2)]
    thb = sb("thb", [128, D], F32)
    sg = [sb("sg%d" % k, [128, D], BF16) for k in range(2)]
    sgp = sb("sgp", [128, D], BF16)
    sgT = sb("sgT", [128, 8, 128], BF16)
    st = sb("st", [128, 96], F32)
    ik_tok = sb("ik_tok", [128, 64], BF16)
    iq_tok = sb("iq_tok", [128, 512], BF16)
    iqT = sb("iqT", [64, 8, 128], BF16)
    wst = sb("wst", [128, 16], F32)
    dsg = sb("dsg", [128, 8, 128], BF16)
    NRB = 4
    Rb = [sb("Rb%d" % k, [128, 512], BF16) for k in range(NRB)]
    isc = sb("isc", [128, T], F32)
    bis = sb("bis", [128, 8 + 2 * NBIS], F32)
    g2 = isc[:, 0:1024]
    hn1 = isc[:, 1024:1536].bitcast(BF16)
    hT1 = isc[:, 1536:2048].bitcast(BF16)
    HN = [(hn0[:, :], "hn0"), (hn1, "isc")]
    HT = [(hT0[:, :, :].rearrange("p a b -> p (a b)"), "hT0"), (hT1, "isc")]
    mask_tok = sb("mask_tok", [128, T], BF16)
    maskT = [sb("maskT%d" % k, [128, NT, 128], BF16) for k in range(2)]
    NPT = 5
    pT = [sb("pT%d" % k, [128, 512], BF16) for k in range(NPT)]
    drow = [sb("drow%d" % k, [1, 512], BF16) for k in range(2)]
    ones1 = sb("ones1", [1, 8], BF16)
    rdt = sb("rdt", [128, H], F32)
    ogT = sb("ogT", [128, 8, 128], BF16)
    if doB:
        kmT = sb("kmT", [64, G, 8], BF16)
        kmTf = sb("kmTf", [64, G, 8], F32)
        qsum = sb("qsum", [128, G, DH], BF16)
        qsumf = sb("qsumf", [128, G, DH], F32)
        qsT = sb("qsT", [64, G, 128], BF16)
        gsb = sb("gsb", [128, G, 8], F32)
        m8 = sb("m8", [128, G, 8], F32)
        sel = sb("sel", [128, G, 8], F32)

    RBK = ps("PS_R", [128, 1536], F32)
    PO = ps("PS_O", [128, 2048], F32)
    PT = ps("PS_T", [128, 1024], BF16)
    PTF = PT[:, :].bitcast(F32)
    NRK = 3
    R = [RBK[:, k * 512:(k + 1) * 512] for k in range(NRK)]
    RN = ["R%d" % k for k in range(NRK)]
    OB = ["O0", "O1", "O2", "O3"]
    bank_ctr = [0]
    held = set()

    def bank(hold=False):
        while True:
            k = bank_ctr[0] % NRK
            bank_ctr[0] += 1
            if k not in held:
                break
        if hold:
            held.add(k)
        return k

    S.op("pool", lambda e: e.memset(isc[:, 0:128], 0.0), writes=["isc"])
    S.op("pool", lambda e: e.affine_select(out=isc[:, 0:128], in_=isc[:, 0:128], pattern=[[-1, 128]],
                                           compare_op=ALU.not_equal, fill=1.0, base=0, channel_multiplier=1),
         reads=["isc"], writes=["isc"])
    S.op("dve", lambda e: e.tensor_copy(out=ident[:], in_=isc[:, 0:128]), reads=["isc"], writes=["ident"])
    S.op("pool", lambda e: e.memset(junk_a[:, :], 0.0))
    S.op("pool", lambda e: e.memset(ones1[:, :], 1.0), writes=["ones1"])
    S.op("pool", lambda e: e.memset(epsb[:, 0:1], float(D * EPS)), writes=["epsb"])
    S.op("pool", lambda e: e.memset(epsb[:, 1:2], float(DH * EPS)), reads=["epsb"], writes=["epsb"])
    S.op("pool", lambda e: e.memset(negc[:], 0.0), writes=["negc"])
    S.op("pool", lambda e: e.affine_select(out=negc[:], in_=negc[:], pattern=[[-1, 128]],
                                           compare_op=ALU.is_ge, fill=-BIG, base=0, channel_multiplier=1),
         reads=["negc"], writes=["negc"])
    for k in range(NBIS):
        S.op("pool", (lambda k: lambda e: e.memset(halfpow[:, k:k + 1], 2.0 ** -(k + 1)))(k), writes=["halfpow"])
    S.op("pool", lambda e: e.memset(vc[:, :, :, 64:65], 1.0), writes=["vc_init"])
    S.op("pool", lambda e: e.memset(k_aug[:, :, 64:W], 0.0), writes=["k_aug"])
    S.op("pool", lambda e: e.memset(k_aug[:, :, 64:65], 1.0), reads=["k_aug"], writes=["k_aug"])
    S.dma("sp", "cb", lambda e: e.dma_start(out=cb[:], in_=rel_bias[31, :].partition_broadcast(128)), writes=["cb"])
    for k in range(2):
        S.op("pool", (lambda k: lambda e: e.memset(q_aug[k][:, :, 64:W], 0.0))(k), writes=["q_aug%d" % k])
        S.op("dve", (lambda k: lambda e: e.tensor_copy(out=q_aug[k][:, :, 64:65], in_=cb[:, :].unsqueeze(2)))(k),
             reads=["cb", "q_aug%d" % k], writes=["q_aug%d" % k])
    for d in range(2):
        S.dma("sp", "isc", (lambda d: lambda e: e.dma_start(out=isc[:, :], in_=nbias_d[d]))(d), writes=["isc"])
        S.op("dve", lambda e: e.tensor_tensor(out=isc[:, :].rearrange("p (h t) -> p h t", h=H),
                                              in0=isc[:, :].rearrange("p (h t) -> p h t", h=H),
                                              in1=cb[:, :].unsqueeze(2).to_broadcast([128, H, 128]),
                                              op=ALU.subtract), reads=["isc", "cb"], writes=["isc"])
        if d == 0:
            S.op("pool", lambda e: e.affine_select(out=isc[:, :].rearrange("p (h t) -> p h t", h=H),
                                                   in_=isc[:, :].rearrange("p (h t) -> p h t", h=H),
                                                   pattern=[[0, H], [1, 128]], compare_op=ALU.is_ge, fill=-BIG,
                                                   base=0, channel_multiplier=-1), reads=["isc"], writes=["isc"])
        S.op("act", (lambda d: lambda e: e.copy(out=NB[:, d, :, :], in_=isc[:, :].rearrange("p (h t) -> p h t", h=H)))(d),
             reads=["isc"], writes=["NB"])

    def load_gain(dst, src_ap, scale, name):
        S.dma("sp", name, lambda e: e.dma_start(out=dst[:], in_=src_ap.partition_broadcast(128)), writes=[name])
        S.op("dve", lambda e: e.tensor_scalar(out=dst[:], in0=dst[:], scalar1=float(scale), scalar2=None, op0=ALU.mult),
             reads=[name], writes=[name])

    def load_gk(kn_ap, qn_ap):
        S.dma("sp", "gk", lambda e: e.dma_start(out=gk[:], in_=kn_ap.partition_broadcast(128)), writes=["gk"])
        S.dma("sp", "gtmp", lambda e: e.dma_start(out=gtmp[:], in_=qn_ap.partition_broadcast(128)), writes=["gtmp"])
        S.op("dve", lambda e: e.scalar_tensor_tensor(out=gk[:], in0=gk[:], scalar=8.0, in1=gtmp[:],
                                                     op0=ALU.mult, op1=ALU.mult), reads=["gk", "gtmp"], writes=["gk"])

    def load_w(dst, src, ncols, col0, bufname):
        srcv = src.rearrange("(kc p) n -> p kc n", p=128)
        for kc in range(8):
            S.dma("pool", bufname + str(kc),
                  (lambda kc: lambda e: e.dma_start(out=dst[:, kc, col0:col0 + ncols], in_=srcv[:, kc, :]))(kc),
                  writes=[bufname + str(kc)])

    pt_ctr = [0]
    rb_ctr = [0]
    dbg_n = [0]

    def fence_dve(bufs):
        S.op("dve", lambda e: e.tensor_copy(out=junk_a[:, 0:512], in_=junk_a[:, 512:1024]), reads=bufs, writes=bufs)

    def dbg_dump(name, ap, shape, dt, bufs):
        dn = nc.dram_tensor("dbg_" + name, list(shape), dt, kind="ExternalOutput").ap()
        dbg_n[0] += 1
        S.dma("sp", "dbg%d" % dbg_n[0], lambda e: e.dma_start(out=dn, in_=ap), reads=bufs)

    def rsqrt_act(out_ap, in_ap, c, rbuf, wbuf, toff):
        n = in_ap.shape[1]
        tmp = st[:, toff:toff + n]
        tn = "st_tmp%d" % toff
        S.op("act", lambda e: e.activation(out=tmp, in_=in_ap, func=AF.Ln, bias=epsb[:, {1024: 0, 64: 1}[int(round(c / EPS))]:{1024: 1, 64: 2}[int(round(c / EPS))]]),
             reads=[rbuf, "epsb"], writes=[tn])
        S.op("act", lambda e: e.activation(out=out_ap, in_=tmp, func=AF.Exp, scale=-0.5), reads=[tn], writes=[wbuf])

    def rms_and_transpose(xs, gains, nh):
        S.op("act", lambda e: e.activation(out=junk_a[:], in_=xt[xs][:], func=AF.Square, accum_out=st[:, 0:1]),
             reads=["xt%d" % xs], writes=["st_ss"])
        rsqrt_act(st[:, 1:2], st[:, 0:1], float(D * EPS), "st_ss", "st_r", 2)
        for k in range(nh):
            hn_ap, hn_nm = HN[k]
            ht_ap, ht_nm = HT[k]
            g_ap, g_nm = gains[k]
            S.op("dve", (lambda hn_ap, g_ap: lambda e: e.scalar_tensor_tensor(out=hn_ap, in0=xt[xs][:], scalar=st[:, 1:2],
                                                                             in1=g_ap, op0=ALU.mult, op1=ALU.mult))(hn_ap, g_ap),
                 reads=["xt%d" % xs, "st_r", g_nm], writes=[hn_nm])
            for kc in range(8):
                S.op("pe", (lambda hn_ap, kc: lambda e: e.transpose(out=PT[:, kc * 128:(kc + 1) * 128],
                                                                   in_=hn_ap[:, kc * 128:(kc + 1) * 128], identity=ident[:]))(hn_ap, kc),
                     reads=[hn_nm, "ident"], writes=["T"])
            S.op("act", (lambda ht_ap: lambda e: e.copy(out=ht_ap, in_=PT[:, :]))(ht_ap), reads=["T"], writes=[ht_nm])

    def proj(b, hk, col0, ncols):
        ht_ap, ht_nm = HT[hk]
        for kc in range(8):
            S.op("pe", (lambda kc: lambda e: e.matmul(R[b][:, 0:ncols], lhsT=ht_ap[:, kc * 128:(kc + 1) * 128], rhs=wbig[:, kc, col0:col0 + ncols],
                                                      start=(kc == 0), stop=(kc == 7)))(kc),
                 reads=[ht_nm, "wbig%d" % kc], writes=[RN[b]])

    def q_group(qs, grp, hk, col0):
        b = bank()
        proj(b, hk, col0, 512)
        so = 16 + 8 * grp
        S.op("act", lambda e: e.activation(out=sqb[:, grp * 512:(grp + 1) * 512], in_=R[b], func=AF.Square),
             reads=[RN[b]], writes=["sqb%d" % grp])
        S.op("dve", lambda e: e.tensor_reduce(out=st[:, so:so + 8], in_=sqb[:, grp * 512:(grp + 1) * 512].rearrange("p (h d) -> p h d", h=8),
                                              axis=AX.X, op=ALU.add), reads=["sqb%d" % grp], writes=["st_q%d" % grp])
        rsqrt_act(st[:, so:so + 8], st[:, so:so + 8], float(DH * EPS), "st_q%d" % grp, "st_q%d" % grp, 40 + 8 * grp)
        S.op("dve", lambda e: e.tensor_tensor(out=q_aug[qs][:, 8 * grp:8 * grp + 8, 0:DH], in0=R[b].rearrange("p (h d) -> p h d", h=8),
                                              in1=st[:, so:so + 8].unsqueeze(2).to_broadcast([128, 8, DH]), op=ALU.mult),
             reads=[RN[b], "st_q%d" % grp, "q_aug%d" % qs], writes=["q_aug%d" % qs])

    def kv_group(i, hk, col0):
        b = bank()
        proj(b, hk, col0, 512)
        kv_ap = R[b]
        S.op("act", lambda e: e.activation(out=junk_a[:, 0:256], in_=kv_ap[:, 0:256], func=AF.Square), reads=[RN[b]], writes=["sqk"])
        S.op("dve", lambda e: e.tensor_reduce(out=st[:, 32:36], in_=junk_a[:, 0:256].rearrange("p (g d) -> p g d", g=G),
                                              axis=AX.X, op=ALU.add), reads=["sqk"], writes=["st_k"])
        rsqrt_act(st[:, 32:36], st[:, 32:36], float(DH * EPS), "st_k", "st_k", 56)
        S.op("dve", lambda e: e.tensor_tensor(out=ktmp[:, :].rearrange("p (g d) -> p g d", g=G),
                                              in0=kv_ap[:, 0:256].rearrange("p (g d) -> p g d", g=G),
                                              in1=st[:, 32:36].unsqueeze(2).to_broadcast([128, G, DH]), op=ALU.mult),
             reads=[RN[b], "st_k"], writes=["ktmp"])
        S.op("pool", lambda e: e.tensor_tensor(out=k_aug[:, :, 0:DH], in0=ktmp[:, :].rearrange("p (g d) -> p g d", g=G),
                                               in1=gk[:, :].unsqueeze(1).to_broadcast([128, G, DH]), op=ALU.mult),
             reads=["ktmp", "gk", "k_aug"], writes=["k_aug"])
        S.op("act", lambda e: e.copy(out=vc[:, i, :, 0:DH], in_=kv_ap[:, 256:512].rearrange("p (g d) -> p g d", g=G)),
             reads=[RN[b], "vc_init"], writes=["vc_%d" % i])

    def k_transpose(i, Wk):
        for g in range(G):
            S.op("pe", (lambda g: lambda e: e.transpose(out=PT[0:Wk, g * 128:(g + 1) * 128], in_=k_aug[:, g, 0:Wk],
                                                        identity=ident[:]))(g), reads=["k_aug", "ident"], writes=["T"])
        S.op("act", lambda e: e.copy(out=kTc[0:Wk, :, i * 128:(i + 1) * 128],
                                     in_=PT[0:Wk, 0:512].rearrange("p (g t) -> p g t", g=G)), reads=["T"], writes=["kTc_%d" % i])

    def gate_group(gs_, grp, hk, col0):
        b = bank()
        proj(b, hk, col0, 512)
        tb = thb[:, grp * 512:(grp + 1) * 512]
        tn = "thb%d" % grp
        S.op("act", lambda e: e.activation(out=tb, in_=R[b], func=AF.Exp, scale=-1.0), reads=[RN[b]], writes=[tn])
        S.op("act", lambda e: e.activation(out=tb, in_=tb, func=AF.Ln, bias=1.0), reads=[tn], writes=[tn])
        S.op("act", lambda e: e.activation(out=tb, in_=tb, func=AF.Exp, scale=-1.0), reads=[tn], writes=[tn])
        S.op("dve", lambda e: e.tensor_tensor(out=sg[gs_][:, grp * 512:(grp + 1) * 512], in0=R[b], in1=tb, op=ALU.mult),
             reads=[tn, RN[b], "sg%d" % gs_], writes=["sg%d" % gs_])

    def q_transpose(qs, Wq):
        for rnd in range(2):
            b = bank()
            pv = R[b].bitcast(BF16)
            for hh in range(8):
                h = rnd * 8 + hh
                S.op("pe", (lambda h, hh, pv: lambda e: e.transpose(out=pv[0:Wq, hh * 128:(hh + 1) * 128], in_=q_aug[qs][:, h, 0:Wq],
                                                                    identity=ident[:]))(h, hh, pv), reads=["q_aug%d" % qs, "ident"], writes=[RN[b]])
            S.op("act", (lambda rnd, pv: lambda e: e.copy(out=qT[qs][0:Wq, rnd * 8:(rnd + 1) * 8, :].rearrange("p a b -> p (a b)"), in_=pv[0:Wq, :]))(rnd, pv),
                 reads=[RN[b], "qT%d" % qs], writes=["qT%d" % qs])

    def attention(i, qs, Wq, ms):
        steps = [(j, g) for j in range(i + 1) for g in range(G)]
        info = {}

        def qk(n):
            j, g = steps[n]
            near = (i - j) <= 1
            slot = pt_ctr[0] % NPT
            pt_ctr[0] += 1
            b = bank(hold=True)
            info[n] = (b, slot)
            masked = ms is not None
            S.op("pe", (lambda g, j, b: lambda e: e.matmul(
                R[b], lhsT=kTc[0:Wq, g, j * 128:(j + 1) * 128],
                rhs=qT[qs][0:Wq, 4 * g:4 * g + 4, :].rearrange("p a b -> p (a b)"),
                start=True, stop=(not near and not masked)))(g, j, b),
                reads=["kTc_%d" % j, "qT%d" % qs], writes=[RN[b]])
            if near:
                S.op("pe", (lambda g, j, b: lambda e: e.matmul(
                    R[b], lhsT=ident[:], rhs=NB[:, i - j, 4 * g:4 * g + 4, :].rearrange("p a b -> p (a b)"),
                    start=False, stop=(not masked)))(g, j, b), reads=["ident", "NB"], writes=[RN[b]])
            if masked:
                S.op("pe", (lambda j, b: lambda e: e.matmul(
                    R[b].rearrange("p (h t) -> p h t", h=4), lhsT=ident[:],
                    rhs=maskT[ms][:, j, :].unsqueeze(1).to_broadcast([128, 4, 128]),
                    start=False, stop=True))(j, b), reads=["ident", "maskT%d" % ms], writes=[RN[b]])

        qk(0)
        for n in range(len(steps)):
            j, g = steps[n]
            b, slot = info.pop(n)
            S.op("act", (lambda b, slot: lambda e: e.activation(out=pT[slot][:], in_=R[b], func=AF.Exp))(b, slot),
                 reads=[RN[b]], writes=["pT%d" % slot])
            held.discard(b)
            if n + 1 < len(steps):
                qk(n + 1)
            S.op("pe", (lambda g, slot, j: lambda e: e.matmul(
                PO[0:65, g * 512:(g + 1) * 512], lhsT=vc[:, j, g, :], rhs=pT[slot][:, :],
                start=(j == 0), stop=(j == i)))(g, slot, j),
                reads=["pT%d" % slot, "vc_%d" % j], writes=[OB[g]])
            yield

    def finish_tile(sq, i, xs, gs_, dst_d, out_stream):
        bd = bank()
        for g in range(G):
            rs = g % 2
            S.op("act", (lambda g, rs: lambda e: e.copy(out=drow[rs][0:1, :], in_=PO[64:65, g * 512:(g + 1) * 512]))(g, rs),
                 reads=[OB[g]], writes=["drow%d" % rs])
            for jh in range(4):
                h = 4 * g + jh
                S.op("pe", (lambda rs, jh, h, bd: lambda e: e.matmul(R[bd][:, h:h + 1], lhsT=drow[rs][0:1, jh * 128:(jh + 1) * 128],
                                                                     rhs=ones1[0:1, 0:1], start=True, stop=True))(rs, jh, h, bd),
                     reads=["drow%d" % rs, "ones1"], writes=[RN[bd]])
        S.op("dve", lambda e: e.reciprocal(out=rdt[:, :], in_=R[bd][:, 0:H]), reads=[RN[bd]], writes=["rdt"])
        S.op("dve", lambda e: e.tensor_tensor(out=sgp[:, :].rearrange("p (h d) -> p h d", h=H),
                                              in0=sg[gs_][:, :].rearrange("p (h d) -> p h d", h=H),
                                              in1=rdt[:, :].unsqueeze(2).to_broadcast([128, H, DH]), op=ALU.mult),
             reads=["sg%d" % gs_, "rdt"], writes=["sgp"])
        yield
        bt = bank()
        ptv = R[bt].bitcast(BF16)
        for kc in range(8):
            S.op("pe", (lambda kc: lambda e: e.transpose(out=ptv[:, kc * 128:(kc + 1) * 128], in_=sgp[:, kc * 128:(kc + 1) * 128],
                                                         identity=ident[:]))(kc), reads=["sgp", "ident"], writes=[RN[bt]])
        S.op("act", lambda e: e.copy(out=sgT[:, :, :].rearrange("p a b -> p (a b)"), in_=ptv[:, :]), reads=[RN[bt]], writes=["sgT"])
        yield
        for g in range(G):
            for par in range(2):
                S.op("dve", (lambda g, par: lambda e: e.tensor_tensor(
                    out=ogT[par * 64:(par + 1) * 64, 2 * g:2 * g + 2, :],
                    in0=PO[0:64, g * 512:(g + 1) * 512].rearrange("p (a b t) -> p a b t", a=2, b=2)[:, :, par, :],
                    in1=sgT[par * 64:(par + 1) * 64, 2 * g:2 * g + 2, :], op=ALU.mult))(g, par),
                    reads=[OB[g], "sgT", "ogT"], writes=["ogT"])
            yield
        for nb in range(2):
            b = bank()
            for kc in range(8):
                S.op("pe", (lambda nb, kc, b: lambda e: e.matmul(R[b], lhsT=ogT[:, kc, :], rhs=wo[:, kc, nb * 512:(nb + 1) * 512],
                                                                 start=(kc == 0), stop=(kc == 7)))(nb, kc, b),
                     reads=["ogT", "wo%d" % kc], writes=[RN[b]])
            S.op("dve", (lambda nb, b: lambda e: e.tensor_tensor(out=xt[xs][:, nb * 512:(nb + 1) * 512], in0=xt[xs][:, nb * 512:(nb + 1) * 512],
                                                                 in1=R[b], op=ALU.add))(nb, b),
                 reads=["xt%d" % xs, RN[b]], writes=["xt%d" % xs])
            yield
        S.dma("sp", out_stream + str(xs), lambda e: e.dma_start(out=dst_d[sq, i * 128:(i + 1) * 128, :], in_=xt[xs][:]),
              reads=["xt%d" % xs], writes=[out_stream + "_dram%d" % xs])

    def stage1_A(sq, i, tc):
        xs = tc % 2
        qs = tc % 2
        use_sel = i >= 2
        S.dma("sp", "xt%d" % xs, lambda e: e.dma_start(out=xt[xs][:], in_=x_d[sq, i * 128:(i + 1) * 128, :]), writes=["xt%d" % xs])
        rms_and_transpose(xs, [(g1[:], "g1")], 1)
        yield
        kv_group(i, 0, 1024)
        yield
        b = bank()
        proj(b, 0, 3072, 72)
        S.op("act", lambda e: e.copy(out=ik_tok[:], in_=R[b][:, 8:72]), reads=[RN[b]], writes=["ik_tok"])
        if use_sel:
            S.op("act", lambda e: e.activation(out=wst[:, 0:8], in_=R[b][:, 0:8], func=AF.Abs), reads=[RN[b]], writes=["wabs"])
            S.op("act", lambda e: e.activation(out=wst[:, 8:16], in_=R[b][:, 0:8], func=AF.Sign), reads=[RN[b]], writes=["wsg"])
            for h in range(8):
                S.op("dve", (lambda h: lambda e: e.tensor_scalar(out=dsg[:, h, :], in0=ident[:], scalar1=wst[:, 8 + h:9 + h],
                                                                 scalar2=None, op0=ALU.mult))(h),
                     reads=["ident", "wsg", "dsg"], writes=["dsg"])
            fence_dve(["dsg"])
            b2 = bank()
            proj(b2, 0, 2560, 512)
            S.op("dve", lambda e: e.tensor_tensor(out=iq_tok[:, :].rearrange("p (h d) -> p h d", h=8),
                                                  in0=R[b2].rearrange("p (h d) -> p h d", h=8),
                                                  in1=wst[:, 0:8].unsqueeze(2).to_broadcast([128, 8, 64]), op=ALU.mult),
                 reads=[RN[b2], "wabs"], writes=["iq_tok"])
        yield
        k_transpose(i, 65)
        S.op("pe", lambda e: e.transpose(out=PT[0:64, 512:640], in_=ik_tok[:, :], identity=ident[:]),
             reads=["ik_tok", "ident"], writes=["T"])
        S.op("act", lambda e: e.copy(out=ikT[:, i * 128:(i + 1) * 128], in_=PT[0:64, 512:640]), reads=["T"], writes=["ikT_%d" % i])
        yield
        ms = None
        if use_sel:
            ms = tc % 2
            n = 128 * (i + 1)
            for h in range(8):
                S.op("pe", (lambda h: lambda e: e.transpose(out=PT[0:64, h * 128:(h + 1) * 128], in_=iq_tok[:, h * 64:(h + 1) * 64],
                                                            identity=ident[:]))(h), reads=["iq_tok", "ident"], writes=["T"])
            S.op("act", lambda e: e.copy(out=iqT[:, :, :].rearrange("p a b -> p (a b)"), in_=PT[0:64, :]), reads=["T"], writes=["iqT"])
            yield
            nchunk = (n + 511) // 512
            for c in range(nchunk):
                ncol = min(512, n - 512 * c)
                ikbufs = ["ikT_%d" % jj for jj in range(4 * c, min(4 * c + 4, i + 1))]
                for h in range(8):
                    by = bank()
                    rs = rb_ctr[0] % NRB
                    rb_ctr[0] += 1
                    S.op("pe", (lambda h, by, c, ncol: lambda e: e.matmul(
                        R[by][:, 0:ncol], lhsT=iqT[:, h, :], rhs=ikT[:, c * 512:c * 512 + ncol],
                        start=True, stop=True))(h, by, c, ncol), reads=["iqT"] + ikbufs, writes=[RN[by]])
                    if h % 2 == 0:
                        S.op("act", (lambda by, rs, ncol: lambda e: e.activation(out=Rb[rs][:, 0:ncol], in_=R[by][:, 0:ncol], func=AF.Relu))(by, rs, ncol),
                             reads=[RN[by]], writes=["Rb%d" % rs])
                    else:
                        S.op("dve", (lambda by, rs, ncol: lambda e: e.tensor_scalar(out=Rb[rs][:, 0:ncol], in0=R[by][:, 0:ncol],
                                                                                    scalar1=0.0, scalar2=None, op0=ALU.max))(by, rs, ncol),
                             reads=[RN[by]], writes=["Rb%d" % rs])
                    S.op("pe", (lambda h, rs, ncol: lambda e: e.matmul(
                        PTF[:, 0:ncol], lhsT=dsg[:, h, :], rhs=Rb[rs][:, 0:ncol],
                        start=(h == 0), stop=(h == 7)))(h, rs, ncol), reads=["dsg", "Rb%d" % rs], writes=["T"])
                    if h % 2 == 1:
                        yield
                S.op("act", (lambda c, ncol: lambda e: e.copy(out=isc[:, c * 512:c * 512 + ncol], in_=PTF[:, 0:ncol]))(c, ncol),
                     reads=["T", "isc"], writes=["isc"])
        pending = [lambda: q_group(qs, 0, 0, 0), lambda: q_group(qs, 1, 0, 512),
                   lambda: gate_group(qs, 0, 0, 1536), lambda: gate_group(qs, 1, 0, 2048),
                   lambda: q_transpose(qs, 65)]
        if not use_sel:
            for f in pending:
                f()
                yield
        if use_sel:
            S.op("dve", lambda e: e.tensor_tensor(out=isc[:, i * 128:(i + 1) * 128], in0=isc[:, i * 128:(i + 1) * 128],
                                                  in1=negc[:], op=ALU.add), reads=["isc", "negc"], writes=["isc"])
            HI, LO, W0, MID, CNT, TMP, THR = 0, 1, 2, 3, 4, 5, 6
            HB = 8
            S.op("dve", lambda e: e.tensor_reduce(out=bis[:, HI:HI + 1], in_=isc[:, 0:n], axis=AX.X, op=ALU.max),
                 reads=["isc"], writes=["b_hi"])
            S.op("dve", lambda e: e.tensor_reduce(out=bis[:, LO:LO + 1], in_=isc[:, 0:128 * i], axis=AX.X, op=ALU.min),
                 reads=["isc"], writes=["b_lo"])
            S.op("dve", lambda e: e.tensor_tensor(out=bis[:, W0:W0 + 1], in0=bis[:, HI:HI + 1], in1=bis[:, LO:LO + 1], op=ALU.subtract),
                 reads=["b_hi", "b_lo"], writes=["b_w0"])
            S.op("dve", lambda e: e.tensor_scalar(out=bis[:, HB:HB + NBIS], in0=halfpow[:], scalar1=bis[:, W0:W0 + 1], scalar2=None,
                                                  op0=ALU.mult), reads=["halfpow", "b_w0"], writes=["b_h"])
            S.op("dve", lambda e: e.tensor_scalar(out=bis[:, HB + NBIS:HB + 2 * NBIS], in0=bis[:, HB:HB + NBIS], scalar1=2.0, scalar2=None,
                                                  op0=ALU.mult), reads=["b_h"], writes=["b_h2"])
            S.op("dve", lambda e: e.tensor_tensor(out=bis[:, MID:MID + 1], in0=bis[:, LO:LO + 1], in1=bis[:, HB:HB + 1], op=ALU.add),
                 reads=["b_lo", "b_h"], writes=["b_mid"])
            yield
            for k in range(NBIS):
                S.op("dve", lambda e: e.tensor_scalar(out=mask_tok[:, 0:n], in0=isc[:, 0:n], scalar1=bis[:, MID:MID + 1], scalar2=0.0,
                                                      op0=ALU.is_ge, op1=ALU.add, accum_out=bis[:, CNT:CNT + 1]),
                     reads=["isc", "b_mid"], writes=["b_cnt", "mask_tok"])
                if k < NBIS - 1:
                    S.op("dve", (lambda k: lambda e: e.scalar_tensor_tensor(
                        out=bis[:, TMP:TMP + 1], in0=bis[:, CNT:CNT + 1], scalar=TOPK - 0.5,
                        in1=bis[:, HB + NBIS + k + 1:HB + NBIS + k + 2], op0=ALU.is_ge, op1=ALU.mult))(k),
                        reads=["b_cnt", "b_h2"], writes=["b_tmp"])
                    S.op("dve", (lambda k: lambda e: e.scalar_tensor_tensor(
                        out=bis[:, MID:MID + 1], in0=bis[:, TMP:TMP + 1], scalar=bis[:, HB + k + 1:HB + k + 2],
                        in1=bis[:, MID:MID + 1], op0=ALU.subtract, op1=ALU.add))(k),
                        reads=["b_tmp", "b_h", "b_mid"], writes=["b_mid"])
                else:
                    S.op("dve", (lambda k: lambda e: e.scalar_tensor_tensor(
                        out=bis[:, TMP:TMP + 1], in0=bis[:, CNT:CNT + 1], scalar=TOPK - 0.5,
                        in1=bis[:, HB + k:HB + k + 1], op0=ALU.is_ge, op1=ALU.mult))(k),
                        reads=["b_cnt", "b_h"], writes=["b_tmp"])
                    S.op("dve", (lambda k: lambda e: e.scalar_tensor_tensor(
                        out=bis[:, THR:THR + 1], in0=bis[:, TMP:TMP + 1], scalar=bis[:, HB + k:HB + k + 1],
                        in1=bis[:, MID:MID + 1], op0=ALU.subtract, op1=ALU.add))(k),
                        reads=["b_tmp", "b_h", "b_mid"], writes=["b_thr"])
                if k % 3 == 1 and pending:
                    pending.pop(0)()
                yield
            while pending:
                pending.pop(0)()
                yield
            S.op("dve", lambda e: e.tensor_scalar(out=mask_tok[:, 0:n], in0=isc[:, 0:n], scalar1=bis[:, THR:THR + 1], scalar2=-BIG,
                                                  op0=ALU.is_lt, op1=ALU.mult), reads=["isc", "b_thr"], writes=["mask_tok"])
            for j0 in range(0, i + 1, 8):
                j1 = min(i + 1, j0 + 8)
                for j in range(j0, j1):
                    S.op("pe", (lambda j, j0: lambda e: e.transpose(out=PT[:, (j - j0) * 128:(j - j0 + 1) * 128],
                                                                    in_=mask_tok[:, j * 128:(j + 1) * 128], identity=ident[:]))(j, j0),
                         reads=["mask_tok", "ident"], writes=["T"])
                S.op("act", (lambda j0, j1: lambda e: e.copy(out=maskT[ms][:, j0:j1, :].rearrange("p a b -> p (a b)"),
                                                             in_=PT[:, 0:(j1 - j0) * 128]))(j0, j1),
                     reads=["T", "maskT%d" % ms], writes=["maskT%d" % ms])
                yield

    def stage2_A(sq, i, tc):
        xs = tc % 2
        qs = tc % 2
        ms = (tc % 2) if i >= 2 else None
        yield from attention(i, qs, 65, ms)
        yield from finish_tile(sq, i, xs, qs, h1_d, "h1o")

    def stage1_B(sq, i, tc):
        xs = tc % 2
        qs = tc % 2
        own = i // 2
        S.dma("sp", "xt%d" % xs, lambda e: e.dma_start(out=xt[xs][:], in_=h1_d[sq, i * 128:(i + 1) * 128, :]),
              reads=["h1o_dram0", "h1o_dram1"], writes=["xt%d" % xs])
        rms_and_transpose(xs, [(g1[:], "g1"), (g2, "isc")], 2)
        yield
        S.op("pool", lambda e: e.memset(k_aug[:, :, 65:W], 0.0), reads=["k_aug"], writes=["k_aug"])
        S.op("pool", lambda e: e.memset(k_aug[:, :, 65 + own:66 + own], 1.0), reads=["k_aug"], writes=["k_aug"])
        kv_group(i, 0, 0)
        yield
        k_transpose(i, W)
        if i % 2 == 1:
            S.op("dve", lambda e: e.tensor_reduce(out=kmTf[:, :, own], in_=kTc[0:64, :, own * 256:(own + 1) * 256], axis=AX.X, op=ALU.add),
                 reads=["kTc_%d" % (i - 1), "kTc_%d" % i, "kmTf"], writes=["kmTf"])
            S.op("dve", lambda e: e.tensor_copy(out=kmT[:], in_=kmTf[:]), reads=["kmTf"], writes=["kmT"])
        yield
        q_group(qs, 0, 1, 512)
        yield
        q_group(qs, 1, 1, 1024)
        yield
        if own >= 4:
            S.op("dve", lambda e: e.tensor_reduce(out=qsumf[:, :, :], in_=q_aug[qs][:, :, 0:DH].rearrange("p (g j) d -> p g d j", g=G),
                                                  axis=AX.X, op=ALU.add), reads=["q_aug%d" % qs], writes=["qsumf"])
            S.op("dve", lambda e: e.tensor_copy(out=qsum[:], in_=qsumf[:]), reads=["qsumf"], writes=["qsum"])
            for g in range(G):
                S.op("pe", (lambda g: lambda e: e.transpose(out=PT[0:64, g * 128:(g + 1) * 128], in_=qsum[:, g, :], identity=ident[:]))(g),
                     reads=["qsum", "ident"], writes=["T"])
            S.op("act", lambda e: e.copy(out=qsT[:, :, :].rearrange("p a b -> p (a b)"), in_=PT[0:64, 0:512]), reads=["T"], writes=["qsT"])
            b = bank()
            for g in range(G):
                S.op("pe", (lambda g: lambda e: e.matmul(R[b][:, g * 8:(g + 1) * 8], lhsT=qsT[:, g, :], rhs=kmT[:, g, :],
                                                         start=True, stop=True))(g), reads=["qsT", "kmT"], writes=[RN[b]])
            S.op("dve", lambda e: e.tensor_copy(out=gsb[:, :, :].rearrange("p g n -> p (g n)"), in_=R[b][:, 0:32]), reads=[RN[b]], writes=["gsb"])
            S.op("dve", lambda e: e.memset(gsb[:, :, own:8], -1e30), reads=["gsb"], writes=["gsb"])
            for g in range(G):
                S.op("dve", (lambda g: lambda e: e.max(out=m8[:, g, :], in_=gsb[:, g, :]))(g), reads=["gsb", "m8"], writes=["m8"])
            for g in range(G):
                S.op("dve", (lambda g: lambda e: e.tensor_scalar(out=sel[:, g, :], in0=gsb[:, g, :], scalar1=m8[:, g, 2:3], scalar2=BIG,
                                                                 op0=ALU.is_ge, op1=ALU.mult))(g), reads=["gsb", "m8", "sel"], writes=["sel"])
            for g in range(G):
                S.op("dve", (lambda g: lambda e: e.tensor_scalar(out=q_aug[qs][:, 4 * g:4 * g + 4, 65:W],
                                                                 in0=sel[:, g, :].unsqueeze(1).to_broadcast([128, 4, 8]),
                                                                 scalar1=-BIG, scalar2=None, op0=ALU.add))(g),
                     reads=["sel", "q_aug%d" % qs], writes=["q_aug%d" % qs])
            S.op("dve", lambda e: e.memset(q_aug[qs][:, :, 65 + own:66 + own], 0.0), reads=["q_aug%d" % qs], writes=["q_aug%d" % qs])
            fence_dve(["q_aug%d" % qs])
            if dbg_tile is not None and i == dbg_tile and sq == 0:
                dbg_dump("gsb", gsb[:], [128, G, 8], F32, ["gsb"])
                dbg_dump("qaug", q_aug[qs][:], [128, H, W], BF16, ["q_aug%d" % qs])
        else:
            S.op("dve", lambda e: e.memset(q_aug[qs][:, :, 65:W], 0.0), reads=["q_aug%d" % qs], writes=["q_aug%d" % qs])
            fence_dve(["q_aug%d" % qs])
        yield
        q_transpose(qs, W)
        yield
        gate_group(qs, 0, 1, 1536)
        yield
        gate_group(qs, 1, 1, 2048)
        yield

    def stage2_B(sq, i, tc):
        xs = tc % 2
        qs = tc % 2
        yield from attention(i, qs, W, None)
        yield from finish_tile(sq, i, xs, qs, out_d, "outo")

    class _Null:
        pass

    def count_steps(genfn, args):
        saved = (bank_ctr[0], pt_ctr[0], rb_ctr[0], set(held), dbg_n[0])
        real_op, real_dma = S.op, S.dma
        S.op = lambda *a, **k: None
        S.dma = lambda *a, **k: None
        n = 0
        for _ in genfn(*args):
            n += 1
        S.op, S.dma = real_op, real_dma
        bank_ctr[0], pt_ctr[0], rb_ctr[0] = saved[0], saved[1], saved[2]
        held.clear()
        held.update(saved[3])
        dbg_n[0] = saved[4]
        return n + 1

    def drain(gen):
        for _ in gen:
            pass

    def run_phase(tiles, s1, s2):
        prev = None
        for t in list(tiles) + [None]:
            if not pipeline:
                if t is not None:
                    drain(s1(*t))
                    drain(s2(*t))
                continue
            if t is not None and prev is not None:
                n1 = count_steps(s1, t)
                n2 = count_steps(s2, prev)
                ga, gb = s1(*t), s2(*prev)
                d1 = d2 = 0
                a_alive = b_alive = True
                while a_alive or b_alive:
                    if b_alive and (not a_alive or d2 * n1 <= d1 * n2):
                        try:
                            next(gb)
                            d2 += 1
                        except StopIteration:
                            b_alive = False
                    else:
                        try:
                            next(ga)
                            d1 += 1
                        except StopIteration:
                            a_alive = False
            elif t is not None:
                drain(s1(*t))
            elif prev is not None:
                drain(s2(*prev))
            prev = t

    tile_ctr = [0]

    def tiles_of_phase():
        out = []
        for sq in range(nseq):
            for i in range(ntiles):
                out.append((sq, i, tile_ctr[0]))
                tile_ctr[0] += 1
        return out

    if doA:
        load_gain(g1, norm_a_g, 32.0, "g1")
        load_gk(kn_a_g, qn_a_g)
        load_w(wbig, w_in_a, A_COLS, 0, "wbig")
        load_w(wo, w_out_a, D, 0, "wo")
        run_phase(tiles_of_phase(), stage1_A, stage2_A)

    if doB:
        load_gain(g1, norm_kv_g, 32.0, "g1")
        S.dma("sp", "g2", lambda e: e.dma_start(out=g2, in_=norm_b_g.partition_broadcast(128)), writes=["isc"])
        S.op("dve", lambda e: e.tensor_scalar(out=g2, in0=g2, scalar1=32.0, scalar2=None, op0=ALU.mult), reads=["isc"], writes=["isc"])
        load_gk(kn_b_g, qn_b_g)
        load_w(wbig, w_kv, 512, 0, "wbig")
        load_w(wbig, w_in_b, 2048, 512, "wbig")
        load_w(wo, w_out_b, D, 0, "wo")
        S.op("pool", lambda e: e.memset(kmT[:], 0.0), writes=["kmT"])
        S.op("pool", lambda e: e.memset(kmTf[:], 0.0), writes=["kmTf"])
        run_phase(tiles_of_phase(), stage1_B, stage2_B)

    S.finalize()
    S.emit()
    es.close()
    return nc


_PROG_CACHE = {}


def _get_prog(mode):
    if mode not in _PROG_CACHE:
        _PROG_CACHE[mode] = build_program(mode)
    return _PROG_CACHE[mode]


FUSED = True


def kernel(x, norm_a_g, w_in_a, qn_a_g, kn_a_g, w_out_a, rel_bias, norm_kv_g, w_kv,
           kn_b_g, norm_b_g, w_in_b, qn_b_g, w_out_b):
    f = lambda a: np.ascontiguousarray(np.asarray(a, dtype=np.float32))
    x = f(x)
    rel_bias = f(rel_bias)
    nbias = _near_bias_layout(rel_bias).reshape(2, 128, H * 128)
    a_in = {"w_in_a": f(w_in_a)[0], "w_out_a": f(w_out_a)[0], "norm_a_g": f(norm_a_g)[0],
            "qn_a_g": f(qn_a_g)[0], "kn_a_g": f(kn_a_g)[0]}
    b_in = {"w_kv": f(w_kv), "w_in_b": f(w_in_b)[0], "w_out_b": f(w_out_b)[0], "norm_kv_g": f(norm_kv_g),
            "norm_b_g": f(norm_b_g)[0], "kn_b_g": f(kn_b_g), "qn_b_g": f(qn_b_g)[0]}
    common = {"rel_bias": rel_bias, "nbias": nbias}
    xs = [np.ascontiguousarray(x[NSEQ * c:NSEQ * (c + 1)]) for c in range(NCORES)]
    cores = list(range(NCORES))
    if FUSED:
        nc = _get_prog("AB")
        in_maps = [dict(x=xs[c], **a_in, **b_in, **common) for c in cores]
        res = run_bass_kernel_spmd(nc, in_maps, core_ids=cores)
        outs = [r["out"] for r in res.results]
    else:
        ncA = _get_prog("A")
        resA = run_bass_kernel_spmd(ncA, [dict(x=xs[c], **a_in, **common) for c in cores], core_ids=cores)
        h1 = [r["h1"] for r in resA.results]
        ncB = _get_prog("B")
        resB = run_bass_kernel_spmd(ncB, [dict(h1=h1[c], **b_in, **common) for c in cores], core_ids=cores)
        outs = [r["out"] for r in resB.results]
    return np.concatenate(outs, axis=0).astype(np.float32)
```

```python
import numpy as np
import concourse.bass as bass
import concourse.mybir as mybir
from concourse.bass_utils import run_bass_kernel_spmd
from contextlib import ExitStack

F32 = mybir.dt.float32
BF16 = mybir.dt.bfloat16
AF = mybir.ActivationFunctionType
ALU = mybir.AluOpType
AX = mybir.AxisListType

T = 2048
D = 1024
NT = 16
NSEQ = 2
NCORES = 8
H = 16
G = 4
DH = 64
EPS = 1e-6
BIG = 30000.0
A_COLS = 3144
NBIS = 12
TOPK = 256

ENGS = ("pe", "act", "dve", "pool", "sp")


class Buf:
    __slots__ = ("name", "last_w", "readers")

    def __init__(self, name):
        self.name = name
        self.last_w = None
        self.readers = []


class Op:
    __slots__ = ("eng", "idx", "fn", "deps", "inc", "incval", "stream", "is_dma", "waits")

    def __init__(self, eng, idx, fn, is_dma=False, stream=None):
        self.eng = eng
        self.idx = idx
        self.fn = fn
        self.deps = []
        self.inc = False
        self.incval = 0
        self.stream = stream
        self.is_dma = is_dma
        self.waits = []


class Sched:
    def __init__(self, nc, es):
        self.nc = nc
        self.es = es
        self.ops = {e: [] for e in ENGS}
        self.streams = {}
        self.sems = {}
        self.bufs = {}

    def _B(self, x):
        if isinstance(x, Buf):
            return x
        b = self.bufs.get(x)
        if b is None:
            b = self.bufs[x] = Buf(x)
        return b

    def _record(self, op, reads, writes):
        deps = []
        for r in reads:
            r = self._B(r)
            if r.last_w is not None:
                deps.append(r.last_w)
            r.readers.append(op)
        for w in writes:
            w = self._B(w)
            if w.last_w is not None:
                deps.append(w.last_w)
            last = {}
            for x in w.readers:
                if x is op:
                    continue
                key = ("dma", x.stream, x.idx) if x.is_dma else x.eng
                if key not in last or x.idx > last[key].idx:
                    last[key] = x
            deps.extend(last.values())
            w.last_w = op
            w.readers = []
        op.deps = deps

    def op(self, eng, fn, reads=(), writes=()):
        o = Op(eng, len(self.ops[eng]), fn)
        self.ops[eng].append(o)
        self._record(o, reads, writes)
        return o

    def dma(self, eng, stream, fn, reads=(), writes=()):
        o = Op(eng, len(self.ops[eng]), fn, is_dma=True, stream=stream)
        self.ops[eng].append(o)
        self.streams.setdefault(stream, []).append(o)
        self._record(o, reads, writes)
        return o

    def finalize(self, final_wait_eng="sp"):
        fin = Op(final_wait_eng, len(self.ops[final_wait_eng]), None)
        fin.deps = [lst[-1] for lst in self.streams.values()]
        self.ops[final_wait_eng].append(fin)
        for name, lst in self.streams.items():
            for i, o in enumerate(lst):
                o.incval = 16 * (i + 1)
        sel = {}
        for e in ENGS:
            waited = {}
            for o in self.ops[e]:
                need = {}
                for d in o.deps:
                    if d.is_dma:
                        key = "dma:" + d.stream
                        pos = d.incval
                    else:
                        if d.eng == o.eng and d.eng == "pe":
                            continue
                        key = d.eng
                        pos = d.idx
                    if key not in need or pos > need[key][0]:
                        need[key] = (pos, d)
                lst = []
                for k, (pos, d) in need.items():
                    if pos > waited.get(k, -1):
                        waited[k] = pos
                        lst.append((k, d))
                        if not d.is_dma:
                            d.inc = True
                sel[id(o)] = lst
        for e in ENGS:
            c = 0
            for o in self.ops[e]:
                if o.is_dma:
                    continue
                if o.inc:
                    c += 1
                    o.incval = c
        for e in ENGS:
            for o in self.ops[e]:
                o.waits = [(k, d.incval) for (k, d) in sel[id(o)]]

    def emit(self):
        nc = self.nc
        keys = list(ENGS) + ["dma:" + s for s in self.streams]
        for k in keys:
            self.sems[k] = self.es.enter_context(nc.semaphore("s_" + k.replace(":", "_")))
        sems = self.sems
        ops = self.ops

        def run(engname, eng):
            for o in ops[engname]:
                if o.fn is None:
                    for (k, v) in o.waits:
                        eng.wait_ge(sems[k], v)
                    continue
                for (k, v) in o.waits[1:]:
                    eng.wait_ge(sems[k], v)
                ins = o.fn(eng)
                if o.waits:
                    ins._wait_ge(sems[o.waits[0][0]], o.waits[0][1])
                if o.is_dma:
                    ins.then_inc(sems["dma:" + o.stream], 16)
                elif o.inc:
                    ins.then_inc(sems[engname], 1)

        with nc.Block() as block:
            @block.tensor
            def _(e):
                run("pe", e)

            @block.scalar
            def _(e):
                run("act", e)

            @block.vector
            def _(e):
                run("dve", e)

            @block.gpsimd
            def _(e):
                run("pool", e)

            @block.sync
            def _(e):
                run("sp", e)


def _rel_bucket_np(dist):
    n = np.maximum(dist, 0)
    max_exact = 16
    nf = np.maximum(n, 1).astype(np.float32)
    large = max_exact + (np.log(nf / np.float32(max_exact)) / np.float32(np.log(128 / max_exact))
                         * np.float32(32 - max_exact)).astype(np.int32)
    large = np.minimum(large, 31)
    return np.where(n < max_exact, n, large)


def _near_bias_layout(rel_bias):
    s = np.arange(128)[:, None]
    t = np.arange(128)[None, :]
    out = np.empty((2, 128, H, 128), np.float32)
    for d in range(2):
        bk = _rel_bucket_np(128 * d + t - s)
        out[d] = np.transpose(rel_bias[bk], (0, 2, 1))
    return np.ascontiguousarray(out)


def build_program(mode="AB", nseq=NSEQ, ntiles=NT, dbg_tile=None, pipeline=True):
    nc = bass.Bass("TRN2", target_bir_lowering=False)
    doA = "A" in mode
    doB = "B" in mode

    def din(name, shape):
        return nc.dram_tensor(name, shape, F32, kind="ExternalInput").ap()

    if doA:
        x_d = din("x", [nseq, T, D])
        w_in_a = din("w_in_a", [D, A_COLS])
        w_out_a = din("w_out_a", [D, D])
        norm_a_g = din("norm_a_g", [D])
        qn_a_g = din("qn_a_g", [DH])
        kn_a_g = din("kn_a_g", [DH])
    if doB:
        w_kv = din("w_kv", [D, 512])
        w_in_b = din("w_in_b", [D, 2048])
        w_out_b = din("w_out_b", [D, D])
        norm_kv_g = din("norm_kv_g", [D])
        norm_b_g = din("norm_b_g", [D])
        kn_b_g = din("kn_b_g", [DH])
        qn_b_g = din("qn_b_g", [DH])
        out_d = nc.dram_tensor("out", [nseq, T, D], F32, kind="ExternalOutput").ap()
    rel_bias = din("rel_bias", [32, H])
    nbias_d = din("nbias", [2, 128, H * 128])
    if mode == "A":
        h1_d = nc.dram_tensor("h1", [nseq, T, D], F32, kind="ExternalOutput").ap()
    elif mode == "B":
        h1_d = din("h1", [nseq, T, D])
    else:
        h1_d = nc.dram_tensor("h1", [nseq, T, D], F32, kind="Internal").ap()

    es = ExitStack()
    S = Sched(nc, es)

    def sb(name, shape, dt):
        return es.enter_context(nc.sbuf_tensor(name, shape, dt))

    def ps(name, shape, dt):
        return es.enter_context(nc.psum_tensor(name, shape, dt))

    W = 73

    ident = sb("ident", [128, 128], BF16)
    NB = sb("NB", [128, 2, H, 128], BF16)
    g1 = sb("g1", [128, D], F32)
    gk = sb("gk", [128, DH], F32)
    gtmp = sb("gtmp", [128, DH], F32)
    cb = sb("cb", [128, H], F32)
    epsb = sb("epsb", [128, 2], F32)
    negc = sb("negc", [128, 128], F32)
    halfpow = sb("halfpow", [128, NBIS], F32)
    vc = sb("vc", [128, NT, G, 65], BF16)
    kTc = sb("kTc", [W, G, T], BF16)
    ikT = sb("ikT", [64, T], BF16)
    wbig = sb("wbig", [128, 8, A_COLS], BF16)
    wo = sb("wo", [128, 8, D], BF16)
    xt = [sb("xt%d" % k, [128, D], F32) for k in range(2)]
    junk_a = sb("junk_a", [128, D], BF16)
    sqb = sb("sqb", [128, D], F32)
    hn0 = sb("hn0", [128, D], BF16)
    hT0 = sb("hT0", [128, 8, 128], BF16)
    q_aug = [sb("q_aug%d" % k, [128, H, W], BF16) for k in range(2)]
    k_aug = sb("k_aug", [128, G, W], BF16)
    ktmp = sb("ktmp", [128, G * DH], F32)
    qT = [sb("qT%d" % k, [W, H, 128], BF16) for k in range(2)]
    thb = sb("thb", [128, D], F32)
    sg = [sb("sg%d" % k, [128, D], BF16) for k in range(2)]
    sgp = sb("sgp", [128, D], BF16)
    sgT = sb("sgT", [128, 8, 128], BF16)
    st = sb("st", [128, 96], F32)
    ik_tok = sb("ik_tok", [128, 64], BF16)
    iq_tok = sb("iq_tok", [128, 512], BF16)
    iqT = sb("iqT", [64, 8, 128], BF16)
    wst = sb("wst", [128, 16], F32)
    dsg = sb("dsg", [128, 8, 128], BF16)
    NRB = 4
    Rb = [sb("Rb%d" % k, [128, 512], BF16) for k in range(NRB)]
    isc = sb("isc", [128, T], F32)
    bis = sb("bis", [128, 8 + 2 * NBIS], F32)
    g2 = isc[:, 0:1024]
    hn1 = isc[:, 1024:1536].bitcast(BF16)
    hT1 = isc[:, 1536:2048].bitcast(BF16)
    HN = [(hn0[:, :], "hn0"), (hn1, "isc")]
    HT = [(hT0[:, :, :].rearrange("p a b -> p (a b)"), "hT0"), (hT1, "isc")]
    mask_tok = sb("mask_tok", [128, T], BF16)
    maskT = [sb("maskT%d" % k, [128, NT, 128], BF16) for k in range(2)]
    NPT = 5
    pT = [sb("pT%d" % k, [128, 512], BF16) for k in range(NPT)]
    drow = [sb("drow%d" % k, [1, 512], BF16) for k in range(2)]
    ones1 = sb("ones1", [1, 8], BF16)
    rdt = sb("rdt", [128, H], F32)
    ogT = sb("ogT", [128, 8, 128], BF16)
    if doB:
        kmT = sb("kmT", [64, G, 8], BF16)
        kmTf = sb("kmTf", [64, G, 8], F32)
        qsum = sb("qsum", [128, G, DH], BF16)
        qsumf = sb("qsumf", [128, G, DH], F32)
        qsT = sb("qsT", [64, G, 128], BF16)
        gsb = sb("gsb", [128, G, 8], F32)
        m8 = sb("m8", [128, G, 8], F32)
        sel = sb("sel", [128, G, 8], F32)

    RBK = ps("PS_R", [128, 1536], F32)
    PO = ps("PS_O", [128, 2048], F32)
    PT = ps("PS_T", [128, 1024], BF16)
    PTF = PT[:, :].bitcast(F32)
    NRK = 3
    R = [RBK[:, k * 512:(k + 1) * 512] for k in range(NRK)]
    RN = ["R%d" % k for k in range(NRK)]
    OB = ["O0", "O1", "O2", "O3"]
    bank_ctr = [0]
    held = set()

    def bank(hold=False):
        while True:
            k = bank_ctr[0] % NRK
            bank_ctr[0] += 1
            if k not in held:
                break
        if hold:
            held.add(k)
        return k

    S.op("pool", lambda e: e.memset(isc[:, 0:128], 0.0), writes=["isc"])
    S.op("pool", lambda e: e.affine_select(out=isc[:, 0:128], in_=isc[:, 0:128], pattern=[[-1, 128]],
                                           compare_op=ALU.not_equal, fill=1.0, base=0, channel_multiplier=1),
         reads=["isc"], writes=["isc"])
    S.op("dve", lambda e: e.tensor_copy(out=ident[:], in_=isc[:, 0:128]), reads=["isc"], writes=["ident"])
    S.op("pool", lambda e: e.memset(junk_a[:, :], 0.0))
    S.op("pool", lambda e: e.memset(ones1[:, :], 1.0), writes=["ones1"])
    S.op("pool", lambda e: e.memset(epsb[:, 0:1], float(D * EPS)), writes=["epsb"])
    S.op("pool", lambda e: e.memset(epsb[:, 1:2], float(DH * EPS)), reads=["epsb"], writes=["epsb"])
    S.op("pool", lambda e: e.memset(negc[:], 0.0), writes=["negc"])
    S.op("pool", lambda e: e.affine_select(out=negc[:], in_=negc[:], pattern=[[-1, 128]],
                                           compare_op=ALU.is_ge, fill=-BIG, base=0, channel_multiplier=1),
         reads=["negc"], writes=["negc"])
    for k in range(NBIS):
        S.op("pool", (lambda k: lambda e: e.memset(halfpow[:, k:k + 1], 2.0 ** -(k + 1)))(k), writes=["halfpow"])
    S.op("pool", lambda e: e.memset(vc[:, :, :, 64:65], 1.0), writes=["vc_init"])
    S.op("pool", lambda e: e.memset(k_aug[:, :, 64:W], 0.0), writes=["k_aug"])
    S.op("pool", lambda e: e.memset(k_aug[:, :, 64:65], 1.0), reads=["k_aug"], writes=["k_aug"])
    S.dma("sp", "cb", lambda e: e.dma_start(out=cb[:], in_=rel_bias[31, :].partition_broadcast(128)), writes=["cb"])
    for k in range(2):
        S.op("pool", (lambda k: lambda e: e.memset(q_aug[k][:, :, 64:W], 0.0))(k), writes=["q_aug%d" % k])
        S.op("dve", (lambda k: lambda e: e.tensor_copy(out=q_aug[k][:, :, 64:65], in_=cb[:, :].unsqueeze(2)))(k),
             reads=["cb", "q_aug%d" % k], writes=["q_aug%d" % k])
    for d in range(2):
        S.dma("sp", "isc", (lambda d: lambda e: e.dma_start(out=isc[:, :], in_=nbias_d[d]))(d), writes=["isc"])
        S.op("dve", lambda e: e.tensor_tensor(out=isc[:, :].rearrange("p (h t) -> p h t", h=H),
                                              in0=isc[:, :].rearrange("p (h t) -> p h t", h=H),
                                              in1=cb[:, :].unsqueeze(2).to_broadcast([128, H, 128]),
                                              op=ALU.subtract), reads=["isc", "cb"], writes=["isc"])
        if d == 0:
            S.op("pool", lambda e: e.affine_select(out=isc[:, :].rearrange("p (h t) -> p h t", h=H),
                                                   in_=isc[:, :].rearrange("p (h t) -> p h t", h=H),
                                                   pattern=[[0, H], [1, 128]], compare_op=ALU.is_ge, fill=-BIG,
                                                   base=0, channel_multiplier=-1), reads=["isc"], writes=["isc"])
        S.op("act", (lambda d: lambda e: e.copy(out=NB[:, d, :, :], in_=isc[:, :].rearrange("p (h t) -> p h t", h=H)))(d),
             reads=["isc"], writes=["NB"])

    def load_gain(dst, src_ap, scale, name):
        S.dma("sp", name, lambda e: e.dma_start(out=dst[:], in_=src_ap.partition_broadcast(128)), writes=[name])
        S.op("dve", lambda e: e.tensor_scalar(out=dst[:], in0=dst[:], scalar1=float(scale), scalar2=None, op0=ALU.mult),
             reads=[name], writes=[name])

    def load_gk(kn_ap, qn_ap):
        S.dma("sp", "gk", lambda e: e.dma_start(out=gk[:], in_=kn_ap.partition_broadcast(128)), writes=["gk"])
        S.dma("sp", "gtmp", lambda e: e.dma_start(out=gtmp[:], in_=qn_ap.partition_broadcast(128)), writes=["gtmp"])
        S.op("dve", lambda e: e.scalar_tensor_tensor(out=gk[:], in0=gk[:], scalar=8.0, in1=gtmp[:],
                                                     op0=ALU.mult, op1=ALU.mult), reads=["gk", "gtmp"], writes=["gk"])

    def load_w(dst, src, ncols, col0, bufname):
        srcv = src.rearrange("(kc p) n -> p kc n", p=128)
        for kc in range(8):
            S.dma("pool", bufname + str(kc),
                  (lambda kc: lambda e: e.dma_start(out=dst[:, kc, col0:col0 + ncols], in_=srcv[:, kc, :]))(kc),
                  writes=[bufname + str(kc)])

    pt_ctr = [0]
    rb_ctr = [0]
    dbg_n = [0]

    def fence_dve(bufs):
        S.op("dve", lambda e: e.tensor_copy(out=junk_a[:, 0:512], in_=junk_a[:, 512:1024]), reads=bufs, writes=bufs)

    def dbg_dump(name, ap, shape, dt, bufs):
        dn = nc.dram_tensor("dbg_" + name, list(shape), dt, kind="ExternalOutput").ap()
        dbg_n[0] += 1
        S.dma("sp", "dbg%d" % dbg_n[0], lambda e: e.dma_start(out=dn, in_=ap), reads=bufs)

    def rsqrt_act(out_ap, in_ap, c, rbuf, wbuf, toff):
        n = in_ap.shape[1]
        tmp = st[:, toff:toff + n]
        tn = "st_tmp%d" % toff
        S.op("act", lambda e: e.activation(out=tmp, in_=in_ap, func=AF.Ln, bias=epsb[:, {1024: 0, 64: 1}[int(round(c / EPS))]:{1024: 1, 64: 2}[int(round(c / EPS))]]),
             reads=[rbuf, "epsb"], writes=[tn])
        S.op("act", lambda e: e.activation(out=out_ap, in_=tmp, func=AF.Exp, scale=-0.5), reads=[tn], writes=[wbuf])

    def rms_and_transpose(xs, gains, nh):
        S.op("act", lambda e: e.activation(out=junk_a[:], in_=xt[xs][:], func=AF.Square, accum_out=st[:, 0:1]),
             reads=["xt%d" % xs], writes=["st_ss"])
        rsqrt_act(st[:, 1:2], st[:, 0:1], float(D * EPS), "st_ss", "st_r", 2)
        for k in range(nh):
            hn_ap, hn_nm = HN[k]
            ht_ap, ht_nm = HT[k]
            g_ap, g_nm = gains[k]
            S.op("dve", (lambda hn_ap, g_ap: lambda e: e.scalar_tensor_tensor(out=hn_ap, in0=xt[xs][:], scalar=st[:, 1:2],
                                                                             in1=g_ap, op0=ALU.mult, op1=ALU.mult))(hn_ap, g_ap),
                 reads=["xt%d" % xs, "st_r", g_nm], writes=[hn_nm])
            for kc in range(8):
                S.op("pe", (lambda hn_ap, kc: lambda e: e.transpose(out=PT[:, kc * 128:(kc + 1) * 128],
                                                                   in_=hn_ap[:, kc * 128:(kc + 1) * 128], identity=ident[:]))(hn_ap, kc),
                     reads=[hn_nm, "ident"], writes=["T"])
            S.op("act", (lambda ht_ap: lambda e: e.copy(out=ht_ap, in_=PT[:, :]))(ht_ap), reads=["T"], writes=[ht_nm])

    def proj(b, hk, col0, ncols):
        ht_ap, ht_nm = HT[hk]
        for kc in range(8):
            S.op("pe", (lambda kc: lambda e: e.matmul(R[b][:, 0:ncols], lhsT=ht_ap[:, kc * 128:(kc + 1) * 128], rhs=wbig[:, kc, col0:col0 + ncols],
                                                      start=(kc == 0), stop=(kc == 7)))(kc),
                 reads=[ht_nm, "wbig%d" % kc], writes=[RN[b]])

    def q_group(qs, grp, hk, col0):
        b = bank()
        proj(b, hk, col0, 512)
        so = 16 + 8 * grp
        S.op("act", lambda e: e.activation(out=sqb[:, grp * 512:(grp + 1) * 512], in_=R[b], func=AF.Square),
             reads=[RN[b]], writes=["sqb%d" % grp])
        S.op("dve", lambda e: e.tensor_reduce(out=st[:, so:so + 8], in_=sqb[:, grp * 512:(grp + 1) * 512].rearrange("p (h d) -> p h d", h=8),
                                              axis=AX.X, op=ALU.add), reads=["sqb%d" % grp], writes=["st_q%d" % grp])
        rsqrt_act(st[:, so:so + 8], st[:, so:so + 8], float(DH * EPS), "st_q%d" % grp, "st_q%d" % grp, 40 + 8 * grp)
        S.op("dve", lambda e: e.tensor_tensor(out=q_aug[qs][:, 8 * grp:8 * grp + 8, 0:DH], in0=R[b].rearrange("p (h d) -> p h d", h=8),
                                              in1=st[:, so:so + 8].unsqueeze(2).to_broadcast([128, 8, DH]), op=ALU.mult),
             reads=[RN[b], "st_q%d" % grp, "q_aug%d" % qs], writes=["q_aug%d" % qs])

    def kv_group(i, hk, col0):
        b = bank()
        proj(b, hk, col0, 512)
        kv_ap = R[b]
        S.op("act", lambda e: e.activation(out=junk_a[:, 0:256], in_=kv_ap[:, 0:256], func=AF.Square), reads=[RN[b]], writes=["sqk"])
        S.op("dve", lambda e: e.tensor_reduce(out=st[:, 32:36], in_=junk_a[:, 0:256].rearrange("p (g d) -> p g d", g=G),
                                              axis=AX.X, op=ALU.add), reads=["sqk"], writes=["st_k"])
        rsqrt_act(st[:, 32:36], st[:, 32:36], float(DH * EPS), "st_k", "st_k", 56)
        S.op("dve", lambda e: e.tensor_tensor(out=ktmp[:, :].rearrange("p (g d) -> p g d", g=G),
                                              in0=kv_ap[:, 0:256].rearrange("p (g d) -> p g d", g=G),
                                              in1=st[:, 32:36].unsqueeze(2).to_broadcast([128, G, DH]), op=ALU.mult),
             reads=[RN[b], "st_k"], writes=["ktmp"])
        S.op("pool", lambda e: e.tensor_tensor(out=k_aug[:, :, 0:DH], in0=ktmp[:, :].rearrange("p (g d) -> p g d", g=G),
                                               in1=gk[:, :].unsqueeze(1).to_broadcast([128, G, DH]), op=ALU.mult),
             reads=["ktmp", "gk", "k_aug"], writes=["k_aug"])
        S.op("act", lambda e: e.copy(out=vc[:, i, :, 0:DH], in_=kv_ap[:, 256:512].rearrange("p (g d) -> p g d", g=G)),
             reads=[RN[b], "vc_init"], writes=["vc_%d" % i])

    def k_transpose(i, Wk):
        for g in range(G):
            S.op("pe", (lambda g: lambda e: e.transpose(out=PT[0:Wk, g * 128:(g + 1) * 128], in_=k_aug[:, g, 0:Wk],
                                                        identity=ident[:]))(g), reads=["k_aug", "ident"], writes=["T"])
        S.op("act", lambda e: e.copy(out=kTc[0:Wk, :, i * 128:(i + 1) * 128],
                                     in_=PT[0:Wk, 0:512].rearrange("p (g t) -> p g t", g=G)), reads=["T"], writes=["kTc_%d" % i])

    def gate_group(gs_, grp, hk, col0):
        b = bank()
        proj(b, hk, col0, 512)
        tb = thb[:, grp * 512:(grp + 1) * 512]
        tn = "thb%d" % grp
        S.op("act", lambda e: e.activation(out=tb, in_=R[b], func=AF.Exp, scale=-1.0), reads=[RN[b]], writes=[tn])
        S.op("act", lambda e: e.activation(out=tb, in_=tb, func=AF.Ln, bias=1.0), reads=[tn], writes=[tn])
        S.op("act", lambda e: e.activation(out=tb, in_=tb, func=AF.Exp, scale=-1.0), reads=[tn], writes=[tn])
        S.op("dve", lambda e: e.tensor_tensor(out=sg[gs_][:, grp * 512:(grp + 1) * 512], in0=R[b], in1=tb, op=ALU.mult),
             reads=[tn, RN[b], "sg%d" % gs_], writes=["sg%d" % gs_])

    def q_transpose(qs, Wq):
        for rnd in range(2):
            b = bank()
            pv = R[b].bitcast(BF16)
            for hh in range(8):
                h = rnd * 8 + hh
                S.op("pe", (lambda h, hh, pv: lambda e: e.transpose(out=pv[0:Wq, hh * 128:(hh + 1) * 128], in_=q_aug[qs][:, h, 0:Wq],
                                                                    identity=ident[:]))(h, hh, pv), reads=["q_aug%d" % qs, "ident"], writes=[RN[b]])
            S.op("act", (lambda rnd, pv: lambda e: e.copy(out=qT[qs][0:Wq, rnd * 8:(rnd + 1) * 8, :].rearrange("p a b -> p (a b)"), in_=pv[0:Wq, :]))(rnd, pv),
                 reads=[RN[b], "qT%d" % qs], writes=["qT%d" % qs])

    def attention(i, qs, Wq, ms):
        steps = [(j, g) for j in range(i + 1) for g in range(G)]
        info = {}

        def qk(n):
            j, g = steps[n]
            near = (i - j) <= 1
            slot = pt_ctr[0] % NPT
            pt_ctr[0] += 1
            b = bank(hold=True)
            info[n] = (b, slot)
            masked = ms is not None
            S.op("pe", (lambda g, j, b: lambda e: e.matmul(
                R[b], lhsT=kTc[0:Wq, g, j * 128:(j + 1) * 128],
                rhs=qT[qs][0:Wq, 4 * g:4 * g + 4, :].rearrange("p a b -> p (a b)"),
                start=True, stop=(not near and not masked)))(g, j, b),
                reads=["kTc_%d" % j, "qT%d" % qs], writes=[RN[b]])
            if near:
                S.op("pe", (lambda g, j, b: lambda e: e.matmul(
                    R[b], lhsT=ident[:], rhs=NB[:, i - j, 4 * g:4 * g + 4, :].rearrange("p a b -> p (a b)"),
                    start=False, stop=(not masked)))(g, j, b), reads=["ident", "NB"], writes=[RN[b]])
            if masked:
                S.op("pe", (lambda j, b: lambda e: e.matmul(
                    R[b].rearrange("p (h t) -> p h t", h=4), lhsT=ident[:],
                    rhs=maskT[ms][:, j, :].unsqueeze(1).to_broadcast([128, 4, 128]),
                    start=False, stop=True))(j, b), reads=["ident", "maskT%d" % ms], writes=[RN[b]])

        qk(0)
        for n in range(len(steps)):
            j, g = steps[n]
            b, slot = info.pop(n)
            S.op("act", (lambda b, slot: lambda e: e.activation(out=pT[slot][:], in_=R[b], func=AF.Exp))(b, slot),
                 reads=[RN[b]], writes=["pT%d" % slot])
            held.discard(b)
            if n + 1 < len(steps):
                qk(n + 1)
            S.op("pe", (lambda g, slot, j: lambda e: e.matmul(
                PO[0:65, g * 512:(g + 1) * 512], lhsT=vc[:, j, g, :], rhs=pT[slot][:, :],
                start=(j == 0), stop=(j == i)))(g, slot, j),
                reads=["pT%d" % slot, "vc_%d" % j], writes=[OB[g]])
            yield

    def finish_tile(sq, i, xs, gs_, dst_d, out_stream):
        bd = bank()
        for g in range(G):
            rs = g % 2
            S.op("act", (lambda g, rs: lambda e: e.copy(out=drow[rs][0:1, :], in_=PO[64:65, g * 512:(g + 1) * 512]))(g, rs),
                 reads=[OB[g]], writes=["drow%d" % rs])
            for jh in range(4):
                h = 4 * g + jh
                S.op("pe", (lambda rs, jh, h, bd: lambda e: e.matmul(R[bd][:, h:h + 1], lhsT=drow[rs][0:1, jh * 128:(jh + 1) * 128],
                                                                     rhs=ones1[0:1, 0:1], start=True, stop=True))(rs, jh, h, bd),
                     reads=["drow%d" % rs, "ones1"], writes=[RN[bd]])
        S.op("dve", lambda e: e.reciprocal(out=rdt[:, :], in_=R[bd][:, 0:H]), reads=[RN[bd]], writes=["rdt"])
        S.op("dve", lambda e: e.tensor_tensor(out=sgp[:, :].rearrange("p (h d) -> p h d", h=H),
                                              in0=sg[gs_][:, :].rearrange("p (h d) -> p h d", h=H),
                                              in1=rdt[:, :].unsqueeze(2).to_broadcast([128, H, DH]), op=ALU.mult),
             reads=["sg%d" % gs_, "rdt"], writes=["sgp"])
        yield
        bt = bank()
        ptv = R[bt].bitcast(BF16)
        for kc in range(8):
            S.op("pe", (lambda kc: lambda e: e.transpose(out=ptv[:, kc * 128:(kc + 1) * 128], in_=sgp[:, kc * 128:(kc + 1) * 128],
                                                         identity=ident[:]))(kc), reads=["sgp", "ident"], writes=[RN[bt]])
        S.op("act", lambda e: e.copy(out=sgT[:, :, :].rearrange("p a b -> p (a b)"), in_=ptv[:, :]), reads=[RN[bt]], writes=["sgT"])
        yield
        for g in range(G):
            for par in range(2):
                S.op("dve", (lambda g, par: lambda e: e.tensor_tensor(
                    out=ogT[par * 64:(par + 1) * 64, 2 * g:2 * g + 2, :],
                    in0=PO[0:64, g * 512:(g + 1) * 512].rearrange("p (a b t) -> p a b t", a=2, b=2)[:, :, par, :],
                    in1=sgT[par * 64:(par + 1) * 64, 2 * g:2 * g + 2, :], op=ALU.mult))(g, par),
                    reads=[OB[g], "sgT", "ogT"], writes=["ogT"])
            yield
        for nb in range(2):
            b = bank()
            for kc in range(8):
                S.op("pe", (lambda nb, kc, b: lambda e: e.matmul(R[b], lhsT=ogT[:, kc, :], rhs=wo[:, kc, nb * 512:(nb + 1) * 512],
                                                                 start=(kc == 0), stop=(kc == 7)))(nb, kc, b),
                     reads=["ogT", "wo%d" % kc], writes=[RN[b]])
            S.op("dve", (lambda nb, b: lambda e: e.tensor_tensor(out=xt[xs][:, nb * 512:(nb + 1) * 512], in0=xt[xs][:, nb * 512:(nb + 1) * 512],
                                                                 in1=R[b], op=ALU.add))(nb, b),
                 reads=["xt%d" % xs, RN[b]], writes=["xt%d" % xs])
            yield
        S.dma("sp", out_stream + str(xs), lambda e: e.dma_start(out=dst_d[sq, i * 128:(i + 1) * 128, :], in_=xt[xs][:]),
              reads=["xt%d" % xs], writes=[out_stream + "_dram%d" % xs])

    def stage1_A(sq, i, tc):
        xs = tc % 2
        qs = tc % 2
        use_sel = i >= 2
        S.dma("sp", "xt%d" % xs, lambda e: e.dma_start(out=xt[xs][:], in_=x_d[sq, i * 128:(i + 1) * 128, :]), writes=["xt%d" % xs])
        rms_and_transpose(xs, [(g1[:], "g1")], 1)
        yield
        kv_group(i, 0, 1024)
        yield
        b = bank()
        proj(b, 0, 3072, 72)
        S.op("act", lambda e: e.copy(out=ik_tok[:], in_=R[b][:, 8:72]), reads=[RN[b]], writes=["ik_tok"])
        if use_sel:
            S.op("act", lambda e: e.activation(out=wst[:, 0:8], in_=R[b][:, 0:8], func=AF.Abs), reads=[RN[b]], writes=["wabs"])
            S.op("act", lambda e: e.activation(out=wst[:, 8:16], in_=R[b][:, 0:8], func=AF.Sign), reads=[RN[b]], writes=["wsg"])
            for h in range(8):
                S.op("dve", (lambda h: lambda e: e.tensor_scalar(out=dsg[:, h, :], in0=ident[:], scalar1=wst[:, 8 + h:9 + h],
                                                                 scalar2=None, op0=ALU.mult))(h),
                     reads=["ident", "wsg", "dsg"], writes=["dsg"])
            fence_dve(["dsg"])
            b2 = bank()
            proj(b2, 0, 2560, 512)
            S.op("dve", lambda e: e.tensor_tensor(out=iq_tok[:, :].rearrange("p (h d) -> p h d", h=8),
                                                  in0=R[b2].rearrange("p (h d) -> p h d", h=8),
                                                  in1=wst[:, 0:8].unsqueeze(2).to_broadcast([128, 8, 64]), op=ALU.mult),
                 reads=[RN[b2], "wabs"], writes=["iq_tok"])
        yield
        k_transpose(i, 65)
        S.op("pe", lambda e: e.transpose(out=PT[0:64, 512:640], in_=ik_tok[:, :], identity=ident[:]),
             reads=["ik_tok", "ident"], writes=["T"])
        S.op("act", lambda e: e.copy(out=ikT[:, i * 128:(i + 1) * 128], in_=PT[0:64, 512:640]), reads=["T"], writes=["ikT_%d" % i])
        yield
        ms = None
        if use_sel:
            ms = tc % 2
            n = 128 * (i + 1)
            for h in range(8):
                S.op("pe", (lambda h: lambda e: e.transpose(out=PT[0:64, h * 128:(h + 1) * 128], in_=iq_tok[:, h * 64:(h + 1) * 64],
                                                            identity=ident[:]))(h), reads=["iq_tok", "ident"], writes=["T"])
            S.op("act", lambda e: e.copy(out=iqT[:, :, :].rearrange("p a b -> p (a b)"), in_=PT[0:64, :]), reads=["T"], writes=["iqT"])
            yield
            nchunk = (n + 511) // 512
            for c in range(nchunk):
                ncol = min(512, n - 512 * c)
                ikbufs = ["ikT_%d" % jj for jj in range(4 * c, min(4 * c + 4, i + 1))]
                for h in range(8):
                    by = bank()
                    rs = rb_ctr[0] % NRB
                    rb_ctr[0] += 1
                    S.op("pe", (lambda h, by, c, ncol: lambda e: e.matmul(
                        R[by][:, 0:ncol], lhsT=iqT[:, h, :], rhs=ikT[:, c * 512:c * 512 + ncol],
                        start=True, stop=True))(h, by, c, ncol), reads=["iqT"] + ikbufs, writes=[RN[by]])
                    if h % 2 == 0:
                        S.op("act", (lambda by, rs, ncol: lambda e: e.activation(out=Rb[rs][:, 0:ncol], in_=R[by][:, 0:ncol], func=AF.Relu))(by, rs, ncol),
                             reads=[RN[by]], writes=["Rb%d" % rs])
                    else:
                        S.op("dve", (lambda by, rs, ncol: lambda e: e.tensor_scalar(out=Rb[rs][:, 0:ncol], in0=R[by][:, 0:ncol],
                                                                                    scalar1=0.0, scalar2=None, op0=ALU.max))(by, rs, ncol),
                             reads=[RN[by]], writes=["Rb%d" % rs])
                    S.op("pe", (lambda h, rs, ncol: lambda e: e.matmul(
                        PTF[:, 0:ncol], lhsT=dsg[:, h, :], rhs=Rb[rs][:, 0:ncol],
                        start=(h == 0), stop=(h == 7)))(h, rs, ncol), reads=["dsg", "Rb%d" % rs], writes=["T"])
                    if h % 2 == 1:
                        yield
                S.op("act", (lambda c, ncol: lambda e: e.copy(out=isc[:, c * 512:c * 512 + ncol], in_=PTF[:, 0:ncol]))(c, ncol),
                     reads=["T", "isc"], writes=["isc"])
        pending = [lambda: q_group(qs, 0, 0, 0), lambda: q_group(qs, 1, 0, 512),
                   lambda: gate_group(qs, 0, 0, 1536), lambda: gate_group(qs, 1, 0, 2048),
                   lambda: q_transpose(qs, 65)]
        if not use_sel:
            for f in pending:
                f()
                yield
        if use_sel:
            S.op("dve", lambda e: e.tensor_tensor(out=isc[:, i * 128:(i + 1) * 128], in0=isc[:, i * 128:(i + 1) * 128],
                                                  in1=negc[:], op=ALU.add), reads=["isc", "negc"], writes=["isc"])
            HI, LO, W0, MID, CNT, TMP, THR = 0, 1, 2, 3, 4, 5, 6
            HB = 8
            S.op("dve", lambda e: e.tensor_reduce(out=bis[:, HI:HI + 1], in_=isc[:, 0:n], axis=AX.X, op=ALU.max),
                 reads=["isc"], writes=["b_hi"])
            S.op("dve", lambda e: e.tensor_reduce(out=bis[:, LO:LO + 1], in_=isc[:, 0:128 * i], axis=AX.X, op=ALU.min),
                 reads=["isc"], writes=["b_lo"])
            S.op("dve", lambda e: e.tensor_tensor(out=bis[:, W0:W0 + 1], in0=bis[:, HI:HI + 1], in1=bis[:, LO:LO + 1], op=ALU.subtract),
                 reads=["b_hi", "b_lo"], writes=["b_w0"])
            S.op("dve", lambda e: e.tensor_scalar(out=bis[:, HB:HB + NBIS], in0=halfpow[:], scalar1=bis[:, W0:W0 + 1], scalar2=None,
                                                  op0=ALU.mult), reads=["halfpow", "b_w0"], writes=["b_h"])
            S.op("dve", lambda e: e.tensor_scalar(out=bis[:, HB + NBIS:HB + 2 * NBIS], in0=bis[:, HB:HB + NBIS], scalar1=2.0, scalar2=None,
                                                  op0=ALU.mult), reads=["b_h"], writes=["b_h2"])
            S.op("dve", lambda e: e.tensor_tensor(out=bis[:, MID:MID + 1], in0=bis[:, LO:LO + 1], in1=bis[:, HB:HB + 1], op=ALU.add),
                 reads=["b_lo", "b_h"], writes=["b_mid"])
            yield
            for k in range(NBIS):
                S.op("dve", lambda e: e.tensor_scalar(out=mask_tok[:, 0:n], in0=isc[:, 0:n], scalar1=bis[:, MID:MID + 1], scalar2=0.0,
                                                      op0=ALU.is_ge, op1=ALU.add, accum_out=bis[:, CNT:CNT + 1]),
                     reads=["isc", "b_mid"], writes=["b_cnt", "mask_tok"])
                if k < NBIS - 1:
                    S.op("dve", (lambda k: lambda e: e.scalar_tensor_tensor(
                        out=bis[:, TMP:TMP + 1], in0=bis[:, CNT:CNT + 1], scalar=TOPK - 0.5,
                        in1=bis[:, HB + NBIS + k + 1:HB + NBIS + k + 2], op0=ALU.is_ge, op1=ALU.mult))(k),
                        reads=["b_cnt", "b_h2"], writes=["b_tmp"])
                    S.op("dve", (lambda k: lambda e: e.scalar_tensor_tensor(
                        out=bis[:, MID:MID + 1], in0=bis[:, TMP:TMP + 1], scalar=bis[:, HB + k + 1:HB + k + 2],
                        in1=bis[:, MID:MID + 1], op0=ALU.subtract, op1=ALU.add))(k),
                        reads=["b_tmp", "b_h", "b_mid"], writes=["b_mid"])
                else:
                    S.op("dve", (lambda k: lambda e: e.scalar_tensor_tensor(
                        out=bis[:, TMP:TMP + 1], in0=bis[:, CNT:CNT + 1], scalar=TOPK - 0.5,
                        in1=bis[:, HB + k:HB + k + 1], op0=ALU.is_ge, op1=ALU.mult))(k),
                        reads=["b_cnt", "b_h"], writes=["b_tmp"])
                    S.op("dve", (lambda k: lambda e: e.scalar_tensor_tensor(
                        out=bis[:, THR:THR + 1], in0=bis[:, TMP:TMP + 1], scalar=bis[:, HB + k:HB + k + 1],
                        in1=bis[:, MID:MID + 1], op0=ALU.subtract, op1=ALU.add))(k),
                        reads=["b_tmp", "b_h", "b_mid"], writes=["b_thr"])
                if k % 2 == 1 and pending:
                    pending.pop(0)()
                yield
            while pending:
                pending.pop(0)()
                yield
            S.op("dve", lambda e: e.tensor_scalar(out=mask_tok[:, 0:n], in0=isc[:, 0:n], scalar1=bis[:, THR:THR + 1], scalar2=-BIG,
                                                  op0=ALU.is_lt, op1=ALU.mult), reads=["isc", "b_thr"], writes=["mask_tok"])
            for j0 in range(0, i + 1, 8):
                j1 = min(i + 1, j0 + 8)
                for j in range(j0, j1):
                    S.op("pe", (lambda j, j0: lambda e: e.transpose(out=PT[:, (j - j0) * 128:(j - j0 + 1) * 128],
                                                                    in_=mask_tok[:, j * 128:(j + 1) * 128], identity=ident[:]))(j, j0),
                         reads=["mask_tok", "ident"], writes=["T"])
                S.op("act", (lambda j0, j1: lambda e: e.copy(out=maskT[ms][:, j0:j1, :].rearrange("p a b -> p (a b)"),
                                                             in_=PT[:, 0:(j1 - j0) * 128]))(j0, j1),
                     reads=["T", "maskT%d" % ms], writes=["maskT%d" % ms])
                yield

    def stage2_A(sq, i, tc):
        xs = tc % 2
        qs = tc % 2
        ms = (tc % 2) if i >= 2 else None
        yield from attention(i, qs, 65, ms)
        yield from finish_tile(sq, i, xs, qs, h1_d, "h1o")

    def stage1_B(sq, i, tc):
        xs = tc % 2
        qs = tc % 2
        own = i // 2
        S.dma("sp", "xt%d" % xs, lambda e: e.dma_start(out=xt[xs][:], in_=h1_d[sq, i * 128:(i + 1) * 128, :]),
              reads=["h1o_dram0", "h1o_dram1"], writes=["xt%d" % xs])
        rms_and_transpose(xs, [(g1[:], "g1"), (g2, "isc")], 2)
        yield
        S.op("pool", lambda e: e.memset(k_aug[:, :, 65:W], 0.0), reads=["k_aug"], writes=["k_aug"])
        S.op("pool", lambda e: e.memset(k_aug[:, :, 65 + own:66 + own], 1.0), reads=["k_aug"], writes=["k_aug"])
        kv_group(i, 0, 0)
        yield
        k_transpose(i, W)
        if i % 2 == 1:
            S.op("dve", lambda e: e.tensor_reduce(out=kmTf[:, :, own], in_=kTc[0:64, :, own * 256:(own + 1) * 256], axis=AX.X, op=ALU.add),
                 reads=["kTc_%d" % (i - 1), "kTc_%d" % i, "kmTf"], writes=["kmTf"])
            S.op("dve", lambda e: e.tensor_copy(out=kmT[:], in_=kmTf[:]), reads=["kmTf"], writes=["kmT"])
        yield
        q_group(qs, 0, 1, 512)
        yield
        q_group(qs, 1, 1, 1024)
        yield
        if own >= 4:
            S.op("dve", lambda e: e.tensor_reduce(out=qsumf[:, :, :], in_=q_aug[qs][:, :, 0:DH].rearrange("p (g j) d -> p g d j", g=G),
                                                  axis=AX.X, op=ALU.add), reads=["q_aug%d" % qs], writes=["qsumf"])
            S.op("dve", lambda e: e.tensor_copy(out=qsum[:], in_=qsumf[:]), reads=["qsumf"], writes=["qsum"])
            for g in range(G):
                S.op("pe", (lambda g: lambda e: e.transpose(out=PT[0:64, g * 128:(g + 1) * 128], in_=qsum[:, g, :], identity=ident[:]))(g),
                     reads=["qsum", "ident"], writes=["T"])
            S.op("act", lambda e: e.copy(out=qsT[:, :, :].rearrange("p a b -> p (a b)"), in_=PT[0:64, 0:512]), reads=["T"], writes=["qsT"])
            b = bank()
            for g in range(G):
                S.op("pe", (lambda g: lambda e: e.matmul(R[b][:, g * 8:(g + 1) * 8], lhsT=qsT[:, g, :], rhs=kmT[:, g, :],
                                                         start=True, stop=True))(g), reads=["qsT", "kmT"], writes=[RN[b]])
            S.op("dve", lambda e: e.tensor_copy(out=gsb[:, :, :].rearrange("p g n -> p (g n)"), in_=R[b][:, 0:32]), reads=[RN[b]], writes=["gsb"])
            S.op("dve", lambda e: e.memset(gsb[:, :, own:8], -1e30), reads=["gsb"], writes=["gsb"])
            for g in range(G):
                S.op("dve", (lambda g: lambda e: e.max(out=m8[:, g, :], in_=gsb[:, g, :]))(g), reads=["gsb", "m8"], writes=["m8"])
            for g in range(G):
                S.op("dve", (lambda g: lambda e: e.tensor_scalar(out=sel[:, g, :], in0=gsb[:, g, :], scalar1=m8[:, g, 2:3], scalar2=BIG,
                                                                 op0=ALU.is_ge, op1=ALU.mult))(g), reads=["gsb", "m8", "sel"], writes=["sel"])
            for g in range(G):
                S.op("dve", (lambda g: lambda e: e.tensor_scalar(out=q_aug[qs][:, 4 * g:4 * g + 4, 65:W],
                                                                 in0=sel[:, g, :].unsqueeze(1).to_broadcast([128, 4, 8]),
                                                                 scalar1=-BIG, scalar2=None, op0=ALU.add))(g),
                     reads=["sel", "q_aug%d" % qs], writes=["q_aug%d" % qs])
            S.op("dve", lambda e: e.memset(q_aug[qs][:, :, 65 + own:66 + own], 0.0), reads=["q_aug%d" % qs], writes=["q_aug%d" % qs])
            fence_dve(["q_aug%d" % qs])
            if dbg_tile is not None and i == dbg_tile and sq == 0:
                dbg_dump("gsb", gsb[:], [128, G, 8], F32, ["gsb"])
                dbg_dump("qaug", q_aug[qs][:], [128, H, W], BF16, ["q_aug%d" % qs])
        else:
            S.op("dve", lambda e: e.memset(q_aug[qs][:, :, 65:W], 0.0), reads=["q_aug%d" % qs], writes=["q_aug%d" % qs])
            fence_dve(["q_aug%d" % qs])
        yield
        q_transpose(qs, W)
        yield
        gate_group(qs, 0, 1, 1536)
        yield
        gate_group(qs, 1, 1, 2048)
        yield

    def stage2_B(sq, i, tc):
        xs = tc % 2
        qs = tc % 2
        yield from attention(i, qs, W, None)
        yield from finish_tile(sq, i, xs, qs, out_d, "outo")

    class _Null:
        pass

    def count_steps(genfn, args):
        saved = (bank_ctr[0], pt_ctr[0], rb_ctr[0], set(held), dbg_n[0])
        real_op, real_dma = S.op, S.dma
        S.op = lambda *a, **k: None
        S.dma = lambda *a, **k: None
        n = 0
        for _ in genfn(*args):
            n += 1
        S.op, S.dma = real_op, real_dma
        bank_ctr[0], pt_ctr[0], rb_ctr[0] = saved[0], saved[1], saved[2]
        held.clear()
        held.update(saved[3])
        dbg_n[0] = saved[4]
        return n + 1

    def drain(gen):
        for _ in gen:
            pass

    def run_phase(tiles, s1, s2):
        prev = None
        for t in list(tiles) + [None]:
            if not pipeline:
                if t is not None:
                    drain(s1(*t))
                    drain(s2(*t))
                continue
            if t is not None and prev is not None:
                n1 = count_steps(s1, t)
                n2 = count_steps(s2, prev)
                ga, gb = s1(*t), s2(*prev)
                d1 = d2 = 0
                a_alive = b_alive = True
                while a_alive or b_alive:
                    if b_alive and (not a_alive or d2 * n1 <= d1 * n2):
                        try:
                            next(gb)
                            d2 += 1
                        except StopIteration:
                            b_alive = False
                    else:
                        try:
                            next(ga)
                            d1 += 1
                        except StopIteration:
                            a_alive = False
            elif t is not None:
                drain(s1(*t))
            elif prev is not None:
                drain(s2(*prev))
            prev = t

    tile_ctr = [0]

    def tiles_of_phase():
        out = []
        for sq in range(nseq):
            for i in range(ntiles):
                out.append((sq, i, tile_ctr[0]))
                tile_ctr[0] += 1
        return out

    if doA:
        load_gain(g1, norm_a_g, 32.0, "g1")
        load_gk(kn_a_g, qn_a_g)
        load_w(wbig, w_in_a, A_COLS, 0, "wbig")
        load_w(wo, w_out_a, D, 0, "wo")
        run_phase(tiles_of_phase(), stage1_A, stage2_A)

    if doB:
        load_gain(g1, norm_kv_g, 32.0, "g1")
        S.dma("sp", "g2", lambda e: e.dma_start(out=g2, in_=norm_b_g.partition_broadcast(128)), writes=["isc"])
        S.op("dve", lambda e: e.tensor_scalar(out=g2, in0=g2, scalar1=32.0, scalar2=None, op0=ALU.mult), reads=["isc"], writes=["isc"])
        load_gk(kn_b_g, qn_b_g)
        load_w(wbig, w_kv, 512, 0, "wbig")
        load_w(wbig, w_in_b, 2048, 512, "wbig")
        load_w(wo, w_out_b, D, 0, "wo")
        S.op("pool", lambda e: e.memset(kmT[:], 0.0), writes=["kmT"])
        S.op("pool", lambda e: e.memset(kmTf[:], 0.0), writes=["kmTf"])
        run_phase(tiles_of_phase(), stage1_B, stage2_B)

    S.finalize()
    S.emit()
    es.close()
    return nc


_PROG_CACHE = {}


def _get_prog(mode):
    if mode not in _PROG_CACHE:
        _PROG_CACHE[mode] = build_program(mode)
    return _PROG_CACHE[mode]


FUSED = True


def kernel(x, norm_a_g, w_in_a, qn_a_g, kn_a_g, w_out_a, rel_bias, norm_kv_g, w_kv,
           kn_b_g, norm_b_g, w_in_b, qn_b_g, w_out_b):
    f = lambda a: np.ascontiguousarray(np.asarray(a, dtype=np.float32))
    x = f(x)
    rel_bias = f(rel_bias)
    nbias = _near_bias_layout(rel_bias).reshape(2, 128, H * 128)
    a_in = {"w_in_a": f(w_in_a)[0], "w_out_a": f(w_out_a)[0], "norm_a_g": f(norm_a_g)[0],
            "qn_a_g": f(qn_a_g)[0], "kn_a_g": f(kn_a_g)[0]}
    b_in = {"w_kv": f(w_kv), "w_in_b": f(w_in_b)[0], "w_out_b": f(w_out_b)[0], "norm_kv_g": f(norm_kv_g),
            "norm_b_g": f(norm_b_g)[0], "kn_b_g": f(kn_b_g), "qn_b_g": f(qn_b_g)[0]}
    common = {"rel_bias": rel_bias, "nbias": nbias}
    xs = [np.ascontiguousarray(x[NSEQ * c:NSEQ * (c + 1)]) for c in range(NCORES)]
    cores = list(range(NCORES))
    if FUSED:
        nc = _get_prog("AB")
        in_maps = [dict(x=xs[c], **a_in, **b_in, **common) for c in cores]
        res = run_bass_kernel_spmd(nc, in_maps, core_ids=cores)
        outs = [r["out"] for r in res.results]
    else:
        ncA = _get_prog("A")
        resA = run_bass_kernel_spmd(ncA, [dict(x=xs[c], **a_in, **common) for c in cores], core_ids=cores)
        h1 = [r["h1"] for r in resA.results]
        ncB = _get_prog("B")
        resB = run_bass_kernel_spmd(ncB, [dict(h1=h1[c], **b_in, **common) for c in cores], core_ids=cores)
        outs = [r["out"] for r in resB.results]
    return np.concatenate(outs, axis=0).astype(np.float32)
```

```python
import numpy as np
import concourse.bass as bass
import concourse.mybir as mybir
from concourse.bass_utils import run_bass_kernel_spmd
from contextlib import ExitStack

F32 = mybir.dt.float32
BF16 = mybir.dt.bfloat16
AF = mybir.ActivationFunctionType
ALU = mybir.AluOpType
AX = mybir.AxisListType

T = 2048
D = 1024
NT = 16
NSEQ = 2
NCORES = 8
H = 16
G = 4
DH = 64
EPS = 1e-6
BIG = 30000.0
A_COLS = 3144
NBIS = 12
TOPK = 256

ENGS = ("pe", "act", "dve", "pool", "sp")


class Buf:
    __slots__ = ("name", "last_w", "readers")

    def __init__(self, name):
        self.name = name
        self.last_w = None
        self.readers = []


class Op:
    __slots__ = ("eng", "idx", "fn", "deps", "inc", "incval", "stream", "is_dma", "waits")

    def __init__(self, eng, idx, fn, is_dma=False, stream=None):
        self.eng = eng
        self.idx = idx
        self.fn = fn
        self.deps = []
        self.inc = False
        self.incval = 0
        self.stream = stream
        self.is_dma = is_dma
        self.waits = []


class Sched:
    def __init__(self, nc, es):
        self.nc = nc
        self.es = es
        self.ops = {e: [] for e in ENGS}
        self.streams = {}
        self.sems = {}
        self.bufs = {}

    def _B(self, x):
        if isinstance(x, Buf):
            return x
        b = self.bufs.get(x)
        if b is None:
            b = self.bufs[x] = Buf(x)
        return b

    def _record(self, op, reads, writes):
        deps = []
        for r in reads:
            r = self._B(r)
            if r.last_w is not None:
                deps.append(r.last_w)
            r.readers.append(op)
        for w in writes:
            w = self._B(w)
            if w.last_w is not None:
                deps.append(w.last_w)
            last = {}
            for x in w.readers:
                if x is op:
                    continue
                key = ("dma", x.stream, x.idx) if x.is_dma else x.eng
                if key not in last or x.idx > last[key].idx:
                    last[key] = x
            deps.extend(last.values())
            w.last_w = op
            w.readers = []
        op.deps = deps

    def op(self, eng, fn, reads=(), writes=()):
        o = Op(eng, len(self.ops[eng]), fn)
        self.ops[eng].append(o)
        self._record(o, reads, writes)
        return o

    def dma(self, eng, stream, fn, reads=(), writes=()):
        o = Op(eng, len(self.ops[eng]), fn, is_dma=True, stream=stream)
        self.ops[eng].append(o)
        self.streams.setdefault(stream, []).append(o)
        self._record(o, reads, writes)
        return o

    def finalize(self, final_wait_eng="sp"):
        fin = Op(final_wait_eng, len(self.ops[final_wait_eng]), None)
        fin.deps = [lst[-1] for lst in self.streams.values()]
        self.ops[final_wait_eng].append(fin)
        for name, lst in self.streams.items():
            for i, o in enumerate(lst):
                o.incval = 16 * (i + 1)
        sel = {}
        for e in ENGS:
            waited = {}
            for o in self.ops[e]:
                need = {}
                for d in o.deps:
                    if d.is_dma:
                        key = "dma:" + d.stream
                        pos = d.incval
                    else:
                        if d.eng == o.eng and d.eng == "pe":
                            continue
                        key = d.eng
                        pos = d.idx
                    if key not in need or pos > need[key][0]:
                        need[key] = (pos, d)
                lst = []
                for k, (pos, d) in need.items():
                    if pos > waited.get(k, -1):
                        waited[k] = pos
                        lst.append((k, d))
                        if not d.is_dma:
                            d.inc = True
                sel[id(o)] = lst
        for e in ENGS:
            c = 0
            for o in self.ops[e]:
                if o.is_dma:
                    continue
                if o.inc:
                    c += 1
                    o.incval = c
        for e in ENGS:
            for o in self.ops[e]:
                o.waits = [(k, d.incval) for (k, d) in sel[id(o)]]

    def emit(self):
        nc = self.nc
        keys = list(ENGS) + ["dma:" + s for s in self.streams]
        for k in keys:
            self.sems[k] = self.es.enter_context(nc.semaphore("s_" + k.replace(":", "_")))
        sems = self.sems
        ops = self.ops

        def run(engname, eng):
            for o in ops[engname]:
                if o.fn is None:
                    for (k, v) in o.waits:
                        eng.wait_ge(sems[k], v)
                    continue
                for (k, v) in o.waits[1:]:
                    eng.wait_ge(sems[k], v)
                ins = o.fn(eng)
                if o.waits:
                    ins._wait_ge(sems[o.waits[0][0]], o.waits[0][1])
                if o.is_dma:
                    ins.then_inc(sems["dma:" + o.stream], 16)
                elif o.inc:
                    ins.then_inc(sems[engname], 1)

        with nc.Block() as block:
            @block.tensor
            def _(e):
                run("pe", e)

            @block.scalar
            def _(e):
                run("act", e)

            @block.vector
            def _(e):
                run("dve", e)

            @block.gpsimd
            def _(e):
                run("pool", e)

            @block.sync
            def _(e):
                run("sp", e)


def _rel_bucket_np(dist):
    n = np.maximum(dist, 0)
    max_exact = 16
    nf = np.maximum(n, 1).astype(np.float32)
    large = max_exact + (np.log(nf / np.float32(max_exact)) / np.float32(np.log(128 / max_exact))
                         * np.float32(32 - max_exact)).astype(np.int32)
    large = np.minimum(large, 31)
    return np.where(n < max_exact, n, large)


def _near_bias_layout(rel_bias):
    s = np.arange(128)[:, None]
    t = np.arange(128)[None, :]
    out = np.empty((2, 128, H, 128), np.float32)
    for d in range(2):
        bk = _rel_bucket_np(128 * d + t - s)
        out[d] = np.transpose(rel_bias[bk], (0, 2, 1))
    return np.ascontiguousarray(out)


def build_program(mode="AB", nseq=NSEQ, ntiles=NT, dbg_tile=None, pipeline=True):
    nc = bass.Bass("TRN2", target_bir_lowering=False)
    doA = "A" in mode
    doB = "B" in mode

    def din(name, shape):
        return nc.dram_tensor(name, shape, F32, kind="ExternalInput").ap()

    if doA:
        x_d = din("x", [nseq, T, D])
        w_in_a = din("w_in_a", [D, A_COLS])
        w_out_a = din("w_out_a", [D, D])
        norm_a_g = din("norm_a_g", [D])
        qn_a_g = din("qn_a_g", [DH])
        kn_a_g = din("kn_a_g", [DH])
    if doB:
        w_kv = din("w_kv", [D, 512])
        w_in_b = din("w_in_b", [D, 2048])
        w_out_b = din("w_out_b", [D, D])
        norm_kv_g = din("norm_kv_g", [D])
        norm_b_g = din("norm_b_g", [D])
        kn_b_g = din("kn_b_g", [DH])
        qn_b_g = din("qn_b_g", [DH])
        out_d = nc.dram_tensor("out", [nseq, T, D], F32, kind="ExternalOutput").ap()
    rel_bias = din("rel_bias", [32, H])
    nbias_d = din("nbias", [2, 128, H * 128])
    if mode == "A":
        h1_d = nc.dram_tensor("h1", [nseq, T, D], F32, kind="ExternalOutput").ap()
    elif mode == "B":
        h1_d = din("h1", [nseq, T, D])
    else:
        h1_d = nc.dram_tensor("h1", [nseq, T, D], F32, kind="Internal").ap()

    es = ExitStack()
    S = Sched(nc, es)

    def sb(name, shape, dt):
        return es.enter_context(nc.sbuf_tensor(name, shape, dt))

    def ps(name, shape, dt):
        return es.enter_context(nc.psum_tensor(name, shape, dt))

    W = 73

    ident = sb("ident", [128, 128], BF16)
    NB = sb("NB", [128, 2, H, 128], BF16)
    g1 = sb("g1", [128, D], F32)
    gk = sb("gk", [128, DH], F32)
    gtmp = sb("gtmp", [128, DH], F32)
    cb = sb("cb", [128, H], F32)
    epsb = sb("epsb", [128, 2], F32)
    negc = sb("negc", [128, 128], F32)
    halfpow = sb("halfpow", [128, NBIS], F32)
    vc = sb("vc", [128, NT, G, 65], BF16)
    kTc = sb("kTc", [W, G, T], BF16)
    ikT = sb("ikT", [128, T], BF16)
    wbig = sb("wbig", [128, 8, A_COLS], BF16)
    wo = sb("wo", [128, 8, D], BF16)
    xt = [sb("xt%d" % k, [128, D], F32) for k in range(2)]
    junk_a = sb("junk_a", [128, D], BF16)
    sqb = sb("sqb", [128, D], F32)
    hn0 = sb("hn0", [128, D], BF16)
    hT0 = sb("hT0", [128, 8, 128], BF16)
    q_aug = [sb("q_aug%d" % k, [128, H, W], BF16) for k in range(2)]
    k_aug = sb("k_aug", [128, G, W], BF16)
    ktmp = sb("ktmp", [128, G * DH], F32)
    qT = [sb("qT%d" % k, [W, H, 128], BF16) for k in range(2)]
    thb = sb("thb", [128, D], F32)
    sg = [sb("sg%d" % k, [128, D], BF16) for k in range(2)]
    sgp = sb("sgp", [128, D], BF16)
    sgT = sb("sgT", [128, 8, 128], BF16)
    st = sb("st", [128, 96], F32)
    ik_tok = sb("ik_tok", [128, 2, 64], BF16)
    iq_tok = sb("iq_tok", [128, 512], BF16)
    iqT = sb("iqT", [128, 4, 128], BF16)
    wst = sb("wst", [128, 16], F32)
    dsg = sb("dsg", [128, 8, 128], BF16)
    NRB = 4
    Rb = [sb("Rb%d" % k, [128, 512], BF16) for k in range(NRB)]
    isc = sb("isc", [128, T], F32)
    bis = sb("bis", [128, 8 + 2 * NBIS], F32)
    g2 = isc[:, 0:1024]
    hn1 = isc[:, 1024:1536].bitcast(BF16)
    hT1 = isc[:, 1536:2048].bitcast(BF16)
    HN = [(hn0[:, :], "hn0"), (hn1, "isc")]
    HT = [(hT0[:, :, :].rearrange("p a b -> p (a b)"), "hT0"), (hT1, "isc")]
    mask_tok = sb("mask_tok", [128, T], BF16)
    maskT = [sb("maskT%d" % k, [128, NT, 128], BF16) for k in range(2)]
    NPT = 5
    pT = [sb("pT%d" % k, [128, 512], BF16) for k in range(NPT)]
    drow = [sb("drow%d" % k, [1, 512], BF16) for k in range(2)]
    ones1 = sb("ones1", [1, 8], BF16)
    rdt = sb("rdt", [128, H], F32)
    ogT = sb("ogT", [128, 8, 128], BF16)
    if doB:
        kmT = sb("kmT", [64, G, 8], BF16)
        kmTf = sb("kmTf", [64, G, 8], F32)
        qsum = sb("qsum", [128, G, DH], BF16)
        qsumf = sb("qsumf", [128, G, DH], F32)
        qsT = sb("qsT", [64, G, 128], BF16)
        gsb = sb("gsb", [128, G, 8], F32)
        m8 = sb("m8", [128, G, 8], F32)
        sel = sb("sel", [128, G, 8], F32)

    RBK = ps("PS_R", [128, 1536], F32)
    PO = ps("PS_O", [128, 2048], F32)
    PT = ps("PS_T", [128, 1024], BF16)
    PTF = PT[:, :].bitcast(F32)
    NRK = 3
    R = [RBK[:, k * 512:(k + 1) * 512] for k in range(NRK)]
    RN = ["R%d" % k for k in range(NRK)]
    OB = ["O0", "O1", "O2", "O3"]
    bank_ctr = [0]
    held = set()

    def bank(hold=False):
        while True:
            k = bank_ctr[0] % NRK
            bank_ctr[0] += 1
            if k not in held:
                break
        if hold:
            held.add(k)
        return k

    S.op("pool", lambda e: e.memset(isc[:, 0:128], 0.0), writes=["isc"])
    S.op("pool", lambda e: e.affine_select(out=isc[:, 0:128], in_=isc[:, 0:128], pattern=[[-1, 128]],
                                           compare_op=ALU.not_equal, fill=1.0, base=0, channel_multiplier=1),
         reads=["isc"], writes=["isc"])
    S.op("dve", lambda e: e.tensor_copy(out=ident[:], in_=isc[:, 0:128]), reads=["isc"], writes=["ident"])
    S.op("pool", lambda e: e.memset(junk_a[:, :], 0.0))
    S.op("pool", lambda e: e.memset(ones1[:, :], 1.0), writes=["ones1"])
    S.op("pool", lambda e: e.memset(epsb[:, 0:1], float(D * EPS)), writes=["epsb"])
    S.op("pool", lambda e: e.memset(epsb[:, 1:2], float(DH * EPS)), reads=["epsb"], writes=["epsb"])
    S.op("pool", lambda e: e.memset(negc[:], 0.0), writes=["negc"])
    S.op("pool", lambda e: e.affine_select(out=negc[:], in_=negc[:], pattern=[[-1, 128]],
                                           compare_op=ALU.is_ge, fill=-BIG, base=0, channel_multiplier=1),
         reads=["negc"], writes=["negc"])
    for k in range(NBIS):
        S.op("pool", (lambda k: lambda e: e.memset(halfpow[:, k:k + 1], 2.0 ** -(k + 1)))(k), writes=["halfpow"])
    S.op("pool", lambda e: e.memset(vc[:, :, :, 64:65], 1.0), writes=["vc_init"])
    S.op("pool", lambda e: e.memset(k_aug[:, :, 64:W], 0.0), writes=["k_aug"])
    S.op("pool", lambda e: e.memset(k_aug[:, :, 64:65], 1.0), reads=["k_aug"], writes=["k_aug"])
    S.dma("sp", "cb", lambda e: e.dma_start(out=cb[:], in_=rel_bias[31, :].partition_broadcast(128)), writes=["cb"])
    for k in range(2):
        S.op("pool", (lambda k: lambda e: e.memset(q_aug[k][:, :, 64:W], 0.0))(k), writes=["q_aug%d" % k])
        S.op("dve", (lambda k: lambda e: e.tensor_copy(out=q_aug[k][:, :, 64:65], in_=cb[:, :].unsqueeze(2)))(k),
             reads=["cb", "q_aug%d" % k], writes=["q_aug%d" % k])
    for d in range(2):
        S.dma("sp", "isc", (lambda d: lambda e: e.dma_start(out=isc[:, :], in_=nbias_d[d]))(d), writes=["isc"])
        S.op("dve", lambda e: e.tensor_tensor(out=isc[:, :].rearrange("p (h t) -> p h t", h=H),
                                              in0=isc[:, :].rearrange("p (h t) -> p h t", h=H),
                                              in1=cb[:, :].unsqueeze(2).to_broadcast([128, H, 128]),
                                              op=ALU.subtract), reads=["isc", "cb"], writes=["isc"])
        if d == 0:
            S.op("pool", lambda e: e.affine_select(out=isc[:, :].rearrange("p (h t) -> p h t", h=H),
                                                   in_=isc[:, :].rearrange("p (h t) -> p h t", h=H),
                                                   pattern=[[0, H], [1, 128]], compare_op=ALU.is_ge, fill=-BIG,
                                                   base=0, channel_multiplier=-1), reads=["isc"], writes=["isc"])
        S.op("act", (lambda d: lambda e: e.copy(out=NB[:, d, :, :], in_=isc[:, :].rearrange("p (h t) -> p h t", h=H)))(d),
             reads=["isc"], writes=["NB"])

    def load_gain(dst, src_ap, scale, name):
        S.dma("sp", name, lambda e: e.dma_start(out=dst[:], in_=src_ap.partition_broadcast(128)), writes=[name])
        S.op("dve", lambda e: e.tensor_scalar(out=dst[:], in0=dst[:], scalar1=float(scale), scalar2=None, op0=ALU.mult),
             reads=[name], writes=[name])

    def load_gk(kn_ap, qn_ap):
        S.dma("sp", "gk", lambda e: e.dma_start(out=gk[:], in_=kn_ap.partition_broadcast(128)), writes=["gk"])
        S.dma("sp", "gtmp", lambda e: e.dma_start(out=gtmp[:], in_=qn_ap.partition_broadcast(128)), writes=["gtmp"])
        S.op("dve", lambda e: e.scalar_tensor_tensor(out=gk[:], in0=gk[:], scalar=8.0, in1=gtmp[:],
                                                     op0=ALU.mult, op1=ALU.mult), reads=["gk", "gtmp"], writes=["gk"])

    def load_w(dst, src, ncols, col0, bufname):
        srcv = src.rearrange("(kc p) n -> p kc n", p=128)
        for kc in range(8):
            S.dma("pool", bufname + str(kc),
                  (lambda kc: lambda e: e.dma_start(out=dst[:, kc, col0:col0 + ncols], in_=srcv[:, kc, :]))(kc),
                  writes=[bufname + str(kc)])

    pt_ctr = [0]
    rb_ctr = [0]
    dbg_n = [0]

    def fence_dve(bufs):
        S.op("dve", lambda e: e.tensor_copy(out=junk_a[:, 0:512], in_=junk_a[:, 512:1024]), reads=bufs, writes=bufs)

    def dbg_dump(name, ap, shape, dt, bufs):
        dn = nc.dram_tensor("dbg_" + name, list(shape), dt, kind="ExternalOutput").ap()
        dbg_n[0] += 1
        S.dma("sp", "dbg%d" % dbg_n[0], lambda e: e.dma_start(out=dn, in_=ap), reads=bufs)

    def rsqrt_act(out_ap, in_ap, c, rbuf, wbuf, toff):
        n = in_ap.shape[1]
        tmp = st[:, toff:toff + n]
        tn = "st_tmp%d" % toff
        S.op("act", lambda e: e.activation(out=tmp, in_=in_ap, func=AF.Ln, bias=epsb[:, {1024: 0, 64: 1}[int(round(c / EPS))]:{1024: 1, 64: 2}[int(round(c / EPS))]]),
             reads=[rbuf, "epsb"], writes=[tn])
        S.op("act", lambda e: e.activation(out=out_ap, in_=tmp, func=AF.Exp, scale=-0.5), reads=[tn], writes=[wbuf])

    def rms_and_transpose(xs, gains, nh):
        S.op("act", lambda e: e.activation(out=junk_a[:], in_=xt[xs][:], func=AF.Square, accum_out=st[:, 0:1]),
             reads=["xt%d" % xs], writes=["st_ss"])
        rsqrt_act(st[:, 1:2], st[:, 0:1], float(D * EPS), "st_ss", "st_r", 2)
        for k in range(nh):
            hn_ap, hn_nm = HN[k]
            ht_ap, ht_nm = HT[k]
            g_ap, g_nm = gains[k]
            S.op("dve", (lambda hn_ap, g_ap: lambda e: e.scalar_tensor_tensor(out=hn_ap, in0=xt[xs][:], scalar=st[:, 1:2],
                                                                             in1=g_ap, op0=ALU.mult, op1=ALU.mult))(hn_ap, g_ap),
                 reads=["xt%d" % xs, "st_r", g_nm], writes=[hn_nm])
            for kc in range(8):
                S.op("pe", (lambda hn_ap, kc: lambda e: e.transpose(out=PT[:, kc * 128:(kc + 1) * 128],
                                                                   in_=hn_ap[:, kc * 128:(kc + 1) * 128], identity=ident[:]))(hn_ap, kc),
                     reads=[hn_nm, "ident"], writes=["T"])
            S.op("act", (lambda ht_ap: lambda e: e.copy(out=ht_ap, in_=PT[:, :]))(ht_ap), reads=["T"], writes=[ht_nm])

    def proj(b, hk, col0, ncols):
        ht_ap, ht_nm = HT[hk]
        for kc in range(8):
            S.op("pe", (lambda kc: lambda e: e.matmul(R[b][:, 0:ncols], lhsT=ht_ap[:, kc * 128:(kc + 1) * 128], rhs=wbig[:, kc, col0:col0 + ncols],
                                                      start=(kc == 0), stop=(kc == 7)))(kc),
                 reads=[ht_nm, "wbig%d" % kc], writes=[RN[b]])

    def q_group(qs, grp, hk, col0):
        b = bank()
        proj(b, hk, col0, 512)
        so = 16 + 8 * grp
        S.op("act", lambda e: e.activation(out=sqb[:, grp * 512:(grp + 1) * 512], in_=R[b], func=AF.Square),
             reads=[RN[b]], writes=["sqb%d" % grp])
        S.op("dve", lambda e: e.tensor_reduce(out=st[:, so:so + 8], in_=sqb[:, grp * 512:(grp + 1) * 512].rearrange("p (h d) -> p h d", h=8),
                                              axis=AX.X, op=ALU.add), reads=["sqb%d" % grp], writes=["st_q%d" % grp])
        rsqrt_act(st[:, so:so + 8], st[:, so:so + 8], float(DH * EPS), "st_q%d" % grp, "st_q%d" % grp, 40 + 8 * grp)
        S.op("dve", lambda e: e.tensor_tensor(out=q_aug[qs][:, 8 * grp:8 * grp + 8, 0:DH], in0=R[b].rearrange("p (h d) -> p h d", h=8),
                                              in1=st[:, so:so + 8].unsqueeze(2).to_broadcast([128, 8, DH]), op=ALU.mult),
             reads=[RN[b], "st_q%d" % grp, "q_aug%d" % qs], writes=["q_aug%d" % qs])

    def kv_group(i, hk, col0):
        b = bank()
        proj(b, hk, col0, 512)
        kv_ap = R[b]
        S.op("act", lambda e: e.activation(out=junk_a[:, 0:256], in_=kv_ap[:, 0:256], func=AF.Square), reads=[RN[b]], writes=["sqk"])
        S.op("dve", lambda e: e.tensor_reduce(out=st[:, 32:36], in_=junk_a[:, 0:256].rearrange("p (g d) -> p g d", g=G),
                                              axis=AX.X, op=ALU.add), reads=["sqk"], writes=["st_k"])
        rsqrt_act(st[:, 32:36], st[:, 32:36], float(DH * EPS), "st_k", "st_k", 56)
        S.op("dve", lambda e: e.tensor_tensor(out=ktmp[:, :].rearrange("p (g d) -> p g d", g=G),
                                              in0=kv_ap[:, 0:256].rearrange("p (g d) -> p g d", g=G),
                                              in1=st[:, 32:36].unsqueeze(2).to_broadcast([128, G, DH]), op=ALU.mult),
             reads=[RN[b], "st_k"], writes=["ktmp"])
        S.op("pool", lambda e: e.tensor_tensor(out=k_aug[:, :, 0:DH], in0=ktmp[:, :].rearrange("p (g d) -> p g d", g=G),
                                               in1=gk[:, :].unsqueeze(1).to_broadcast([128, G, DH]), op=ALU.mult),
             reads=["ktmp", "gk", "k_aug"], writes=["k_aug"])
        S.op("act", lambda e: e.copy(out=vc[:, i, :, 0:DH], in_=kv_ap[:, 256:512].rearrange("p (g d) -> p g d", g=G)),
             reads=[RN[b], "vc_init"], writes=["vc_%d" % i])

    def k_transpose(i, Wk):
        for g in range(G):
            S.op("pe", (lambda g: lambda e: e.transpose(out=PT[0:Wk, g * 128:(g + 1) * 128], in_=k_aug[:, g, 0:Wk],
                                                        identity=ident[:]))(g), reads=["k_aug", "ident"], writes=["T"])
        S.op("act", lambda e: e.copy(out=kTc[0:Wk, :, i * 128:(i + 1) * 128],
                                     in_=PT[0:Wk, 0:512].rearrange("p (g t) -> p g t", g=G)), reads=["T"], writes=["kTc_%d" % i])

    def gate_group(gs_, grp, hk, col0):
        b = bank()
        proj(b, hk, col0, 512)
        tb = thb[:, grp * 512:(grp + 1) * 512]
        tn = "thb%d" % grp
        S.op("act", lambda e: e.activation(out=tb, in_=R[b], func=AF.Exp, scale=-1.0), reads=[RN[b]], writes=[tn])
        S.op("act", lambda e: e.activation(out=tb, in_=tb, func=AF.Ln, bias=1.0), reads=[tn], writes=[tn])
        S.op("act", lambda e: e.activation(out=tb, in_=tb, func=AF.Exp, scale=-1.0), reads=[tn], writes=[tn])
        S.op("dve", lambda e: e.tensor_tensor(out=sg[gs_][:, grp * 512:(grp + 1) * 512], in0=R[b], in1=tb, op=ALU.mult),
             reads=[tn, RN[b], "sg%d" % gs_], writes=["sg%d" % gs_])

    def q_transpose(qs, Wq):
        for rnd in range(2):
            b = bank()
            pv = R[b].bitcast(BF16)
            for hh in range(8):
                h = rnd * 8 + hh
                S.op("pe", (lambda h, hh, pv: lambda e: e.transpose(out=pv[0:Wq, hh * 128:(hh + 1) * 128], in_=q_aug[qs][:, h, 0:Wq],
                                                                    identity=ident[:]))(h, hh, pv), reads=["q_aug%d" % qs, "ident"], writes=[RN[b]])
            S.op("act", (lambda rnd, pv: lambda e: e.copy(out=qT[qs][0:Wq, rnd * 8:(rnd + 1) * 8, :].rearrange("p a b -> p (a b)"), in_=pv[0:Wq, :]))(rnd, pv),
                 reads=[RN[b], "qT%d" % qs], writes=["qT%d" % qs])

    def attention(i, qs, Wq, ms):
        steps = [(j, g) for j in range(i + 1) for g in range(G)]
        info = {}

        def qk(n):
            j, g = steps[n]
            near = (i - j) <= 1
            slot = pt_ctr[0] % NPT
            pt_ctr[0] += 1
            b = bank(hold=True)
            info[n] = (b, slot)
            masked = ms is not None
            S.op("pe", (lambda g, j, b: lambda e: e.matmul(
                R[b], lhsT=kTc[0:Wq, g, j * 128:(j + 1) * 128],
                rhs=qT[qs][0:Wq, 4 * g:4 * g + 4, :].rearrange("p a b -> p (a b)"),
                start=True, stop=(not near and not masked)))(g, j, b),
                reads=["kTc_%d" % j, "qT%d" % qs], writes=[RN[b]])
            if near:
                S.op("pe", (lambda g, j, b: lambda e: e.matmul(
                    R[b], lhsT=ident[:], rhs=NB[:, i - j, 4 * g:4 * g + 4, :].rearrange("p a b -> p (a b)"),
                    start=False, stop=(not masked)))(g, j, b), reads=["ident", "NB"], writes=[RN[b]])
            if masked:
                S.op("pe", (lambda j, b: lambda e: e.matmul(
                    R[b].rearrange("p (h t) -> p h t", h=4), lhsT=ident[:],
                    rhs=maskT[ms][:, j, :].unsqueeze(1).to_broadcast([128, 4, 128]),
                    start=False, stop=True))(j, b), reads=["ident", "maskT%d" % ms], writes=[RN[b]])

        qk(0)
        for n in range(len(steps)):
            j, g = steps[n]
            b, slot = info.pop(n)
            S.op("act", (lambda b, slot: lambda e: e.activation(out=pT[slot][:], in_=R[b], func=AF.Exp))(b, slot),
                 reads=[RN[b]], writes=["pT%d" % slot])
            held.discard(b)
            if n + 1 < len(steps):
                qk(n + 1)
            S.op("pe", (lambda g, slot, j: lambda e: e.matmul(
                PO[0:65, g * 512:(g + 1) * 512], lhsT=vc[:, j, g, :], rhs=pT[slot][:, :],
                start=(j == 0), stop=(j == i)))(g, slot, j),
                reads=["pT%d" % slot, "vc_%d" % j], writes=[OB[g]])
            yield

    def finish_tile(sq, i, xs, gs_, dst_d, out_stream):
        bd = bank()
        for g in range(G):
            rs = g % 2
            S.op("act", (lambda g, rs: lambda e: e.copy(out=drow[rs][0:1, :], in_=PO[64:65, g * 512:(g + 1) * 512]))(g, rs),
                 reads=[OB[g]], writes=["drow%d" % rs])
            for jh in range(4):
                h = 4 * g + jh
                S.op("pe", (lambda rs, jh, h, bd: lambda e: e.matmul(R[bd][:, h:h + 1], lhsT=drow[rs][0:1, jh * 128:(jh + 1) * 128],
                                                                     rhs=ones1[0:1, 0:1], start=True, stop=True))(rs, jh, h, bd),
                     reads=["drow%d" % rs, "ones1"], writes=[RN[bd]])
        S.op("dve", lambda e: e.reciprocal(out=rdt[:, :], in_=R[bd][:, 0:H]), reads=[RN[bd]], writes=["rdt"])
        S.op("dve", lambda e: e.tensor_tensor(out=sgp[:, :].rearrange("p (h d) -> p h d", h=H),
                                              in0=sg[gs_][:, :].rearrange("p (h d) -> p h d", h=H),
                                              in1=rdt[:, :].unsqueeze(2).to_broadcast([128, H, DH]), op=ALU.mult),
             reads=["sg%d" % gs_, "rdt"], writes=["sgp"])
        yield
        bt = bank()
        ptv = R[bt].bitcast(BF16)
        for kc in range(8):
            S.op("pe", (lambda kc: lambda e: e.transpose(out=ptv[:, kc * 128:(kc + 1) * 128], in_=sgp[:, kc * 128:(kc + 1) * 128],
                                                         identity=ident[:]))(kc), reads=["sgp", "ident"], writes=[RN[bt]])
        S.op("act", lambda e: e.copy(out=sgT[:, :, :].rearrange("p a b -> p (a b)"), in_=ptv[:, :]), reads=[RN[bt]], writes=["sgT"])
        yield
        for g in range(G):
            for par in range(2):
                S.op("dve", (lambda g, par: lambda e: e.tensor_tensor(
                    out=ogT[par * 64:(par + 1) * 64, 2 * g:2 * g + 2, :],
                    in0=PO[0:64, g * 512:(g + 1) * 512].rearrange("p (a b t) -> p a b t", a=2, b=2)[:, :, par, :],
                    in1=sgT[par * 64:(par + 1) * 64, 2 * g:2 * g + 2, :], op=ALU.mult))(g, par),
                    reads=[OB[g], "sgT", "ogT"], writes=["ogT"])
            yield
        for nb in range(2):
            b = bank()
            for kc in range(8):
                S.op("pe", (lambda nb, kc, b: lambda e: e.matmul(R[b], lhsT=ogT[:, kc, :], rhs=wo[:, kc, nb * 512:(nb + 1) * 512],
                                                                 start=(kc == 0), stop=(kc == 7)))(nb, kc, b),
                     reads=["ogT", "wo%d" % kc], writes=[RN[b]])
            S.op("dve", (lambda nb, b: lambda e: e.tensor_tensor(out=xt[xs][:, nb * 512:(nb + 1) * 512], in0=xt[xs][:, nb * 512:(nb + 1) * 512],
                                                                 in1=R[b], op=ALU.add))(nb, b),
                 reads=["xt%d" % xs, RN[b]], writes=["xt%d" % xs])
            yield
        S.dma("sp", out_stream + str(xs), lambda e: e.dma_start(out=dst_d[sq, i * 128:(i + 1) * 128, :], in_=xt[xs][:]),
              reads=["xt%d" % xs], writes=[out_stream + "_dram%d" % xs])

    def stage1_A(sq, i, tc):
        xs = tc % 2
        qs = tc % 2
        use_sel = i >= 2
        S.dma("sp", "xt%d" % xs, lambda e: e.dma_start(out=xt[xs][:], in_=x_d[sq, i * 128:(i + 1) * 128, :]), writes=["xt%d" % xs])
        rms_and_transpose(xs, [(g1[:], "g1")], 1)
        yield
        kv_group(i, 0, 1024)
        yield
        b = bank()
        proj(b, 0, 3072, 72)
        S.op("act", lambda e: e.copy(out=ik_tok[:, :, :], in_=R[b][:, 8:72].unsqueeze(1).to_broadcast([128, 2, 64])), reads=[RN[b]], writes=["ik_tok"])
        if use_sel:
            S.op("act", lambda e: e.activation(out=wst[:, 0:8], in_=R[b][:, 0:8], func=AF.Abs), reads=[RN[b]], writes=["wabs"])
            S.op("act", lambda e: e.activation(out=wst[:, 8:16], in_=R[b][:, 0:8], func=AF.Sign), reads=[RN[b]], writes=["wsg"])
            for h in range(8):
                S.op("dve", (lambda h: lambda e: e.tensor_scalar(out=dsg[:, h, :], in0=ident[:], scalar1=wst[:, 8 + h:9 + h],
                                                                 scalar2=None, op0=ALU.mult))(h),
                     reads=["ident", "wsg", "dsg"], writes=["dsg"])
            fence_dve(["dsg"])
            b2 = bank()
            proj(b2, 0, 2560, 512)
            S.op("dve", lambda e: e.tensor_tensor(out=iq_tok[:, :].rearrange("p (h d) -> p h d", h=8),
                                                  in0=R[b2].rearrange("p (h d) -> p h d", h=8),
                                                  in1=wst[:, 0:8].unsqueeze(2).to_broadcast([128, 8, 64]), op=ALU.mult),
                 reads=[RN[b2], "wabs"], writes=["iq_tok"])
        yield
        k_transpose(i, 65)
        S.op("pe", lambda e: e.transpose(out=PT[:, 512:640], in_=ik_tok[:, :, :].rearrange("p a b -> p (a b)"), identity=ident[:]),
             reads=["ik_tok", "ident"], writes=["T"])
        S.op("act", lambda e: e.copy(out=ikT[:, i * 128:(i + 1) * 128], in_=PT[:, 512:640]), reads=["T"], writes=["ikT_%d" % i])
        yield
        ms = None
        if use_sel:
            ms = tc % 2
            n = 128 * (i + 1)
            for p in range(4):
                S.op("pe", (lambda p: lambda e: e.transpose(out=PT[:, p * 128:(p + 1) * 128], in_=iq_tok[:, p * 128:(p + 1) * 128],
                                                            identity=ident[:]))(p), reads=["iq_tok", "ident"], writes=["T"])
            S.op("act", lambda e: e.copy(out=iqT[:, :, :].rearrange("p a b -> p (a b)"), in_=PT[:, 0:512]), reads=["T"], writes=["iqT"])
            yield
            nchunk = (n + 511) // 512
            for c in range(nchunk):
                ncol = min(512, n - 512 * c)
                ikbufs = ["ikT_%d" % jj for jj in range(4 * c, min(4 * c + 4, i + 1))]
                for p in range(4):
                    bys = [bank(hold=True), bank(hold=True)]
                    rss = []
                    for hf in range(2):
                        rss.append(rb_ctr[0] % NRB)
                        rb_ctr[0] += 1
                    for hf in range(2):
                        S.op("pe", (lambda p, hf, by, c, ncol: lambda e: e.matmul(
                            R[by][:, 0:ncol], lhsT=iqT[hf * 64:(hf + 1) * 64, p, :], rhs=ikT[hf * 64:(hf + 1) * 64, c * 512:c * 512 + ncol],
                            start=True, stop=True))(p, hf, bys[hf], c, ncol), reads=["iqT"] + ikbufs, writes=[RN[bys[hf]]])
                    for hf in range(2):
                        by, rs = bys[hf], rss[hf]
                        if hf == 0:
                            S.op("act", (lambda by, rs, ncol: lambda e: e.activation(out=Rb[rs][:, 0:ncol], in_=R[by][:, 0:ncol], func=AF.Relu))(by, rs, ncol),
                                 reads=[RN[by]], writes=["Rb%d" % rs])
                        else:
                            S.op("dve", (lambda by, rs, ncol: lambda e: e.tensor_scalar(out=Rb[rs][:, 0:ncol], in0=R[by][:, 0:ncol],
                                                                                        scalar1=0.0, scalar2=None, op0=ALU.max))(by, rs, ncol),
                                 reads=[RN[by]], writes=["Rb%d" % rs])
                        held.discard(by)
                    for hf in range(2):
                        h = 2 * p + hf
                        rs = rss[hf]
                        S.op("pe", (lambda h, rs, ncol: lambda e: e.matmul(
                            PTF[:, 0:ncol], lhsT=dsg[:, h, :], rhs=Rb[rs][:, 0:ncol],
                            start=(h == 0), stop=(h == 7)))(h, rs, ncol), reads=["dsg", "Rb%d" % rs], writes=["T"])
                    yield
                S.op("act", (lambda c, ncol: lambda e: e.copy(out=isc[:, c * 512:c * 512 + ncol], in_=PTF[:, 0:ncol]))(c, ncol),
                     reads=["T", "isc"], writes=["isc"])
        pending = [lambda: q_group(qs, 0, 0, 0), lambda: q_group(qs, 1, 0, 512),
                   lambda: gate_group(qs, 0, 0, 1536), lambda: gate_group(qs, 1, 0, 2048),
                   lambda: q_transpose(qs, 65)]
        if not use_sel:
            for f in pending:
                f()
                yield
        if use_sel:
            S.op("dve", lambda e: e.tensor_tensor(out=isc[:, i * 128:(i + 1) * 128], in0=isc[:, i * 128:(i + 1) * 128],
                                                  in1=negc[:], op=ALU.add), reads=["isc", "negc"], writes=["isc"])
            HI, LO, W0, MID, CNT, TMP, THR = 0, 1, 2, 3, 4, 5, 6
            HB = 8
            S.op("dve", lambda e: e.tensor_reduce(out=bis[:, HI:HI + 1], in_=isc[:, 0:n], axis=AX.X, op=ALU.max),
                 reads=["isc"], writes=["b_hi"])
            S.op("dve", lambda e: e.tensor_reduce(out=bis[:, LO:LO + 1], in_=isc[:, 0:128 * i], axis=AX.X, op=ALU.min),
                 reads=["isc"], writes=["b_lo"])
            S.op("dve", lambda e: e.tensor_tensor(out=bis[:, W0:W0 + 1], in0=bis[:, HI:HI + 1], in1=bis[:, LO:LO + 1], op=ALU.subtract),
                 reads=["b_hi", "b_lo"], writes=["b_w0"])
            S.op("dve", lambda e: e.tensor_scalar(out=bis[:, HB:HB + NBIS], in0=halfpow[:], scalar1=bis[:, W0:W0 + 1], scalar2=None,
                                                  op0=ALU.mult), reads=["halfpow", "b_w0"], writes=["b_h"])
            S.op("dve", lambda e: e.tensor_scalar(out=bis[:, HB + NBIS:HB + 2 * NBIS], in0=bis[:, HB:HB + NBIS], scalar1=2.0, scalar2=None,
                                                  op0=ALU.mult), reads=["b_h"], writes=["b_h2"])
            S.op("dve", lambda e: e.tensor_tensor(out=bis[:, MID:MID + 1], in0=bis[:, LO:LO + 1], in1=bis[:, HB:HB + 1], op=ALU.add),
                 reads=["b_lo", "b_h"], writes=["b_mid"])
            yield
            for k in range(NBIS):
                S.op("dve", lambda e: e.tensor_scalar(out=mask_tok[:, 0:n], in0=isc[:, 0:n], scalar1=bis[:, MID:MID + 1], scalar2=0.0,
                                                      op0=ALU.is_ge, op1=ALU.add, accum_out=bis[:, CNT:CNT + 1]),
                     reads=["isc", "b_mid"], writes=["b_cnt", "mask_tok"])
                if k < NBIS - 1:
                    S.op("dve", (lambda k: lambda e: e.scalar_tensor_tensor(
                        out=bis[:, TMP:TMP + 1], in0=bis[:, CNT:CNT + 1], scalar=TOPK - 0.5,
                        in1=bis[:, HB + NBIS + k + 1:HB + NBIS + k + 2], op0=ALU.is_ge, op1=ALU.mult))(k),
                        reads=["b_cnt", "b_h2"], writes=["b_tmp"])
                    S.op("dve", (lambda k: lambda e: e.scalar_tensor_tensor(
                        out=bis[:, MID:MID + 1], in0=bis[:, TMP:TMP + 1], scalar=bis[:, HB + k + 1:HB + k + 2],
                        in1=bis[:, MID:MID + 1], op0=ALU.subtract, op1=ALU.add))(k),
                        reads=["b_tmp", "b_h", "b_mid"], writes=["b_mid"])
                else:
                    S.op("dve", (lambda k: lambda e: e.scalar_tensor_tensor(
                        out=bis[:, TMP:TMP + 1], in0=bis[:, CNT:CNT + 1], scalar=TOPK - 0.5,
                        in1=bis[:, HB + k:HB + k + 1], op0=ALU.is_ge, op1=ALU.mult))(k),
                        reads=["b_cnt", "b_h"], writes=["b_tmp"])
                    S.op("dve", (lambda k: lambda e: e.scalar_tensor_tensor(
                        out=bis[:, THR:THR + 1], in0=bis[:, TMP:TMP + 1], scalar=bis[:, HB + k:HB + k + 1],
                        in1=bis[:, MID:MID + 1], op0=ALU.subtract, op1=ALU.add))(k),
                        reads=["b_tmp", "b_h", "b_mid"], writes=["b_thr"])
                if k % 2 == 1 and pending:
                    pending.pop(0)()
                yield
            while pending:
                pending.pop(0)()
                yield
            S.op("dve", lambda e: e.tensor_scalar(out=mask_tok[:, 0:n], in0=isc[:, 0:n], scalar1=bis[:, THR:THR + 1], scalar2=-BIG,
                                                  op0=ALU.is_lt, op1=ALU.mult), reads=["isc", "b_thr"], writes=["mask_tok"])
            for j0 in range(0, i + 1, 8):
                j1 = min(i + 1, j0 + 8)
                for j in range(j0, j1):
                    S.op("pe", (lambda j, j0: lambda e: e.transpose(out=PT[:, (j - j0) * 128:(j - j0 + 1) * 128],
                                                                    in_=mask_tok[:, j * 128:(j + 1) * 128], identity=ident[:]))(j, j0),
                         reads=["mask_tok", "ident"], writes=["T"])
                S.op("act", (lambda j0, j1: lambda e: e.copy(out=maskT[ms][:, j0:j1, :].rearrange("p a b -> p (a b)"),
                                                             in_=PT[:, 0:(j1 - j0) * 128]))(j0, j1),
                     reads=["T", "maskT%d" % ms], writes=["maskT%d" % ms])
                yield

    def stage2_A(sq, i, tc):
        xs = tc % 2
        qs = tc % 2
        ms = (tc % 2) if i >= 2 else None
        yield from attention(i, qs, 65, ms)
        yield from finish_tile(sq, i, xs, qs, h1_d, "h1o")

    def stage1_B(sq, i, tc):
        xs = tc % 2
        qs = tc % 2
        own = i // 2
        S.dma("sp", "xt%d" % xs, lambda e: e.dma_start(out=xt[xs][:], in_=h1_d[sq, i * 128:(i + 1) * 128, :]),
              reads=["h1o_dram0", "h1o_dram1"], writes=["xt%d" % xs])
        rms_and_transpose(xs, [(g1[:], "g1"), (g2, "isc")], 2)
        yield
        S.op("pool", lambda e: e.memset(k_aug[:, :, 65:W], 0.0), reads=["k_aug"], writes=["k_aug"])
        S.op("pool", lambda e: e.memset(k_aug[:, :, 65 + own:66 + own], 1.0), reads=["k_aug"], writes=["k_aug"])
        kv_group(i, 0, 0)
        yield
        k_transpose(i, W)
        if i % 2 == 1:
            S.op("dve", lambda e: e.tensor_reduce(out=kmTf[:, :, own], in_=kTc[0:64, :, own * 256:(own + 1) * 256], axis=AX.X, op=ALU.add),
                 reads=["kTc_%d" % (i - 1), "kTc_%d" % i, "kmTf"], writes=["kmTf"])
            S.op("dve", lambda e: e.tensor_copy(out=kmT[:], in_=kmTf[:]), reads=["kmTf"], writes=["kmT"])
        yield
        q_group(qs, 0, 1, 512)
        yield
        q_group(qs, 1, 1, 1024)
        yield
        if own >= 4:
            S.op("dve", lambda e: e.tensor_reduce(out=qsumf[:, :, :], in_=q_aug[qs][:, :, 0:DH].rearrange("p (g j) d -> p g d j", g=G),
                                                  axis=AX.X, op=ALU.add), reads=["q_aug%d" % qs], writes=["qsumf"])
            S.op("dve", lambda e: e.tensor_copy(out=qsum[:], in_=qsumf[:]), reads=["qsumf"], writes=["qsum"])
            for g in range(G):
                S.op("pe", (lambda g: lambda e: e.transpose(out=PT[0:64, g * 128:(g + 1) * 128], in_=qsum[:, g, :], identity=ident[:]))(g),
                     reads=["qsum", "ident"], writes=["T"])
            S.op("act", lambda e: e.copy(out=qsT[:, :, :].rearrange("p a b -> p (a b)"), in_=PT[0:64, 0:512]), reads=["T"], writes=["qsT"])
            b = bank()
            for g in range(G):
                S.op("pe", (lambda g: lambda e: e.matmul(R[b][:, g * 8:(g + 1) * 8], lhsT=qsT[:, g, :], rhs=kmT[:, g, :],
                                                         start=True, stop=True))(g), reads=["qsT", "kmT"], writes=[RN[b]])
            S.op("dve", lambda e: e.tensor_copy(out=gsb[:, :, :].rearrange("p g n -> p (g n)"), in_=R[b][:, 0:32]), reads=[RN[b]], writes=["gsb"])
            S.op("dve", lambda e: e.memset(gsb[:, :, own:8], -1e30), reads=["gsb"], writes=["gsb"])
            for g in range(G):
                S.op("dve", (lambda g: lambda e: e.max(out=m8[:, g, :], in_=gsb[:, g, :]))(g), reads=["gsb", "m8"], writes=["m8"])
            for g in range(G):
                S.op("dve", (lambda g: lambda e: e.tensor_scalar(out=sel[:, g, :], in0=gsb[:, g, :], scalar1=m8[:, g, 2:3], scalar2=BIG,
                                                                 op0=ALU.is_ge, op1=ALU.mult))(g), reads=["gsb", "m8", "sel"], writes=["sel"])
            for g in range(G):
                S.op("dve", (lambda g: lambda e: e.tensor_scalar(out=q_aug[qs][:, 4 * g:4 * g + 4, 65:W],
                                                                 in0=sel[:, g, :].unsqueeze(1).to_broadcast([128, 4, 8]),
                                                                 scalar1=-BIG, scalar2=None, op0=ALU.add))(g),
                     reads=["sel", "q_aug%d" % qs], writes=["q_aug%d" % qs])
            S.op("dve", lambda e: e.memset(q_aug[qs][:, :, 65 + own:66 + own], 0.0), reads=["q_aug%d" % qs], writes=["q_aug%d" % qs])
            fence_dve(["q_aug%d" % qs])
            if dbg_tile is not None and i == dbg_tile and sq == 0:
                dbg_dump("gsb", gsb[:], [128, G, 8], F32, ["gsb"])
                dbg_dump("qaug", q_aug[qs][:], [128, H, W], BF16, ["q_aug%d" % qs])
        else:
            S.op("dve", lambda e: e.memset(q_aug[qs][:, :, 65:W], 0.0), reads=["q_aug%d" % qs], writes=["q_aug%d" % qs])
            fence_dve(["q_aug%d" % qs])
        yield
        q_transpose(qs, W)
        yield
        gate_group(qs, 0, 1, 1536)
        yield
        gate_group(qs, 1, 1, 2048)
        yield

    def stage2_B(sq, i, tc):
        xs = tc % 2
        qs = tc % 2
        yield from attention(i, qs, W, None)
        yield from finish_tile(sq, i, xs, qs, out_d, "outo")

    class _Null:
        pass

    def count_steps(genfn, args):
        saved = (bank_ctr[0], pt_ctr[0], rb_ctr[0], set(held), dbg_n[0])
        real_op, real_dma = S.op, S.dma
        S.op = lambda *a, **k: None
        S.dma = lambda *a, **k: None
        n = 0
        for _ in genfn(*args):
            n += 1
        S.op, S.dma = real_op, real_dma
        bank_ctr[0], pt_ctr[0], rb_ctr[0] = saved[0], saved[1], saved[2]
        held.clear()
        held.update(saved[3])
        dbg_n[0] = saved[4]
        return n + 1

    def drain(gen):
        for _ in gen:
            pass

    def run_phase(tiles, s1, s2):
        prev = None
        for t in list(tiles) + [None]:
            if not pipeline:
                if t is not None:
                    drain(s1(*t))
                    drain(s2(*t))
                continue
            if t is not None and prev is not None:
                n1 = count_steps(s1, t)
                n2 = count_steps(s2, prev)
                ga, gb = s1(*t), s2(*prev)
                d1 = d2 = 0
                a_alive = b_alive = True
                while a_alive or b_alive:
                    if b_alive and (not a_alive or d2 * n1 <= d1 * n2):
                        try:
                            next(gb)
                            d2 += 1
                        except StopIteration:
                            b_alive = False
                    else:
                        try:
                            next(ga)
                            d1 += 1
                        except StopIteration:
                            a_alive = False
            elif t is not None:
                drain(s1(*t))
            elif prev is not None:
                drain(s2(*prev))
            prev = t

    tile_ctr = [0]

    def tiles_of_phase():
        out = []
        for sq in range(nseq):
            for i in range(ntiles):
                out.append((sq, i, tile_ctr[0]))
                tile_ctr[0] += 1
        return out

    if doA:
        load_gain(g1, norm_a_g, 32.0, "g1")
        load_gk(kn_a_g, qn_a_g)
        load_w(wbig, w_in_a, A_COLS, 0, "wbig")
        load_w(wo, w_out_a, D, 0, "wo")
        run_phase(tiles_of_phase(), stage1_A, stage2_A)

    if doB:
        load_gain(g1, norm_kv_g, 32.0, "g1")
        S.dma("sp", "g2", lambda e: e.dma_start(out=g2, in_=norm_b_g.partition_broadcast(128)), writes=["isc"])
        S.op("dve", lambda e: e.tensor_scalar(out=g2, in0=g2, scalar1=32.0, scalar2=None, op0=ALU.mult), reads=["isc"], writes=["isc"])
        load_gk(kn_b_g, qn_b_g)
        load_w(wbig, w_kv, 512, 0, "wbig")
        load_w(wbig, w_in_b, 2048, 512, "wbig")
        load_w(wo, w_out_b, D, 0, "wo")
        S.op("pool", lambda e: e.memset(kmT[:], 0.0), writes=["kmT"])
        S.op("pool", lambda e: e.memset(kmTf[:], 0.0), writes=["kmTf"])
        run_phase(tiles_of_phase(), stage1_B, stage2_B)

    S.finalize()
    S.emit()
    es.close()
    return nc


_PROG_CACHE = {}


def _get_prog(mode):
    if mode not in _PROG_CACHE:
        _PROG_CACHE[mode] = build_program(mode)
    return _PROG_CACHE[mode]


FUSED = True


def kernel(x, norm_a_g, w_in_a, qn_a_g, kn_a_g, w_out_a, rel_bias, norm_kv_g, w_kv,
           kn_b_g, norm_b_g, w_in_b, qn_b_g, w_out_b):
    f = lambda a: np.ascontiguousarray(np.asarray(a, dtype=np.float32))
    x = f(x)
    rel_bias = f(rel_bias)
    nbias = _near_bias_layout(rel_bias).reshape(2, 128, H * 128)
    a_in = {"w_in_a": f(w_in_a)[0], "w_out_a": f(w_out_a)[0], "norm_a_g": f(norm_a_g)[0],
            "qn_a_g": f(qn_a_g)[0], "kn_a_g": f(kn_a_g)[0]}
    b_in = {"w_kv": f(w_kv), "w_in_b": f(w_in_b)[0], "w_out_b": f(w_out_b)[0], "norm_kv_g": f(norm_kv_g),
            "norm_b_g": f(norm_b_g)[0], "kn_b_g": f(kn_b_g), "qn_b_g": f(qn_b_g)[0]}
    common = {"rel_bias": rel_bias, "nbias": nbias}
    xs = [np.ascontiguousarray(x[NSEQ * c:NSEQ * (c + 1)]) for c in range(NCORES)]
    cores = list(range(NCORES))
    if FUSED:
        nc = _get_prog("AB")
        in_maps = [dict(x=xs[c], **a_in, **b_in, **common) for c in cores]
        res = run_bass_kernel_spmd(nc, in_maps, core_ids=cores)
        outs = [r["out"] for r in res.results]
    else:
        ncA = _get_prog("A")
        resA = run_bass_kernel_spmd(ncA, [dict(x=xs[c], **a_in, **common) for c in cores], core_ids=cores)
        h1 = [r["h1"] for r in resA.results]
        ncB = _get_prog("B")
        resB = run_bass_kernel_spmd(ncB, [dict(h1=h1[c], **b_in, **common) for c in cores], core_ids=cores)
        outs = [r["out"] for r in resB.results]
    return np.concatenate(outs, axis=0).astype(np.float32)
```
